# Optimizing a Trainium2 kernel written in Bass

```python
import math
import jax, jax.numpy as jnp
from jax import lax
import numpy as np

D_MODEL = 4096
BATCH = 4
SEQ = 2048
DEPTH = 1

A_KDIM = 128
A_VDIM = 128
A_WIDTH = D_MODEL // 2
A_HEADS = A_WIDTH // A_VDIM
A_KWIDTH = A_HEADS * A_KDIM
CHUNK = 64

B_HEAD_DIM = 128
B_WIDTH = D_MODEL // 2
B_HEADS = B_WIDTH // (2 * B_HEAD_DIM)
B_QK_WIDTH = B_HEADS * 2 * B_HEAD_DIM
Q_BLOCK = 128

NORM_EPS = 1e-6
SUBLN_EPS = 1e-5
NEG_INF = -1e30

kernel_name = "hgrn2_diffattn_gated_hybrid"


def rmsnorm(x, w, eps=NORM_EPS):
    xf = x.astype(jnp.float32)
    y = xf * lax.rsqrt(jnp.mean(xf * xf, axis=-1, keepdims=True) + eps) * w.astype(jnp.float32)
    return y.astype(x.dtype)


def hgrn2_mix(q, f_logit, i, lb):
    bsz, seq, _ = q.shape
    n_chunks = seq // CHUNK
    z = f_logit.astype(jnp.float32)
    lb = lb.astype(jnp.float32)
    log_f = jnp.log(lb + (1.0 - lb) * jax.nn.sigmoid(z))
    k = (1.0 - lb) * jax.nn.sigmoid(-z)

    def heads(t, d):
        return t.astype(jnp.float32).reshape(bsz, n_chunks, CHUNK, A_HEADS, d).transpose(0, 3, 1, 2, 4)

    qh, kh, lfh = heads(q, A_KDIM), heads(k, A_KDIM), heads(log_f, A_KDIM)
    vh = heads(i, A_VDIM)
    b = jnp.cumsum(lfh, axis=3)
    q_dec = qh * jnp.exp(b)
    k_inv = kh * jnp.exp(-b)
    causal = jnp.tril(jnp.ones((CHUNK, CHUNK), dtype=bool))
    scores = jnp.einsum('bhnck,bhnsk->bhncs', q_dec, k_inv)
    scores = jnp.where(causal, scores, 0.0)
    o_intra = jnp.einsum('bhncs,bhnsv->bhncv', scores, vh)

    b_last = b[:, :, :, -1:, :]
    k_to_end = kh * jnp.exp(b_last - b)
    chunk_state = jnp.einsum('bhnsk,bhnsv->bhnkv', k_to_end, vh)
    chunk_decay = jnp.exp(b_last[:, :, :, 0, :])

    def step(s_prev, inp):
        decay, upd = inp
        return decay[..., None] * s_prev + upd, s_prev

    s0 = jnp.zeros((bsz, A_HEADS, A_KDIM, A_VDIM), jnp.float32)
    _, s_in = lax.scan(step, s0, (chunk_decay.transpose(2, 0, 1, 3),
                                  chunk_state.transpose(2, 0, 1, 3, 4)))
    s_in = s_in.transpose(1, 2, 0, 3, 4)
    o_inter = jnp.einsum('bhnck,bhnkv->bhncv', q_dec, s_in)
    o = o_intra + o_inter
    return o.transpose(0, 2, 3, 1, 4).reshape(bsz, seq, A_HEADS, A_VDIM)


def diff_attention(q, k, v, lam, slopes):
    bsz, n_heads, _, seq, hd = q.shape
    n_blocks = seq // Q_BLOCK
    scale = hd ** -0.5
    q_blocks = q.reshape(bsz, n_heads, 2, n_blocks, Q_BLOCK, hd).transpose(3, 0, 1, 2, 4, 5)
    k_pos = jnp.arange(seq)

    def block(args):
        q_blk, blk_idx = args
        q_pos = blk_idx * Q_BLOCK + jnp.arange(Q_BLOCK)
        s = jnp.einsum('bhiqd,bhisd->bhiqs', q_blk, k).astype(jnp.float32) * scale
        dist = q_pos[:, None] - k_pos[None, :]
        bias = -slopes.astype(jnp.float32)[:, None, None] * dist.astype(jnp.float32)
        s = jnp.where(dist >= 0, s + bias[None, :, None], NEG_INF)
        p = jax.nn.softmax(s, axis=-1)
        a = p[:, :, 0] - lam * p[:, :, 1]
        return jnp.einsum('bhqs,bhsv->bhqv', a.astype(v.dtype), v)

    out = lax.map(block, (q_blocks, jnp.arange(n_blocks)))
    return out.transpose(1, 0, 3, 2, 4).reshape(bsz, seq, n_heads, 2 * hd)


def setup_inputs(seed: int = 0) -> dict:
    key = jax.random.key(seed)
    ks = jax.random.split(key, 16)
    n_in = 2 * A_KWIDTH + 2 * A_WIDTH + 2 * B_QK_WIDTH + 2 * B_WIDTH + 2 * D_MODEL
    f32 = jnp.float32
    return {
        "x": jax.random.normal(ks[0], (BATCH, SEQ, D_MODEL), f32),
        "norm_w": 1.0 + 0.02 * jax.random.normal(ks[1], (DEPTH, D_MODEL), f32),
        "w_in": jax.random.normal(ks[2], (DEPTH, D_MODEL, n_in), f32) * D_MODEL ** -0.5,
        "lower_bound_table": 1.0 + 0.1 * jax.random.normal(ks[3], (DEPTH + 1, A_KWIDTH), f32),
        "hgrn_norm_w": 1.0 + 0.02 * jax.random.normal(ks[4], (DEPTH, A_VDIM), f32),
        "lambda_q1": 0.1 * jax.random.normal(ks[5], (DEPTH, B_HEAD_DIM), f32),
        "lambda_k1": 0.1 * jax.random.normal(ks[6], (DEPTH, B_HEAD_DIM), f32),
        "lambda_q2": 0.1 * jax.random.normal(ks[7], (DEPTH, B_HEAD_DIM), f32),
        "lambda_k2": 0.1 * jax.random.normal(ks[8], (DEPTH, B_HEAD_DIM), f32),
        "subln_w": 1.0 + 0.02 * jax.random.normal(ks[9], (DEPTH, 2 * B_HEAD_DIM), f32),
        "w_branch_a": jax.random.normal(ks[10], (DEPTH, A_WIDTH, D_MODEL), f32) * A_WIDTH ** -0.5,
        "w_branch_b": jax.random.normal(ks[11], (DEPTH, B_WIDTH, D_MODEL), f32) * B_WIDTH ** -0.5,
        "w_out": jax.random.normal(ks[12], (DEPTH, D_MODEL, D_MODEL), f32) * D_MODEL ** -0.5,
        "final_w": 1.0 + 0.02 * jax.random.normal(ks[13], (D_MODEL,), f32),
    }


def reference(x, norm_w, w_in, lower_bound_table, hgrn_norm_w, lambda_q1, lambda_k1,
              lambda_q2, lambda_k2, subln_w, w_branch_a, w_branch_b, w_out, final_w):
    bsz, seq, _ = x.shape
    sizes = [A_KWIDTH, A_KWIDTH, A_WIDTH, A_WIDTH,
             B_QK_WIDTH, B_QK_WIDTH, B_WIDTH, B_WIDTH, D_MODEL, D_MODEL]
    split_idx = [int(s) for s in np.cumsum(sizes)[:-1]]
    lb_all = jnp.cumsum(jax.nn.softmax(lower_bound_table.astype(jnp.float32), axis=0), axis=0)
    slopes = jnp.exp2(-8.0 * (jnp.arange(B_HEADS, dtype=jnp.float32) + 1.0) / B_HEADS)

    h = x
    for l in range(DEPTH):
        u = rmsnorm(h, norm_w[l])
        proj = u @ w_in[l]
        a_q, a_f, a_i, a_g, b_q, b_k, b_v, b_g, gate_a, gate_b = jnp.split(proj, split_idx, axis=-1)

        o_a = hgrn2_mix(a_q, a_f, a_i, lb_all[l]).astype(u.dtype)
        o_a = rmsnorm(o_a, hgrn_norm_w[l]).reshape(bsz, seq, A_WIDTH) * jax.nn.silu(a_g)

        lam_init = 0.8 - 0.6 * math.exp(-0.3 * l)
        lam = (jnp.exp(jnp.sum(lambda_q1[l].astype(jnp.float32) * lambda_k1[l].astype(jnp.float32)))
               - jnp.exp(jnp.sum(lambda_q2[l].astype(jnp.float32) * lambda_k2[l].astype(jnp.float32)))
               + lam_init)
        qb = b_q.reshape(bsz, seq, B_HEADS, 2, B_HEAD_DIM).transpose(0, 2, 3, 1, 4)
        kb = b_k.reshape(bsz, seq, B_HEADS, 2, B_HEAD_DIM).transpose(0, 2, 3, 1, 4)
        vb = b_v.reshape(bsz, seq, B_HEADS, 2 * B_HEAD_DIM).transpose(0, 2, 1, 3)
        o_b = diff_attention(qb, kb, vb, lam, slopes)
        o_b = rmsnorm(o_b, subln_w[l], SUBLN_EPS) * (1.0 - lam_init)
        o_b = o_b.reshape(bsz, seq, B_WIDTH) * jax.nn.silu(b_g)

        y = (jax.nn.sigmoid(gate_a) * (o_a @ w_branch_a[l])
             + jax.nn.sigmoid(gate_b) * (o_b @ w_branch_b[l]))
        h = h + y @ w_out[l]
    return rmsnorm(h, final_w)
```

```python
import numpy as np
from contextlib import ExitStack
import concourse.bass as bass
import concourse.mybir as mybir
from concourse.bass_utils import run_bass_kernel_spmd

F32 = mybir.dt.float32
BF16 = mybir.dt.bfloat16
AF = mybir.ActivationFunctionType
ALU = mybir.AluOpType
AX = mybir.AxisListType

NCORES = 8
T = 1024
NT = 8
D = 4096
KC = 32
NIN = 24576
QSCALE = 128 ** -0.5
WK2N = 14336
C_AQ, C_AF, C_AI, C_AG, C_BQ, C_BK, C_BV, C_BG, C_GA, C_GB = 0, 2048, 4096, 6144, 8192, 10240, 12288, 14336, 16384, 20480


class Sched:
    ENG = ("pe", "act", "dve", "pool", "sp")

    def __init__(self):
        self.ops = []
        self.last_writer = {}
        self.readers = {}
        self.dom_pos = {}
        self.pending = {e: set() for e in self.ENG}
        self.last_in_dom = {}

    def op(self, eng, fn, reads=(), writes=(), dma=None, drain=False):
        oid = len(self.ops)
        raw = set()
        oth = set()
        force = set()
        if drain and eng in self.last_in_dom:
            force.add(self.last_in_dom[eng])
        for k in reads:
            w = self.last_writer.get(k)
            if w is not None:
                raw.add(w)
        for k in writes:
            w = self.last_writer.get(k)
            if w is not None:
                oth.add(w)
            for r in self.readers.get(k, ()):
                oth.add(r)
        raw |= self.pending[eng]
        self.pending[eng] = set()
        for k in reads:
            self.readers.setdefault(k, []).append(oid)
        for k in writes:
            self.last_writer[k] = oid
            self.readers[k] = []
        dom = dma if dma else eng
        pos = self.dom_pos.get(dom, 0)
        self.dom_pos[dom] = pos + 1
        self.ops.append(dict(id=oid, eng=eng, fn=fn, dom=dom, pos=pos, raw=raw, oth=oth,
                             signal=False, waits=[], isdma=dma is not None, force=force))
        self.last_in_dom[dom] = oid
        return oid

    def barrier(self):
        s = set(self.last_in_dom.values())
        for e in self.ENG:
            self.pending[e] |= s

    def finalize(self):
        waited = {}
        by_dom = {}
        for o in self.ops:
            by_dom.setdefault(o["dom"], []).append(o)
        for o in self.ops:
            need = {}
            F = o["eng"]
            for d in o["raw"] | o["oth"] | o["force"]:
                dd = self.ops[d]
                E = dd["dom"]
                if (not dd["isdma"]) and E == F and d not in o["force"]:
                    if F in ("pe", "sp"):
                        continue
                if waited.get((F, E), -1) >= dd["pos"]:
                    continue
                need[E] = max(need.get(E, -1), dd["pos"])
            for E, p in need.items():
                by_dom[E][p]["signal"] = True
                waited[(F, E)] = p
                o["waits"].append((E, p))
        for dom, lst in by_dom.items():
            c = 0
            for o in lst:
                if o["signal"]:
                    c += 1
                    o["val"] = c * (16 if o["isdma"] else 1)
        for o in self.ops:
            o["waitvals"] = [(E, by_dom[E][p]["val"]) for (E, p) in o["waits"]]
        self.domains = list(by_dom.keys())

    def emit(self, nc, stack):
        self.finalize()
        sems = {}
        for dom in self.domains:
            sems[dom] = stack.enter_context(nc.semaphore("s_" + dom))
        block = stack.enter_context(nc.Block())
        streams = {e: [o for o in self.ops if o["eng"] == e] for e in self.ENG}

        def run(eh, lst):
            for o in lst:
                for (E, v) in o["waitvals"]:
                    eh.wait_ge(sems[E], v)
                ins = o["fn"](eh) if o["fn"] is not None else None
                if o["signal"]:
                    assert ins is not None
                    ins.then_inc(sems[o["dom"]], 16 if o["isdma"] else 1)

        @block.tensor
        def _(e):
            run(e, streams["pe"])

        @block.scalar
        def _(e):
            run(e, streams["act"])

        @block.vector
        def _(e):
            run(e, streams["dve"])

        @block.gpsimd
        def _(e):
            run(e, streams["pool"])

        @block.sync
        def _(e):
            run(e, streams["sp"])


class Arena:
    def __init__(self, flat, lo, hi):
        self.flat, self.lo, self.hi, self.cur = flat, lo, hi, lo

    def alloc(self, shape, dt):
        esz = 4 if dt == F32 else 2
        n = 1
        for s in shape[1:]:
            n *= s
        nel = n * esz // 2
        nel = (nel + 31) // 32 * 32
        assert self.cur + nel <= self.hi, ("arena overflow", shape, self.cur, self.hi)
        v = self.flat[:, self.cur:self.cur + n * esz // 2]
        self.cur += nel
        if dt == F32:
            v = v.bitcast(F32)
        if len(shape) == 3:
            v = v.rearrange("p (a b) -> p a b", b=shape[2])
        elif len(shape) == 4:
            v = v.rearrange("p (a b c) -> p a b c", b=shape[2], c=shape[3])
        return v


def build_program(debug=False, cfg=None):
    cfg = cfg or {}
    NPAIR = cfg.get("npair", 8)
    NBH = cfg.get("nbh", 8)
    NGT = cfg.get("ngt", 32)
    DO_C = cfg.get("do_c", True)
    HST = cfg.get("hstage", 99)
    nc = bass.Bass("TRN2", target_bir_lowering=False)

    def din(name, shape, dt=F32):
        return nc.dram_tensor(name, list(shape), dt, kind="ExternalInput").ap()

    x_own = din("x_own", [T, D])
    x_prev = din("x_prev", [T, D])
    w_in = din("w_in", [D, NIN])
    w_a = din("w_a", [2048, D])
    w_b = din("w_b", [2048, D])
    w_out = din("w_out", [D, D])
    normwT_d = din("normwT", [128, 32])
    fwB_d = din("fwB", [128, D])
    lbt_d = din("lbt", [128, 4096])
    hnwB_d = din("hnwB", [128, 128])
    sublnB_d = din("sublnB", [128, 256])
    lamv_d = din("lamv", [128, 512])
    identf_d = din("identf", [128, 128])
    TI_d = din("TI", [128, 130])
    maskA_d = din("maskA", [128, 128])
    cmask_d = din("cmask", [128, 128])
    alibi_d = din("alibi", [128, 1024])
    out = nc.dram_tensor("out", [T, D], F32, kind="ExternalOutput").ap()
    sk = "ExternalOutput" if debug else "Internal"
    kprev = nc.dram_tensor("kprev_scr", [8, 128, 2048], BF16, kind="Internal").ap()
    vprev = nc.dram_tensor("vprev_scr", [8, 128, 2048], BF16, kind="Internal").ap()
    oT_scr = nc.dram_tensor("oT_scr", [32, 128, 1024], BF16, kind=sk).ap()
    sig_scr = nc.dram_tensor("sig_scr", [64, 128, 1024], BF16, kind=sk).ap()
    yT_scr = nc.dram_tensor("yT_scr", [32, 128, 1024], BF16, kind=sk).ap()

    S = Sched()
    st = ExitStack()
    with st:
        def sb(name, shape, dt):
            return st.enter_context(nc.sbuf_tensor(name, shape, dt))

        big = sb("big", [128, 65536], BF16)
        wk2 = sb("wk2", [128, WK2N], BF16)
        stg = [sb("stg%d" % i, [128, 8, 256], F32) for i in range(3)]
        identf = sb("identf_s", [128, 128], F32)
        identb = sb("identb_s", [128, 128], BF16)
        TI = sb("TI_s", [128, 130], F32)
        maskA = sb("maskA_s", [128, 128], F32)
        cmask = sb("cmask_s", [128, 128], BF16)
        alibi = sb("alibi_s", [128, 8, 8, 16], F32)
        normwT = sb("normwT_s", [128, 32], F32)
        hnwB = sb("hnwB_s", [128, 128], F32)
        sublnB = sb("sublnB_s", [128, 256], F32)
        lbB = sb("lbB_s", [128, 2048], F32)
        Smid = sb("Smid_s", [128, 16, 128], F32)
        stat = sb("stat_s", [128, 512], F32)
        lam = stat[:, 0:1]
        neglam = stat[:, 1:2]
        eps6 = stat[:, 8:9]
        eps5 = stat[:, 9:10]
        one1 = stat[:, 10:11]
        pb = [st.enter_context(nc.psum_tensor("pb%d" % i, [128, 512], F32)) for i in range(8)]

        R2 = 32768
        uT = big[:, 0:R2].rearrange("p (k t) -> p k t", t=1024)
        W = [big[:, R2 + s * 8192: R2 + (s + 1) * 8192].rearrange("p (k c) -> p k c", c=256) for s in range(4)]
        wide0 = [10 ** 9]
        A1_LO, A1_HI = R2 + 16384, 65536

        cnt = {"acc": 0, "small": 0, "stg": 0, "w": 0, "s4": 0, "since": 0}

        late_casts = []

        def flush_casts(upto=None):
            keep = []
            while late_casts:
                t, fn = late_casts.pop(0)
                if upto is None or t <= upto:
                    fn()
                else:
                    keep.append((t, fn))
            late_casts.extend(keep)

        def acc_next():
            cnt["since"] += 1
            if cnt["since"] >= 5:
                flush_casts()
            i = cnt["acc"] % 4
            cnt["acc"] += 1
            return pb[i], ("ps", i)

        def sbank_next():
            b = 6 + cnt["small"] % 2
            cnt["small"] += 1
            return b, ("ps", b)

        plan = []
        cur = [0]
        deferred = []

        def pump(n=1):
            while n > 0 and deferred:
                deferred.pop(0)()
                n -= 1

        def slot_of(i):
            return i % 2 if i < wide0[0] else (i - wide0[0]) % 4

        loaded = [0]

        def issue_load(i):
            s = slot_of(i)
            for j, seg in enumerate(plan[i]):
                g = cnt["stg"] % 3
                cnt["stg"] += 1
                S.op("sp", (lambda e, g=g, seg=seg: e.dma_start(out=stg[g][:], in_=seg.rearrange("(k p) c -> p k c", p=128))),
                     writes=[("stg", g)], dma="d_stg%d" % g)
                late = True
                ceng = ("act", "act", "act", "dve")[j]

                def cast(ceng=ceng, g=g, s=s, j=j):
                    if ceng == "act":
                        S.op("act", (lambda e: e.activation(out=W[s][:, j * 8:(j + 1) * 8, :], in_=stg[g][:], func=AF.Copy)),
                             reads=[("stg", g)], writes=[("w", s, j)])
                    else:
                        S.op(ceng, (lambda e: e.tensor_copy(out=W[s][:, j * 8:(j + 1) * 8, :], in_=stg[g][:])),
                             reads=[("stg", g)], writes=[("w", s, j)])
                if late and j != 0:
                    late_casts.append((i, cast))
                    cnt["since"] = 0
                else:
                    cast()

        def next_tile(check=None):
            i = cur[0]
            cur[0] += 1
            if check is not None:
                assert plan[i][0] is check[0], "tile plan mismatch at %d" % i
            flush_casts()
            la = 1 if i < wide0[0] else 3
            while loaded[0] <= min(i + la, len(plan) - 1):
                flush_casts()
                issue_load(loaded[0])
                loaded[0] += 1
            flush_casts(upto=i)
            return slot_of(i)

        def win_tile(col0):
            return [w_in[j * 1024:(j + 1) * 1024, col0:col0 + 256] for j in range(4)]

        def wkeys(s):
            return [("w", s, j) for j in range(4)]

        def proj_fm(s, c0, half, ps, pkey, src=uT, srckeys=None, k0=0, nk=32, ksrc0=0):
            def fn(e):
                ins = None
                for k in range(nk):
                    ins = e.matmul(ps[:, 0:512], lhsT=W[s][:, k0 + k, c0:c0 + 128],
                                   rhs=src[:, ksrc0 + k, half * 512:(half + 1) * 512],
                                   start=(k == 0), stop=(k == nk - 1))
                return ins
            rk = srckeys if srckeys is not None else [("uT", tt) for tt in range(half * 4, half * 4 + 4)]
            S.op("pe", fn, reads=wkeys(s) + rk, writes=[pkey])

        def proj_tm(s, tt, ps, pkey, src=uT, srckeys=None):
            def fn(e):
                ins = None
                for k in range(KC):
                    ins = e.matmul(ps[:, 0:256], lhsT=src[:, k, tt * 128:(tt + 1) * 128], rhs=W[s][:, k, :],
                                   start=(k == 0), stop=(k == KC - 1))
                return ins
            rk = srckeys if srckeys is not None else [("uT", tt)]
            S.op("pe", fn, reads=wkeys(s) + rk, writes=[pkey])

        def cload(dst, src, key):
            S.op("sp", (lambda e: e.dma_start(out=dst, in_=src)), writes=[key], dma="d_c_" + key)

        cload(identf[:], identf_d[:, :], "identf")
        cload(TI[:], TI_d[:, :], "TI")
        cload(maskA[:], maskA_d[:, :], "maskA")
        cload(alibi[:], alibi_d.rearrange("p (a b c) -> p a b c", a=8, b=8), "alibi")
        cload(normwT[:], normwT_d[:, :], "normwT")
        cload(hnwB[:], hnwB_d[:, :], "hnwB")
        cload(sublnB[:], sublnB_d[:, :], "sublnB0")
        ar0 = Arena(big, A1_LO, A1_HI)
        lbt = ar0.alloc([128, 2, 2048], F32)
        lamv = ar0.alloc([128, 4, 128], F32)
        cm32 = ar0.alloc([128, 128], F32)
        lamp = ar0.alloc([128, 2, 128], F32)
        cload(lbt, lbt_d.rearrange("p (a b) -> p a b", a=2), "lbt")
        cload(lamv, lamv_d.rearrange("p (a b) -> p a b", a=4), "lamv")
        cload(cm32, cmask_d[:, :], "cm32")
        S.op("pool", lambda e: e.tensor_copy(out=identb[:], in_=identf[:]), reads=["identf"], writes=["identb"])
        S.op("pool", lambda e: e.tensor_copy(out=cmask[:], in_=cm32), reads=["cm32"], writes=["cmask"])
        S.op("pool", lambda e: e.memset(Smid[:], 0.0), writes=["Smid"])
        S.op("pool", lambda e: e.memset(eps6, 1e-6), writes=["eps6"])
        S.op("pool", lambda e: e.memset(one1, 1.0), writes=["one1"])
        S.op("pool", lambda e: e.memset(eps5, 1e-5), writes=["eps"])
        S.op("dve", lambda e: e.tensor_tensor(out=lbt[:, 0, :], in0=lbt[:, 0, :], in1=lbt[:, 1, :], op=ALU.subtract), reads=["lbt"], writes=["lbd"])
        S.op("act", lambda e: e.activation(out=lbB[:], in_=lbt[:, 0, :], func=AF.Sigmoid), reads=["lbd"], writes=["lbB"])
        S.op("dve", lambda e: e.tensor_tensor(out=lamp[:, 0, :], in0=lamv[:, 0, :], in1=lamv[:, 1, :], op=ALU.mult), reads=["lamv"], writes=["lamp0"])
        S.op("dve", lambda e: e.tensor_tensor(out=lamp[:, 1, :], in0=lamv[:, 2, :], in1=lamv[:, 3, :], op=ALU.mult), reads=["lamv"], writes=["lamp1"])
        S.op("dve", lambda e: e.reduce_sum(out=stat[:, 2:3], in_=lamp[:, 0, :], axis=AX.X), reads=["lamp0"], writes=["ls0"])
        S.op("dve", lambda e: e.reduce_sum(out=stat[:, 3:4], in_=lamp[:, 1, :], axis=AX.X), reads=["lamp1"], writes=["ls1"])
        S.op("act", lambda e: e.activation(out=stat[:, 4:6], in_=stat[:, 2:4], func=AF.Exp), reads=["ls0", "ls1"], writes=["le"])
        S.op("dve", lambda e: e.scalar_tensor_tensor(out=lam, in0=stat[:, 4:5], scalar=0.2, in1=stat[:, 5:6], op0=ALU.add, op1=ALU.subtract), reads=["le"], writes=["lam"])
        S.op("dve", lambda e: e.tensor_scalar(out=neglam, in0=lam, scalar1=-1.0, scalar2=None, op0=ALU.mult), reads=["lam"], writes=["neglam"])
        S.op("dve", lambda e: e.tensor_scalar(out=sublnB[:], in0=sublnB[:], scalar1=0.8, scalar2=None, op0=ALU.mult), reads=["sublnB0"], writes=["sublnB"])
        S.barrier()

        def phase_A(xsrc, tag):
            arA = Arena(big, A1_LO, A1_HI)
            arA2 = Arena(wk2, 0, WK2N)
            xs = [arA.alloc([128, D], F32) for _ in range(2)]
            xn = [arA2.alloc([128, D], BF16) for _ in range(2)]
            for tt in range(NT):
                sl = tt % 2
                S.op("sp", (lambda e, sl=sl, tt=tt: e.dma_start(out=xs[sl], in_=xsrc[tt * 128:(tt + 1) * 128, :])),
                     writes=[("xs", sl)], dma="d_xs%d" % sl)
                c0 = 16 + tt * 3
                S.op("act", (lambda e, sl=sl, c0=c0: e.activation(out=xn[sl], in_=xs[sl], func=AF.Square, accum_out=stat[:, c0:c0 + 1])),
                     reads=[("xs", sl)], writes=[("xn", sl), ("ssA", tt)])
                S.op("act", (lambda e, c0=c0: e.activation(out=stat[:, c0 + 1:c0 + 2], in_=stat[:, c0:c0 + 1], func=AF.Sqrt, bias=1e-6, scale=1.0 / D)),
                     reads=[("ssA", tt)], writes=[("sqA", tt)])
                S.op("dve", (lambda e, c0=c0: e.reciprocal(out=stat[:, c0 + 2:c0 + 3], in_=stat[:, c0 + 1:c0 + 2])),
                     reads=[("sqA", tt)], writes=[("rsA", tt)])
                S.op("act", (lambda e, sl=sl, c0=c0: e.activation(out=xn[sl], in_=xs[sl], func=AF.Copy, scale=stat[:, c0 + 2:c0 + 3])),
                     reads=[("xs", sl), ("rsA", tt)], writes=[("xn", sl)])
                for g in range(4):
                    pbf = pb[g][:, :].bitcast(BF16).rearrange("p (a b) -> p a b", b=128)

                    def trf(e, sl=sl, g=g, pbf=pbf):
                        ins = None
                        for i in range(8):
                            c = g * 8 + i
                            ins = e.transpose(out=pbf[:, i, :], in_=xn[sl][:, c * 128:(c + 1) * 128], identity=identb[:])
                        return ins
                    S.op("pe", trf, reads=[("xn", sl), "identb"], writes=[("ps", g)])
                    S.op("dve", (lambda e, g=g, tt=tt, pbf=pbf: e.tensor_tensor(
                        out=uT[:, g * 8:(g + 1) * 8, tt * 128:(tt + 1) * 128], in0=pbf[:, 0:8, :],
                        in1=normwT[:, g * 8:(g + 1) * 8].unsqueeze(2).to_broadcast([128, 8, 128]), op=ALU.mult)),
                        reads=[("ps", g), "normwT"], writes=[("uT", tt)])
            S.barrier()

        def hgrn_pair(hp, own):
            a1 = Arena(big, A1_LO, A1_HI)
            a2 = Arena(wk2, 0, WK2N)
            kte_tok = a1.alloc([128, 8, 256], BF16)
            v_tok = a1.alloc([128, 8, 256], BF16)
            enRT = a1.alloc([128, 2, 1024], BF16)
            kteT = a1.alloc([128, 2, 1024], BF16)
            qeT = a1.alloc([128, 2, 1024], BF16)
            sgT = a1.alloc([128, 2, 1024], BF16)
            Dbf = a1.alloc([128, 2, 16, 128], BF16)
            sg = [a2.alloc([128, 256], F32) for _ in range(2)]
            t1 = a2.alloc([128, 256], F32)
            lf = [a2.alloc([128, 256], F32) for _ in range(2)]
            ktok = [a2.alloc([128, 256], F32) for _ in range(2)]
            eR = a2.alloc([128, 256], F32)
            omlp = a2.alloc([128, 256], F32)
            scTm = [a2.alloc([128, 4, 128], BF16) for _ in range(2)]
            on_tok = [a2.alloc([128, 4, 128], F32) for _ in range(2)]
            oTh = [a2.alloc([128, 1024], BF16) for _ in range(2)]
            dec = a2.alloc([128, 2, 16], F32)
            hst = a2.alloc([128, 64], F32)
            vm = a2.alloc([128, 8, 2, 256], BF16)
            if hp == 0:
                S.op("pool", lambda e: e.memset(vm, 0.0), writes=["vm0"] + [("vm", tt, j) for tt in range(NT) for j in range(2)])
            lbp = lbB[:, hp * 256:(hp + 1) * 256]
            P = "A%d%d" % (hp, int(own))
            S.op("dve", lambda e: e.tensor_scalar(out=omlp, in0=lbp, scalar1=-1.0, scalar2=1.0, op0=ALU.mult, op1=ALU.add),
                 reads=["lbB"], writes=["omlp"])
            sf = next_tile()

            def rstuff(tt):
                b = tt % 2
                psR, pkR = acc_next()
                S.op("pe", (lambda e, psR=psR, b=b: e.matmul(psR[:, 0:256], lhsT=TI[:, 0:128], rhs=lf[b], start=True, stop=True)),
                     reads=[("lf", b), "TI"], writes=[pkR])
                S.op("act", (lambda e, psR=psR: e.activation(out=eR, in_=psR[:, 0:256], func=AF.Exp)),
                     reads=[pkR], writes=["eR"])
                S.op("dve", (lambda e, tt=tt, b=b: e.tensor_tensor(out=kte_tok[:, tt, :], in0=ktok[b], in1=eR, op=ALU.mult)),
                     reads=[("ktok", b), "eR"], writes=[("kte", tt)])
                psT, pkT = acc_next()
                if own:
                    def rtf(e, psT=psT, b=b):
                        e.matmul(psT[:, 0:130], lhsT=lf[b][:, 0:128], rhs=TI[:, 0:130], start=True, stop=True)
                        return e.matmul(psT[:, 256:386], lhsT=lf[b][:, 128:256], rhs=TI[:, 0:130], start=True, stop=True)
                    S.op("pe", rtf, reads=[("lf", b), "TI"], writes=[pkT])
                    for hh in range(2):
                        S.op("act", (lambda e, psT=psT, hh=hh, tt=tt: e.activation(out=enRT[:, hh, tt * 128:(tt + 1) * 128], in_=psT[:, hh * 256:hh * 256 + 128], func=AF.Exp, scale=-1.0)),
                             reads=[pkT], writes=[("enRT", hh, tt)])
                        S.op("act", (lambda e, psT=psT, hh=hh, tt=tt: e.activation(out=dec[:, hh, tt * 2:tt * 2 + 2], in_=psT[:, hh * 256 + 128:hh * 256 + 130], func=AF.Exp)),
                             reads=[pkT], writes=[("dec", hh, tt)])
                else:
                    def rtf(e, psT=psT, b=b):
                        e.matmul(psT[:, 0:2], lhsT=lf[b][:, 0:128], rhs=TI[:, 128:130], start=True, stop=True)
                        return e.matmul(psT[:, 256:258], lhsT=lf[b][:, 128:256], rhs=TI[:, 128:130], start=True, stop=True)
                    S.op("pe", rtf, reads=[("lf", b), "TI"], writes=[pkT])
                    for hh in range(2):
                        S.op("act", (lambda e, psT=psT, hh=hh, tt=tt: e.activation(out=dec[:, hh, tt * 2:tt * 2 + 2], in_=psT[:, hh * 256:hh * 256 + 2], func=AF.Exp)),
                             reads=[pkT], writes=[("dec", hh, tt)])

            for tt in range(NT):
                ps, pk = acc_next()
                proj_tm(sf, tt, ps, pk)
                b = tt % 2
                S.op("act", (lambda e, ps=ps, b=b: e.activation(out=sg[b], in_=ps[:, 0:256], func=AF.Exp, scale=-1.0)),
                     reads=[pk], writes=[("sg", b)])
                S.op("act", (lambda e, b=b: e.activation(out=sg[b], in_=sg[b], func=AF.Ln, bias=one1[:, 0:1], scale=1.0)),
                     reads=[("sg", b), "eps"], writes=[("sg", b)])
                S.op("act", (lambda e, b=b: e.activation(out=sg[b], in_=sg[b], func=AF.Exp, scale=-1.0)),
                     reads=[("sg", b)], writes=[("sg", b)])
                S.op("dve", (lambda e, b=b: e.tensor_tensor(out=t1, in0=sg[b], in1=omlp, op=ALU.mult)),
                     reads=[("sg", b), "omlp"], writes=["t1"])
                S.op("dve", (lambda e, b=b: e.tensor_tensor(out=sg[b], in0=t1, in1=lbp, op=ALU.add)),
                     reads=["t1", "lbB"], writes=[("sg", b)])
                S.op("dve", (lambda e, b=b: e.tensor_tensor(out=ktok[b], in0=omlp, in1=t1, op=ALU.subtract)),
                     reads=["t1", "omlp"], writes=[("ktok", b)])
                S.op("act", (lambda e, b=b: e.activation(out=lf[b], in_=sg[b], func=AF.Ln)),
                     reads=[("sg", b)], writes=[("lf", b)])
                pump(2 if tt < 4 else 1)
                if tt >= 1:
                    rstuff(tt - 1)
            pump(99)
            rstuff(NT - 1)
            if HST < 1:
                return

            def scan_group(tp):
                for hh in range(2):
                    h = hp * 2 + hh
                    b, bk = sbank_next()
                    pv = pb[b][:, 0:512].rearrange("p (t j v) -> p t j v", t=2, j=2)

                    def csf(e, pv=pv, tp=tp, hh=hh):
                        ins = None
                        for t2 in range(2):
                            tt = tp * 2 + t2
                            ins = e.matmul(pv[:, t2, :, :], lhsT=kte_tok[:, tt, hh * 128:(hh + 1) * 128],
                                           rhs=vm[:, tt, :, hh * 128:(hh + 1) * 128], start=True, stop=True)
                        return ins
                    S.op("pe", csf, reads=[("kte", tp * 2), ("kte", tp * 2 + 1)] + [("vm", tp * 2 + t2, j) for t2 in range(2) for j in range(2)], writes=[bk])
                    for t2 in range(2):
                        for j in range(2):
                            tt = tp * 2 + t2
                            n = tt * 2 + j
                            psc = pv[:, t2, j, :]
                            if own:
                                S.op("dve", (lambda e, h=h, hh=hh, n=n: e.tensor_scalar(out=Dbf[:, hh, n, :], in0=Smid[:, h, :], scalar1=dec[:, hh, n:n + 1], scalar2=None, op0=ALU.mult)),
                                     reads=[("Smid", h), ("dec", hh, tt)], writes=[("Dbf", hh, n)])
                            S.op("dve", (lambda e, h=h, hh=hh, n=n, psc=psc: e.scalar_tensor_tensor(out=Smid[:, h, :], in0=Smid[:, h, :], scalar=dec[:, hh, n:n + 1], in1=psc, op0=ALU.mult, op1=ALU.add)),
                                 reads=[("Smid", h), "Smid", ("dec", hh, tt), bk], writes=[("Smid", h)])

            def kte_transposes():
                if not own:
                    return
                for hh in range(2):
                    b, bk = sbank_next()
                    pvb = pb[b][:, :].bitcast(BF16).rearrange("p (a c) -> p a c", c=128)

                    def ktf(e, pvb=pvb, hh=hh):
                        ins = None
                        for tt in range(NT):
                            ins = e.transpose(out=pvb[:, tt, :], in_=kte_tok[:, tt, hh * 128:(hh + 1) * 128], identity=identb[:])
                        return ins
                    S.op("pe", ktf, reads=[("kte", tt) for tt in range(NT)] + ["identb"], writes=[bk])
                    S.op("act", (lambda e, pvb=pvb, hh=hh: e.activation(out=kteT[:, hh, :].rearrange("p (a c) -> p a c", c=128), in_=pvb[:, 0:8, :], func=AF.Copy)),
                         reads=[bk], writes=[("kteT", hh)])
            si = next_tile()
            for tt in range(NT):
                ps, pk = acc_next()
                proj_tm(si, tt, ps, pk)
                S.op("act", (lambda e, ps=ps, tt=tt: e.activation(out=v_tok[:, tt, :], in_=ps[:, 0:256], func=AF.Copy)),
                     reads=[pk], writes=[("vtok", tt)])
                for j in range(2):
                    S.op("act", (lambda e, ps=ps, tt=tt, j=j: e.activation(out=vm[j * 64:(j + 1) * 64, tt, j, :], in_=ps[j * 64:(j + 1) * 64, 0:256], func=AF.Copy)),
                         reads=[pk, "vm0"], writes=[("vm", tt, j)])
                if tt == 1:
                    kte_transposes()
                if tt >= 2 and tt % 2 == 0:
                    scan_group(tt // 2 - 1)
            if not own:
                scan_group(3)
                return
            if HST < 4:
                return
            sq = next_tile()
            qg = 0
            for hh in range(2):
                for half in range(2):
                    ps, pk = acc_next()
                    proj_fm(sq, hh * 128, half, ps, pk)
                    S.op("dve", (lambda e, ps=ps, hh=hh, half=half: e.tensor_tensor(out=qeT[:, hh, half * 512:(half + 1) * 512], in0=ps[:, 0:512], in1=enRT[:, hh, half * 512:(half + 1) * 512], op=ALU.mult)),
                         reads=[pk] + [("enRT", hh, t_) for t_ in range(half * 4, half * 4 + 4)], writes=[("qeT", hh, half)])
                    if qg == 0:
                        scan_group(3)
                    qg += 1
            sgw = next_tile()
            for hh in range(2):
                for half in range(2):
                    ps, pk = acc_next()
                    proj_fm(sgw, hh * 128, half, ps, pk)
                    S.op("act", (lambda e, ps=ps, hh=hh, half=half: e.activation(out=sgT[:, hh, half * 512:(half + 1) * 512], in_=ps[:, 0:512], func=AF.Silu)),
                         reads=[pk], writes=[("sgT", hh, half)])
            gi = 0
            links = []
            for hh in range(2):
                h = hp * 2 + hh
                ob = hh % 2
                for half in range(2):
                    g = gi % 2
                    gi += 1
                    bs = 6 + g
                    bo = 4 + g
                    tts = list(range(half * 4, half * 4 + 4))
                    pS = pb[bs][:, :].rearrange("p (a c) -> p a c", c=128)
                    pO = pb[bo][:, :].rearrange("p (a c) -> p a c", c=128)

                    def link1(pS=pS, hh=hh, tts=tts, g=g, bs=bs, half=half):
                        def scf(e):
                            ins = None
                            for i, tt in enumerate(tts):
                                ins = e.matmul(pS[:, i, :], lhsT=kteT[:, hh, tt * 128:(tt + 1) * 128], rhs=qeT[:, hh, tt * 128:(tt + 1) * 128], start=True, stop=True)
                            return ins
                        S.op("pe", scf, reads=[("kteT", hh), ("qeT", hh, half)], writes=[("ps", bs)])
                        S.op("dve", (lambda e: e.tensor_tensor(out=scTm[g], in0=pS[:, 0:4, :], in1=maskA[:].unsqueeze(1).to_broadcast([128, 4, 128]), op=ALU.mult)),
                             reads=[("ps", bs), "maskA"], writes=[("scTm", g)])

                    def link2(pO=pO, hh=hh, tts=tts, g=g, bo=bo, half=half):
                        def omm(e):
                            ins = None
                            for i, tt in enumerate(tts):
                                e.matmul(pO[:, i, :], lhsT=scTm[g][:, i, :], rhs=v_tok[:, tt, hh * 128:(hh + 1) * 128], start=True, stop=False)
                                e.matmul(pO[0:64, i, :], lhsT=qeT[:, hh, tt * 128:tt * 128 + 64], rhs=Dbf[:, hh, 2 * tt, :], start=False, stop=True)
                                ins = e.matmul(pO[64:128, i, :], lhsT=qeT[:, hh, tt * 128 + 64:tt * 128 + 128], rhs=Dbf[:, hh, 2 * tt + 1, :], start=False, stop=True)
                            return ins
                        S.op("pe", omm, reads=[("scTm", g), ("qeT", hh, half)] + [("vtok", tt) for tt in tts] + [("Dbf", hh, n) for n in range(half * 8, half * 8 + 8)], writes=[("ps", bo)])
                        c0 = g * 16
                        for i in range(4):
                            S.op("act", (lambda e, i=i: e.activation(out=on_tok[g][:, i, :], in_=pO[:, i, :], func=AF.Square, accum_out=hst[:, c0 + i:c0 + i + 1])),
                                 reads=[("ps", bo)], writes=[("on", g), ("hs0", g, i)])
                        S.op("act", (lambda e: e.activation(out=hst[:, c0 + 4:c0 + 8], in_=hst[:, c0:c0 + 4], func=AF.Ln, bias=eps6[:, 0:1], scale=1.0 / 128)),
                             reads=[("hs0", g, i) for i in range(4)] + ["eps"], writes=[("hs1", g)])
                        S.op("act", (lambda e: e.activation(out=hst[:, c0 + 8:c0 + 12], in_=hst[:, c0 + 4:c0 + 8], func=AF.Exp, scale=-0.5)),
                             reads=[("hs1", g)], writes=[("hs2", g)])
                        for i in range(4):
                            S.op("dve", (lambda e, i=i: e.scalar_tensor_tensor(out=on_tok[g][:, i, :], in0=pO[:, i, :], scalar=hst[:, c0 + 8 + i:c0 + 9 + i], in1=hnwB[:], op0=ALU.mult, op1=ALU.mult)),
                                 reads=[("ps", bo), ("hs2", g), ("on", g), "hnwB"], writes=[("on", g)])

                    def link3(pS=pS, hh=hh, g=g, bs=bs, half=half, ob=ob, h=h):
                        def otf(e):
                            ins = None
                            for i in range(4):
                                ins = e.transpose(out=pS[:, i, :], in_=on_tok[g][:, i, :], identity=identf[:])
                            return ins
                        S.op("pe", otf, reads=[("on", g), "identf"], writes=[("ps", bs)])
                        S.op("dve", (lambda e: e.tensor_tensor(out=oTh[ob][:, half * 512:(half + 1) * 512], in0=pS[:, 0:4, :].rearrange("p a c -> p (a c)"), in1=sgT[:, hh, half * 512:(half + 1) * 512], op=ALU.mult)),
                             reads=[("ps", bs), ("sgT", hh, half)], writes=[("oTh", ob)])
                        if half == 1:
                            S.op("sp", (lambda e: e.dma_start(out=oT_scr[h], in_=oTh[ob])), reads=[("oTh", ob)], writes=[("oTs", h)], dma="d_oTh%d" % ob)
                    links.append((link1, link2, link3))
            order = [(0, 0), (0, 1), (1, 0), (1, 1), (0, 2), (2, 0), (1, 2), (2, 1), (0, 3), (3, 0), (1, 3), (2, 2), (3, 1), (3, 2)]
            L = links
            seq = [L[0][0], L[0][1], L[1][0], L[1][1], L[0][2], L[2][0], L[2][1], L[1][2], L[3][0], L[3][1], L[2][2], L[3][2]]
            deferred.extend(seq)


        def attn_prev(h):
            a1 = Arena(big, A1_LO, A1_HI)
            kTp = a1.alloc([128, 2, 1024], BF16)
            Vp = a1.alloc([128, 8, 256], BF16)
            skw = next_tile()
            for m in range(2):
                for half in range(2):
                    ps, pk = acc_next()
                    proj_fm(skw, m * 128, half, ps, pk)
                    S.op("act", (lambda e, ps=ps, m=m, half=half: e.activation(out=kTp[:, m, half * 512:(half + 1) * 512], in_=ps[:, 0:512], func=AF.Copy)),
                         reads=[pk], writes=[("kTp", m, half)])
            S.op("sp", (lambda e: e.dma_start(out=kprev[h], in_=kTp.rearrange("p a b -> p (a b)"))),
                 reads=[("kTp", m, half) for m in range(2) for half in range(2)], writes=[("kprev", h)], dma="d_kTp")
            svw = next_tile()
            for tt in range(NT):
                ps, pk = acc_next()
                proj_tm(svw, tt, ps, pk)
                S.op("dve", (lambda e, ps=ps, tt=tt: e.tensor_copy(out=Vp[:, tt, :], in_=ps[:, 0:256])),
                     reads=[pk], writes=[("Vp", tt)])
            S.op("sp", (lambda e: e.dma_start(out=vprev[h], in_=Vp.rearrange("p a b -> p (a b)"))),
                 reads=[("Vp", tt) for tt in range(NT)], writes=[("vprev", h)], dma="d_Vp")

        def attn_own(h):
            a1 = Arena(big, A1_LO, A1_HI)
            a2 = Arena(wk2, 0, WK2N)
            kT = a1.alloc([128, 2, 2048], BF16)
            V = a1.alloc([128, 16, 264], BF16)
            qT = a1.alloc([128, 2, 1024], BF16)
            sgT = a1.alloc([128, 2, 1024], BF16)
            PT = [a1.alloc([128, 128], BF16) for _ in range(16)]
            ob_ = [a2.alloc([128, 256], F32) for _ in range(2)]
            tb_ = [a2.alloc([128, 256], F32) for _ in range(2)]
            sqj = a2.alloc([128, 256], BF16)
            oTh = [a2.alloc([128, 1024], BF16) for _ in range(2)]
            bst = a2.alloc([128, 64], F32)
            S.op("pool", lambda e: e.memset(V[:, :, 256:257], 1.0), writes=["Vones"])
            S.op("sp", (lambda e: e.dma_start(out=kT[:, :, 0:1024], in_=kprev[h].rearrange("p (a b) -> p a b", a=2))),
                 reads=[("kprev", h)], writes=["kTprev"], dma="d_kTl")
            S.op("sp", (lambda e: e.dma_start(out=V[:, 0:8, 0:256], in_=vprev[h].rearrange("p (a b) -> p a b", a=8))),
                 reads=[("vprev", h)], writes=["Vprev"], dma="d_Vl")
            skw = next_tile()
            for m in range(2):
                for half in range(2):
                    ps, pk = acc_next()
                    proj_fm(skw, m * 128, half, ps, pk)
                    S.op("act", (lambda e, ps=ps, m=m, half=half: e.activation(out=kT[:, m, 1024 + half * 512:1024 + (half + 1) * 512], in_=ps[:, 0:512], func=AF.Copy)),
                         reads=[pk], writes=[("kT", m, half)])
                    pump(1)
            svw = next_tile()
            for tt in range(NT):
                ps, pk = acc_next()
                proj_tm(svw, tt, ps, pk)
                S.op("dve", (lambda e, ps=ps, tt=tt: e.tensor_copy(out=V[:, 8 + tt, 0:256], in_=ps[:, 0:256])),
                     reads=[pk], writes=[("V", tt)])
            sqw = next_tile()
            for m in range(2):
                for half in range(2):
                    ps, pk = acc_next()
                    proj_fm(sqw, m * 128, half, ps, pk)
                    S.op("act", (lambda e, ps=ps, m=m, half=half: e.activation(out=qT[:, m, half * 512:(half + 1) * 512], in_=ps[:, 0:512], func=AF.Copy)),
                         reads=[pk], writes=[("qT", m, half)])
            sgw = next_tile()
            for j in range(2):
                for half in range(2):
                    ps, pk = acc_next()
                    proj_fm(sgw, j * 128, half, ps, pk)
                    S.op("act", (lambda e, ps=ps, j=j, half=half: e.activation(out=sgT[:, j, half * 512:(half + 1) * 512], in_=ps[:, 0:512], func=AF.Silu)),
                         reads=[pk], writes=[("sgT", j, half)])
            ptc = [0]
            for qb in range(NT):
                qhalf = qb // 4
                nkb = 9 + qb

                def kkeys(m, kbs):
                    ks = set()
                    for kb in kbs:
                        ks.add("kTprev" if kb < 8 else ("kT", m, (kb - 8) // 4))
                    return list(ks)

                def vkeys(kbs):
                    ks = {"Vones"}
                    for kb in kbs:
                        ks.add("Vprev" if kb < 8 else ("V", kb - 8))
                    return list(ks)

                groups = []
                for g0 in range(0, nkb, 4):
                    for m in range(2):
                        groups.append((m, list(range(g0, min(g0 + 4, nkb)))))

                oa = (4, 5) if qb % 2 == 0 else (2, 3)

                def issue_s(m, kbs, qb=qb, nkb=nkb):
                    b = (6, 7, 0, 1)[cnt["s4"] % 4]
                    cnt["s4"] += 1
                    bk = ("ps", b)
                    pS = pb[b][:, :].rearrange("p (a c) -> p a c", c=128)

                    def fn(e, pS=pS, m=m, kbs=kbs):
                        ins = None
                        for i, kb in enumerate(kbs):
                            diag = (kb == nkb - 1)
                            ins = e.matmul(pS[:, i, :], lhsT=kT[:, m, kb * 128:(kb + 1) * 128], rhs=qT[:, m, qb * 128:(qb + 1) * 128], start=True, stop=not diag)
                            if diag:
                                ins = e.matmul(pS[:, i, :], lhsT=identb[:], rhs=cmask[:], start=False, stop=True)
                        return ins
                    S.op("pe", fn, reads=kkeys(m, kbs) + [("qT", m, qhalf), "identb", "cmask"], writes=[bk])
                    slots = []
                    for i, kb in enumerate(kbs):
                        pi = ptc[0] % 16
                        ptc[0] += 1
                        S.op("act", (lambda e, pS=pS, i=i, pi=pi, kb=kb: e.activation(out=PT[pi], in_=pS[:, i, :], func=AF.Exp, bias=alibi[:, h, qb, kb:kb + 1], scale=QSCALE)),
                             reads=[bk, "alibi"], writes=[("PT", pi)])
                        slots.append(pi)
                    return slots

                def issue_av(m, kbs, slots, nkb=nkb, oa=oa):
                    po = pb[oa[m]]

                    def fn(e, po=po, kbs=kbs, slots=slots):
                        ins = None
                        for kb, pi in zip(kbs, slots):
                            ins = e.matmul(po[:, 0:257], lhsT=PT[pi], rhs=V[:, kb, 0:257], start=(kb == 0), stop=(kb == nkb - 1))
                        return ins
                    S.op("pe", fn, reads=[("PT", pi) for pi in slots] + vkeys(kbs), writes=[("ps", oa[m])])

                pend = []
                for gi_, (m, kbs) in enumerate(groups):
                    slots = issue_s(m, kbs)
                    pend.append((m, kbs, slots))
                    if gi_ == 1:
                        pump(1)
                    if len(pend) > 2:
                        issue_av(*pend.pop(0))
                while pend:
                    issue_av(*pend.pop(0))

                ib = qb % 2
                o0, o1 = oa
                S.op("dve", (lambda e, o0=o0: e.reciprocal(out=bst[:, 0:1], in_=pb[o0][:, 256:257])), reads=[("ps", o0)], writes=["r0"])
                S.op("dve", (lambda e, o1=o1: e.reciprocal(out=bst[:, 1:2], in_=pb[o1][:, 256:257])), reads=[("ps", o1)], writes=["r1"])
                S.op("dve", (lambda e: e.tensor_tensor(out=bst[:, 2:3], in0=bst[:, 1:2], in1=neglam, op=ALU.mult)), reads=["r1", "neglam"], writes=["r1l"])
                S.op("dve", (lambda e, ib=ib, o1=o1: e.tensor_scalar(out=tb_[ib], in0=pb[o1][:, 0:256], scalar1=bst[:, 2:3], scalar2=None, op0=ALU.mult)),
                     reads=[("ps", o1), "r1l"], writes=[("tb", ib)])
                S.op("dve", (lambda e, ib=ib, o0=o0: e.scalar_tensor_tensor(out=ob_[ib], in0=pb[o0][:, 0:256], scalar=bst[:, 0:1], in1=tb_[ib], op0=ALU.mult, op1=ALU.add)),
                     reads=[("ps", o0), "r0", ("tb", ib)], writes=[("ob", ib)])
                c0 = 8 + ib * 4
                S.op("act", (lambda e, ib=ib, c0=c0: e.activation(out=sqj, in_=ob_[ib], func=AF.Square, accum_out=bst[:, c0:c0 + 1])),
                     reads=[("ob", ib)], writes=["sqj", ("bs0", ib)])
                S.op("act", (lambda e, c0=c0: e.activation(out=bst[:, c0 + 1:c0 + 2], in_=bst[:, c0:c0 + 1], func=AF.Ln, bias=eps5[:, 0:1], scale=1.0 / 256)),
                     reads=[("bs0", ib), "eps"], writes=[("bs1", ib)])
                S.op("act", (lambda e, c0=c0: e.activation(out=bst[:, c0 + 2:c0 + 3], in_=bst[:, c0 + 1:c0 + 2], func=AF.Exp, scale=-0.5)),
                     reads=[("bs1", ib)], writes=[("bs2", ib)])
                S.op("dve", (lambda e, ib=ib, c0=c0: e.scalar_tensor_tensor(out=ob_[ib], in0=ob_[ib], scalar=bst[:, c0 + 2:c0 + 3], in1=sublnB[:], op0=ALU.mult, op1=ALU.mult)),
                     reads=[("ob", ib), ("bs2", ib), "sublnB"], writes=[("ob", ib)])

                def fin_pe(ib=ib, qb=qb, qhalf=qhalf):
                    bT = (6, 7, 0, 1)[cnt["s4"] % 4]
                    cnt["s4"] += 1
                    bTk = ("ps", bT)

                    def btf(e):
                        ins = None
                        for j in range(2):
                            ins = e.transpose(out=pb[bT][:, j * 128:(j + 1) * 128], in_=ob_[ib][:, j * 128:(j + 1) * 128], identity=identf[:])
                        return ins
                    S.op("pe", btf, reads=[("ob", ib), "identf"], writes=[bTk])
                    for j in range(2):
                        S.op("dve", (lambda e, j=j: e.tensor_tensor(out=oTh[j][:, qb * 128:(qb + 1) * 128], in0=pb[bT][:, j * 128:(j + 1) * 128], in1=sgT[:, j, qb * 128:(qb + 1) * 128], op=ALU.mult)),
                             reads=[bTk, ("sgT", j, qhalf)], writes=[("oThB", j)])
                    if qb == NT - 1:
                        for j in range(2):
                            S.op("sp", (lambda e, j=j: e.dma_start(out=oT_scr[16 + 2 * h + j], in_=oTh[j])), reads=[("oThB", j)], writes=[("oTs", 16 + 2 * h + j)], dma="d_oThB%d" % j)
                deferred.append(fin_pe)

        def gates():
            a2 = Arena(wk2, 0, WK2N)
            sigt = [a2.alloc([128, 1024], BF16) for _ in range(4)]
            gc = 0
            for gt in range(NGT):
                sw = next_tile()
                for j in range(2):
                    b = gc % 4
                    gc += 1
                    for half in range(2):
                        ps, pk = acc_next()
                        proj_fm(sw, j * 128, half, ps, pk)
                        S.op("act", (lambda e, ps=ps, b=b, half=half: e.activation(out=sigt[b][:, half * 512:(half + 1) * 512], in_=ps[:, 0:512], func=AF.Sigmoid)),
                             reads=[pk], writes=[("sigt", b, half)])
                    S.op("act", (lambda e, b=b, gt=gt, j=j: e.dma_start(out=sig_scr[gt * 2 + j], in_=sigt[b])),
                         reads=[("sigt", b, 0), ("sigt", b, 1)], writes=[("sigs", gt * 2 + j)], dma="d_sig%d" % b)

        def phase_C():
            oT = uT
            for q4 in range(4):
                S.op("sp", (lambda e, q4=q4: e.dma_start(out=oT[:, q4 * 8:(q4 + 1) * 8, :], in_=oT_scr[q4 * 8:(q4 + 1) * 8].rearrange("c p t -> p c t"))),
                     reads=[("oTs", c) for c in range(q4 * 8, q4 * 8 + 8)], writes=[("oT", q4)], dma="d_oTl%d" % q4)
            a2 = Arena(wk2, 0, WK2N)
            sA = [a2.alloc([128, 1024], BF16) for _ in range(4)]
            sB = [a2.alloc([128, 1024], BF16) for _ in range(4)]

            def sig_load(ft):
                for j in range(2):
                    fc = ft * 2 + j
                    b = fc % 4
                    S.op("act", (lambda e, b=b, fc=fc: e.dma_start(out=sA[b], in_=sig_scr[fc])), reads=[("sigs", fc)], writes=[("sA", b)], dma="d_sA%d" % b)
                    S.op("act", (lambda e, b=b, fc=fc: e.dma_start(out=sB[b], in_=sig_scr[32 + fc])), reads=[("sigs", 32 + fc)], writes=[("sB", b)], dma="d_sB%d" % b)
            sig_load(0)
            tA = [a2.alloc([128, 512], F32) for _ in range(2)]
            tB = [a2.alloc([128, 512], F32) for _ in range(2)]
            yst = [a2.alloc([128, 1024], BF16) for _ in range(2)]
            fcn = 0
            for ft in range(16):
                sw = next_tile()
                if ft + 1 < 16:
                    sig_load(ft + 1)
                for j in range(2):
                    fc = ft * 2 + j
                    b = fc % 4
                    yb = fc % 2
                    for half in range(2):
                        psA, pkA = acc_next()
                        proj_fm(sw, j * 128, half, psA, pkA, src=oT, srckeys=[("oT", 0), ("oT", 1)], k0=0, nk=16, ksrc0=0)
                        psB, pkB = acc_next()
                        proj_fm(sw, j * 128, half, psB, pkB, src=oT, srckeys=[("oT", 2), ("oT", 3)], k0=16, nk=16, ksrc0=16)
                        tb = half
                        S.op("dve", (lambda e, psA=psA, b=b, half=half, tb=tb: e.tensor_tensor(out=tA[tb], in0=psA[:, 0:512], in1=sA[b][:, half * 512:(half + 1) * 512], op=ALU.mult)),
                             reads=[pkA, ("sA", b)], writes=[("tA", tb)])
                        S.op("dve", (lambda e, psB=psB, b=b, half=half, tb=tb: e.tensor_tensor(out=tB[tb], in0=psB[:, 0:512], in1=sB[b][:, half * 512:(half + 1) * 512], op=ALU.mult)),
                             reads=[pkB, ("sB", b)], writes=[("tB", tb)])
                        S.op("dve", (lambda e, yb=yb, half=half, tb=tb: e.tensor_tensor(out=yst[yb][:, half * 512:(half + 1) * 512], in0=tA[tb], in1=tB[tb], op=ALU.add)),
                             reads=[("tA", tb), ("tB", tb)], writes=[("yst", yb, half)])
                    S.op("act", (lambda e, yb=yb, fc=fc: e.dma_start(out=yT_scr[fc], in_=yst[yb])),
                         reads=[("yst", yb, 0), ("yst", yb, 1)], writes=[("yTs", fc)], dma="d_yst%d" % yb)
            S.barrier()
            yT = uT
            for q4 in range(4):
                S.op("sp", (lambda e, q4=q4: e.dma_start(out=yT[:, q4 * 8:(q4 + 1) * 8, :], in_=yT_scr[q4 * 8:(q4 + 1) * 8].rearrange("c p t -> p c t"))),
                     reads=[("yTs", c) for c in range(q4 * 8, q4 * 8 + 8)], writes=[("yT", q4)], dma="d_yTl%d" % q4)
            a2 = Arena(wk2, 0, WK2N)
            hsb = [a2.alloc([128, 256], F32) for _ in range(4)]
            xr = [a2.alloc([128, 256], F32) for _ in range(4)]
            sqj = a2.alloc([128, 256], BF16)
            ssq = a2.alloc([128, 8, 16], F32)
            fst = a2.alloc([128, 32], F32)
            rc = 0
            for cb in range(16):
                sw = next_tile()
                for tt in range(NT):
                    r = rc % 4
                    rc += 1
                    S.op("act", (lambda e, r=r, tt=tt, cb=cb: e.dma_start(out=xr[r], in_=x_own[tt * 128:(tt + 1) * 128, cb * 256:(cb + 1) * 256])),
                         writes=[("xr", r)], dma="d_xr%d" % r)
                    ps, pk = acc_next()
                    proj_tm(sw, tt, ps, pk, src=yT, srckeys=[("yT", q4) for q4 in range(4)])
                    S.op("dve", (lambda e, ps=ps, r=r: e.tensor_tensor(out=hsb[r], in0=ps[:, 0:256], in1=xr[r], op=ALU.add)),
                         reads=[pk, ("xr", r)], writes=[("hsb", r)])
                    S.op("act", (lambda e, r=r, tt=tt, cb=cb: e.activation(out=sqj, in_=hsb[r], func=AF.Square, accum_out=ssq[:, tt, cb:cb + 1])),
                         reads=[("hsb", r)], writes=["sqjC", ("ssq", tt, cb)])
                    S.op("act", (lambda e, r=r, tt=tt, cb=cb: e.dma_start(out=out[tt * 128:(tt + 1) * 128, cb * 256:(cb + 1) * 256], in_=hsb[r])),
                         reads=[("hsb", r)], writes=[("outh", tt, cb)], dma="d_hs%d" % r)
            S.barrier()
            arF = Arena(big, 0, R2)
            fwB = arF.alloc([128, D], F32)
            hrow = [arF.alloc([128, D], F32) for _ in range(2)]
            S.op("sp", (lambda e: e.dma_start(out=fwB, in_=fwB_d[:, :])), writes=["fwB"], dma="d_fwB")
            for tt in range(NT):
                b = tt % 2
                S.op("sp", (lambda e, b=b, tt=tt: e.dma_start(out=hrow[b], in_=out[tt * 128:(tt + 1) * 128, :])),
                     reads=[("outh", tt, cb) for cb in range(16)], writes=[("hrow", b)], dma="d_hrl%d" % b)
                S.op("dve", (lambda e, tt=tt: e.reduce_sum(out=fst[:, tt * 3:tt * 3 + 1], in_=ssq[:, tt, :], axis=AX.X)),
                     reads=[("ssq", tt, cb) for cb in range(16)], writes=[("f0", tt)])
                S.op("act", (lambda e, tt=tt: e.activation(out=fst[:, tt * 3 + 1:tt * 3 + 2], in_=fst[:, tt * 3:tt * 3 + 1], func=AF.Sqrt, bias=1e-6, scale=1.0 / D)),
                     reads=[("f0", tt)], writes=[("f1", tt)])
                S.op("dve", (lambda e, tt=tt: e.reciprocal(out=fst[:, tt * 3 + 2:tt * 3 + 3], in_=fst[:, tt * 3 + 1:tt * 3 + 2])),
                     reads=[("f1", tt)], writes=[("f2", tt)])
                S.op("act", (lambda e, b=b, tt=tt: e.activation(out=hrow[b], in_=hrow[b], func=AF.Copy, scale=fst[:, tt * 3 + 2:tt * 3 + 3])),
                     reads=[("hrow", b), ("f2", tt)], writes=[("hrowB", b)])
                S.op("dve", (lambda e, b=b, tt=tt: e.tensor_tensor(out=hrow[b], in0=hrow[b], in1=fwB, op=ALU.mult)),
                     reads=[("hrowB", b), "fwB"], writes=[("hrowA", b)])
                S.op("sp", (lambda e, b=b, tt=tt: e.dma_start(out=out[tt * 128:(tt + 1) * 128, :], in_=hrow[b])),
                     reads=[("hrowA", b), ("hrowB", b)], writes=[("outf", tt), ("hrow", b)], dma="d_hrs%d" % b)
            S.op("sp", None, reads=[("outf", tt) for tt in range(NT)])

        for hp in range(NPAIR):
            plan.append(win_tile(C_AF + hp * 256))
            plan.append(win_tile(C_AI + hp * 256))
        for h in range(NBH):
            plan.append(win_tile(C_BK + h * 256))
            plan.append(win_tile(C_BV + h * 256))
        for hp in range(NPAIR):
            for c in (C_AF, C_AI, C_AQ, C_AG):
                plan.append(win_tile(c + hp * 256))
        for h in range(NBH):
            for c in (C_BK, C_BV, C_BQ, C_BG):
                plan.append(win_tile(c + h * 256))
        wide0[0] = len(plan)
        for gt in range(NGT):
            plan.append(win_tile(C_GA + gt * 256))
        if DO_C:
            for ft in range(16):
                plan.append([w_a[0:1024, ft * 256:(ft + 1) * 256], w_a[1024:2048, ft * 256:(ft + 1) * 256],
                             w_b[0:1024, ft * 256:(ft + 1) * 256], w_b[1024:2048, ft * 256:(ft + 1) * 256]])
            for cb in range(16):
                plan.append([w_out[j * 1024:(j + 1) * 1024, cb * 256:(cb + 1) * 256] for j in range(4)])

        phase_A(x_prev, "p")
        for hp in range(NPAIR):
            hgrn_pair(hp, own=False)
        S.barrier()
        for h in range(NBH):
            attn_prev(h)
        S.barrier()
        phase_A(x_own, "o")
        for hp in range(NPAIR):
            hgrn_pair(hp, own=True)
        pump(999)
        S.barrier()
        for h in range(NBH):
            attn_own(h)
        pump(999)
        S.barrier()
        gates()
        S.barrier()
        if DO_C:
            phase_C()
        else:
            S.op("sp", None, reads=[("oTs", c) for c in range(32)] + [("sigs", c) for c in range(64)])
        assert HST < 99 or cur[0] == len(plan), (cur[0], len(plan))
        S.emit(nc, st)
    return nc


def host_consts(half):
    p = np.arange(128)
    identf = np.eye(128, dtype=np.float32)
    TI = np.zeros((128, 130), np.float32)
    s = p[:, None]
    t = p[None, :]
    TI[:, :128] = ((s // 64 == t // 64) & (s > t)).astype(np.float32)
    TI[:, 128] = (p < 64)
    TI[:, 129] = (p >= 64)
    maskA = ((s // 64 == t // 64) & (s <= t)).astype(np.float32)
    cmask = np.where(s <= t, 0.0, -30000.0).astype(np.float32)
    slopes = np.exp2(-(np.arange(8, dtype=np.float64) + 1.0))
    al = np.zeros((128, 8, 8, 16), np.float32)
    for qb in range(8):
        gq = half * 8 + qb
        for kb in range(16):
            gk = kb if half == 1 else kb - 8
            if gk < 0:
                al[:, :, qb, kb] = -30000.0
            else:
                kpos = gk * 128 + p
                cref = gq * 128 + 64
                al[:, :, qb, kb] = (slopes[None, :] * (kpos[:, None] - cref)).astype(np.float32)
    return dict(identf=identf, TI=TI, maskA=maskA, cmask=cmask, alibi=al.reshape(128, 1024))


def make_in_maps(x, norm_w, w_in, lower_bound_table, hgrn_norm_w, lambda_q1, lambda_k1,
                 lambda_q2, lambda_k2, subln_w, w_branch_a, w_branch_b, w_out, final_w, cores=range(NCORES)):
    f = lambda a: np.ascontiguousarray(np.asarray(a, dtype=np.float32))
    x = f(x)
    shared = dict(
        w_in=f(w_in[0]), w_a=f(w_branch_a[0]), w_b=f(w_branch_b[0]), w_out=f(w_out[0]),
        normwT=f(np.asarray(norm_w[0]).reshape(32, 128).T),
        fwB=f(np.broadcast_to(np.asarray(final_w)[None, :], (128, D))),
        lbt=f(np.broadcast_to(np.asarray(lower_bound_table).reshape(1, 4096), (128, 4096))),
        hnwB=f(np.broadcast_to(np.asarray(hgrn_norm_w[0])[None, :], (128, 128))),
        sublnB=f(np.broadcast_to(np.asarray(subln_w[0])[None, :], (128, 256))),
        lamv=f(np.broadcast_to(np.concatenate([np.asarray(lambda_q1[0]), np.asarray(lambda_k1[0]),
                                               np.asarray(lambda_q2[0]), np.asarray(lambda_k2[0])])[None, :], (128, 512))),
    )
    hc = [host_consts(0), host_consts(1)]
    zeros = np.zeros((T, D), np.float32)
    maps = []
    for c in cores:
        b, half = c // 2, c % 2
        m = dict(shared)
        m.update(hc[half])
        m["x_own"] = f(x[b, half * T:(half + 1) * T])
        m["x_prev"] = f(x[b, 0:T]) if half == 1 else zeros
        maps.append(m)
    return maps


_NC_CACHE = {}


def kernel(**inputs):
    if "nc" not in _NC_CACHE:
        _NC_CACHE["nc"] = build_program()
    nc = _NC_CACHE["nc"]
    maps = make_in_maps(**inputs)
    res = run_bass_kernel_spmd(nc, maps, core_ids=list(range(NCORES)))
    outp = np.zeros((4, 2048, D), np.float32)
    for c in range(NCORES):
        b, half = c // 2, c % 2
        outp[b, half * T:(half + 1) * T] = np.asarray(res.results[c]["out"], dtype=np.float32)
    return outp
```

```python
import numpy as np
from contextlib import ExitStack
import concourse.bass as bass
import concourse.mybir as mybir
from concourse.bass_utils import run_bass_kernel_spmd

F32 = mybir.dt.float32
BF16 = mybir.dt.bfloat16
AF = mybir.ActivationFunctionType
ALU = mybir.AluOpType
AX = mybir.AxisListType

NCORES = 8
T = 1024
NT = 8
D = 4096
KC = 32
NIN = 24576
QSCALE = 128 ** -0.5
WK2N = 14336
C_AQ, C_AF, C_AI, C_AG, C_BQ, C_BK, C_BV, C_BG, C_GA, C_GB = 0, 2048, 4096, 6144, 8192, 10240, 12288, 14336, 16384, 20480


class Sched:
    ENG = ("pe", "act", "dve", "pool", "sp")

    def __init__(self):
        self.ops = []
        self.last_writer = {}
        self.readers = {}
        self.dom_pos = {}
        self.pending = {e: set() for e in self.ENG}
        self.last_in_dom = {}

    def op(self, eng, fn, reads=(), writes=(), dma=None, drain=False):
        oid = len(self.ops)
        raw = set()
        oth = set()
        force = set()
        if drain and eng in self.last_in_dom:
            force.add(self.last_in_dom[eng])
        for k in reads:
            w = self.last_writer.get(k)
            if w is not None:
                raw.add(w)
        for k in writes:
            w = self.last_writer.get(k)
            if w is not None:
                oth.add(w)
            for r in self.readers.get(k, ()):
                oth.add(r)
        raw |= self.pending[eng]
        self.pending[eng] = set()
        for k in reads:
            self.readers.setdefault(k, []).append(oid)
        for k in writes:
            self.last_writer[k] = oid
            self.readers[k] = []
        dom = dma if dma else eng
        pos = self.dom_pos.get(dom, 0)
        self.dom_pos[dom] = pos + 1
        self.ops.append(dict(id=oid, eng=eng, fn=fn, dom=dom, pos=pos, raw=raw, oth=oth,
                             signal=False, waits=[], isdma=dma is not None, force=force))
        self.last_in_dom[dom] = oid
        return oid

    def barrier(self):
        s = set(self.last_in_dom.values())
        for e in self.ENG:
            self.pending[e] |= s

    def finalize(self):
        waited = {}
        by_dom = {}
        for o in self.ops:
            by_dom.setdefault(o["dom"], []).append(o)
        for o in self.ops:
            need = {}
            F = o["eng"]
            for d in o["raw"] | o["oth"] | o["force"]:
                dd = self.ops[d]
                E = dd["dom"]
                if (not dd["isdma"]) and E == F and d not in o["force"]:
                    if F in ("pe", "sp"):
                        continue
                if waited.get((F, E), -1) >= dd["pos"]:
                    continue
                need[E] = max(need.get(E, -1), dd["pos"])
            for E, p in need.items():
                by_dom[E][p]["signal"] = True
                waited[(F, E)] = p
                o["waits"].append((E, p))
        for dom, lst in by_dom.items():
            c = 0
            for o in lst:
                if o["signal"]:
                    c += 1
                    o["val"] = c * (16 if o["isdma"] else 1)
        for o in self.ops:
            o["waitvals"] = [(E, by_dom[E][p]["val"]) for (E, p) in o["waits"]]
        self.domains = list(by_dom.keys())

    def emit(self, nc, stack):
        self.finalize()
        sems = {}
        for dom in self.domains:
            sems[dom] = stack.enter_context(nc.semaphore("s_" + dom))
        block = stack.enter_context(nc.Block())
        streams = {e: [o for o in self.ops if o["eng"] == e] for e in self.ENG}

        def run(eh, lst):
            for o in lst:
                for (E, v) in o["waitvals"]:
                    eh.wait_ge(sems[E], v)
                ins = o["fn"](eh) if o["fn"] is not None else None
                if o["signal"]:
                    assert ins is not None
                    ins.then_inc(sems[o["dom"]], 16 if o["isdma"] else 1)

        @block.tensor
        def _(e):
            run(e, streams["pe"])

        @block.scalar
        def _(e):
            run(e, streams["act"])

        @block.vector
        def _(e):
            run(e, streams["dve"])

        @block.gpsimd
        def _(e):
            run(e, streams["pool"])

        @block.sync
        def _(e):
            run(e, streams["sp"])


class Arena:
    def __init__(self, flat, lo, hi):
        self.flat, self.lo, self.hi, self.cur = flat, lo, hi, lo

    def alloc(self, shape, dt):
        esz = 4 if dt == F32 else 2
        n = 1
        for s in shape[1:]:
            n *= s
        nel = n * esz // 2
        nel = (nel + 31) // 32 * 32
        assert self.cur + nel <= self.hi, ("arena overflow", shape, self.cur, self.hi)
        v = self.flat[:, self.cur:self.cur + n * esz // 2]
        self.cur += nel
        if dt == F32:
            v = v.bitcast(F32)
        if len(shape) == 3:
            v = v.rearrange("p (a b) -> p a b", b=shape[2])
        elif len(shape) == 4:
            v = v.rearrange("p (a b c) -> p a b c", b=shape[2], c=shape[3])
        return v


def build_program(debug=False, cfg=None):
    cfg = cfg or {}
    NPAIR = cfg.get("npair", 8)
    NBH = cfg.get("nbh", 8)
    NGT = cfg.get("ngt", 32)
    DO_C = cfg.get("do_c", True)
    HST = cfg.get("hstage", 99)
    nc = bass.Bass("TRN2", target_bir_lowering=False)

    def din(name, shape, dt=F32):
        return nc.dram_tensor(name, list(shape), dt, kind="ExternalInput").ap()

    x_own = din("x_own", [T, D])
    x_prev = din("x_prev", [T, D])
    w_in = din("w_in", [D, NIN])
    w_a = din("w_a", [2048, D])
    w_b = din("w_b", [2048, D])
    w_out = din("w_out", [D, D])
    normwT_d = din("normwT", [128, 32])
    fwB_d = din("fwB", [128, D])
    lbt_d = din("lbt", [128, 4096])
    hnwB_d = din("hnwB", [128, 128])
    sublnB_d = din("sublnB", [128, 256])
    lamv_d = din("lamv", [128, 512])
    identf_d = din("identf", [128, 128])
    TI_d = din("TI", [128, 130])
    maskA_d = din("maskA", [128, 128])
    cmask_d = din("cmask", [128, 128])
    alibi_d = din("alibi", [128, 1024])
    out = nc.dram_tensor("out", [T, D], F32, kind="ExternalOutput").ap()
    sk = "ExternalOutput" if debug else "Internal"
    kprev = nc.dram_tensor("kprev_scr", [8, 128, 2048], BF16, kind="Internal").ap()
    vprev = nc.dram_tensor("vprev_scr", [8, 128, 2048], BF16, kind="Internal").ap()
    oT_scr = nc.dram_tensor("oT_scr", [32, 128, 1024], BF16, kind=sk).ap()
    sig_scr = nc.dram_tensor("sig_scr", [64, 128, 1024], BF16, kind=sk).ap()
    yT_scr = nc.dram_tensor("yT_scr", [32, 128, 1024], BF16, kind=sk).ap()

    S = Sched()
    st = ExitStack()
    with st:
        def sb(name, shape, dt):
            return st.enter_context(nc.sbuf_tensor(name, shape, dt))

        big = sb("big", [128, 65536], BF16)
        wk2 = sb("wk2", [128, WK2N], BF16)
        stg = [sb("stg%d" % i, [128, 8, 256], F32) for i in range(3)]
        identf = sb("identf_s", [128, 128], F32)
        identb = sb("identb_s", [128, 128], BF16)
        TI = sb("TI_s", [128, 130], F32)
        maskA = sb("maskA_s", [128, 128], F32)
        cmask = sb("cmask_s", [128, 128], BF16)
        alibi = sb("alibi_s", [128, 8, 8, 16], F32)
        normwT = sb("normwT_s", [128, 32], F32)
        hnwB = sb("hnwB_s", [128, 128], F32)
        sublnB = sb("sublnB_s", [128, 256], F32)
        lbB = sb("lbB_s", [128, 2048], F32)
        Smid = sb("Smid_s", [128, 16, 128], F32)
        stat = sb("stat_s", [128, 512], F32)
        lam = stat[:, 0:1]
        neglam = stat[:, 1:2]
        eps6 = stat[:, 8:9]
        eps5 = stat[:, 9:10]
        one1 = stat[:, 10:11]
        pb = [st.enter_context(nc.psum_tensor("pb%d" % i, [128, 512], F32)) for i in range(8)]

        R2 = 32768
        uT = big[:, 0:R2].rearrange("p (k t) -> p k t", t=1024)
        W = [big[:, R2 + s * 8192: R2 + (s + 1) * 8192].rearrange("p (k c) -> p k c", c=256) for s in range(4)]
        wide0 = [10 ** 9]
        A1_LO, A1_HI = R2 + 16384, 65536

        cnt = {"acc": 0, "small": 0, "stg": 0, "w": 0, "s4": 0, "since": 0}

        late_casts = []

        def flush_casts(upto=None):
            keep = []
            while late_casts:
                t, fn = late_casts.pop(0)
                if upto is None or t <= upto:
                    fn()
                else:
                    keep.append((t, fn))
            late_casts.extend(keep)

        def acc_next():
            cnt["since"] += 1
            if cnt["since"] >= 4:
                flush_casts()
            i = cnt["acc"] % 4
            cnt["acc"] += 1
            return pb[i], ("ps", i)

        def sbank_next():
            b = 6 + cnt["small"] % 2
            cnt["small"] += 1
            return b, ("ps", b)

        plan = []
        cur = [0]
        deferred = []

        def pump(n=1):
            while n > 0 and deferred:
                deferred.pop(0)()
                n -= 1

        def slot_of(i):
            return i % 2 if i < wide0[0] else (i - wide0[0]) % 4

        loaded = [0]

        def issue_load(i):
            s = slot_of(i)
            for j, seg in enumerate(plan[i]):
                g = cnt["stg"] % 3
                cnt["stg"] += 1
                S.op("sp", (lambda e, g=g, seg=seg: e.dma_start(out=stg[g][:], in_=seg.rearrange("(k p) c -> p k c", p=128))),
                     writes=[("stg", g)], dma="d_stg%d" % g)
                late = True
                ceng = ("pool", "act", "act", "dve")[j] if i >= wide0[0] else ("act", "act", "act", "dve")[j]

                def cast(ceng=ceng, g=g, s=s, j=j):
                    if ceng == "act":
                        S.op("act", (lambda e: e.activation(out=W[s][:, j * 8:(j + 1) * 8, :], in_=stg[g][:], func=AF.Copy)),
                             reads=[("stg", g)], writes=[("w", s, j)])
                    else:
                        S.op(ceng, (lambda e: e.tensor_copy(out=W[s][:, j * 8:(j + 1) * 8, :], in_=stg[g][:])),
                             reads=[("stg", g)], writes=[("w", s, j)])
                if late and j != 0:
                    late_casts.append((i, cast))
                    cnt["since"] = 0
                else:
                    cast()

        def next_tile(check=None):
            i = cur[0]
            cur[0] += 1
            if check is not None:
                assert plan[i][0] is check[0], "tile plan mismatch at %d" % i
            flush_casts()
            la = 1 if i < wide0[0] else 3
            while loaded[0] <= min(i + la, len(plan) - 1):
                flush_casts()
                issue_load(loaded[0])
                loaded[0] += 1
            flush_casts(upto=i)
            return slot_of(i)

        def win_tile(col0):
            return [w_in[j * 1024:(j + 1) * 1024, col0:col0 + 256] for j in range(4)]

        def wkeys(s):
            return [("w", s, j) for j in range(4)]

        def proj_fm(s, c0, half, ps, pkey, src=uT, srckeys=None, k0=0, nk=32, ksrc0=0):
            def fn(e):
                ins = None
                for k in range(nk):
                    ins = e.matmul(ps[:, 0:512], lhsT=W[s][:, k0 + k, c0:c0 + 128],
                                   rhs=src[:, ksrc0 + k, half * 512:(half + 1) * 512],
                                   start=(k == 0), stop=(k == nk - 1))
                return ins
            rk = srckeys if srckeys is not None else [("uT", tt) for tt in range(half * 4, half * 4 + 4)]
            S.op("pe", fn, reads=wkeys(s) + rk, writes=[pkey])

        def proj_tm(s, tt, ps, pkey, src=uT, srckeys=None):
            def fn(e):
                ins = None
                for k in range(KC):
                    ins = e.matmul(ps[:, 0:256], lhsT=src[:, k, tt * 128:(tt + 1) * 128], rhs=W[s][:, k, :],
                                   start=(k == 0), stop=(k == KC - 1))
                return ins
            rk = srckeys if srckeys is not None else [("uT", tt)]
            S.op("pe", fn, reads=wkeys(s) + rk, writes=[pkey])

        def cload(dst, src, key):
            S.op("sp", (lambda e: e.dma_start(out=dst, in_=src)), writes=[key], dma="d_c_" + key)

        cload(identf[:], identf_d[:, :], "identf")
        cload(TI[:], TI_d[:, :], "TI")
        cload(maskA[:], maskA_d[:, :], "maskA")
        cload(alibi[:], alibi_d.rearrange("p (a b c) -> p a b c", a=8, b=8), "alibi")
        cload(normwT[:], normwT_d[:, :], "normwT")
        cload(hnwB[:], hnwB_d[:, :], "hnwB")
        cload(sublnB[:], sublnB_d[:, :], "sublnB0")
        ar0 = Arena(big, A1_LO, A1_HI)
        lbt = ar0.alloc([128, 2, 2048], F32)
        lamv = ar0.alloc([128, 4, 128], F32)
        cm32 = ar0.alloc([128, 128], F32)
        lamp = ar0.alloc([128, 2, 128], F32)
        cload(lbt, lbt_d.rearrange("p (a b) -> p a b", a=2), "lbt")
        cload(lamv, lamv_d.rearrange("p (a b) -> p a b", a=4), "lamv")
        cload(cm32, cmask_d[:, :], "cm32")
        S.op("pool", lambda e: e.tensor_copy(out=identb[:], in_=identf[:]), reads=["identf"], writes=["identb"])
        S.op("pool", lambda e: e.tensor_copy(out=cmask[:], in_=cm32), reads=["cm32"], writes=["cmask"])
        S.op("pool", lambda e: e.memset(Smid[:], 0.0), writes=["Smid"])
        S.op("pool", lambda e: e.memset(eps6, 1e-6), writes=["eps6"])
        S.op("pool", lambda e: e.memset(one1, 1.0), writes=["one1"])
        S.op("pool", lambda e: e.memset(eps5, 1e-5), writes=["eps"])
        S.op("dve", lambda e: e.tensor_tensor(out=lbt[:, 0, :], in0=lbt[:, 0, :], in1=lbt[:, 1, :], op=ALU.subtract), reads=["lbt"], writes=["lbd"])
        S.op("act", lambda e: e.activation(out=lbB[:], in_=lbt[:, 0, :], func=AF.Sigmoid), reads=["lbd"], writes=["lbB"])
        S.op("dve", lambda e: e.tensor_tensor(out=lamp[:, 0, :], in0=lamv[:, 0, :], in1=lamv[:, 1, :], op=ALU.mult), reads=["lamv"], writes=["lamp0"])
        S.op("dve", lambda e: e.tensor_tensor(out=lamp[:, 1, :], in0=lamv[:, 2, :], in1=lamv[:, 3, :], op=ALU.mult), reads=["lamv"], writes=["lamp1"])
        S.op("dve", lambda e: e.reduce_sum(out=stat[:, 2:3], in_=lamp[:, 0, :], axis=AX.X), reads=["lamp0"], writes=["ls0"])
        S.op("dve", lambda e: e.reduce_sum(out=stat[:, 3:4], in_=lamp[:, 1, :], axis=AX.X), reads=["lamp1"], writes=["ls1"])
        S.op("act", lambda e: e.activation(out=stat[:, 4:6], in_=stat[:, 2:4], func=AF.Exp), reads=["ls0", "ls1"], writes=["le"])
        S.op("dve", lambda e: e.scalar_tensor_tensor(out=lam, in0=stat[:, 4:5], scalar=0.2, in1=stat[:, 5:6], op0=ALU.add, op1=ALU.subtract), reads=["le"], writes=["lam"])
        S.op("dve", lambda e: e.tensor_scalar(out=neglam, in0=lam, scalar1=-1.0, scalar2=None, op0=ALU.mult), reads=["lam"], writes=["neglam"])
        S.op("dve", lambda e: e.tensor_scalar(out=sublnB[:], in0=sublnB[:], scalar1=0.8, scalar2=None, op0=ALU.mult), reads=["sublnB0"], writes=["sublnB"])
        S.barrier()

        def phase_A(xsrc, tag):
            arA = Arena(big, A1_LO, A1_HI)
            arA2 = Arena(wk2, 0, WK2N)
            xs = [arA.alloc([128, D], F32) for _ in range(2)]
            xn = [arA2.alloc([128, D], BF16) for _ in range(2)]
            for tt in range(NT):
                sl = tt % 2
                S.op("sp", (lambda e, sl=sl, tt=tt: e.dma_start(out=xs[sl], in_=xsrc[tt * 128:(tt + 1) * 128, :])),
                     writes=[("xs", sl)], dma="d_xs%d" % sl)
                c0 = 16 + tt * 3
                S.op("act", (lambda e, sl=sl, c0=c0: e.activation(out=xn[sl], in_=xs[sl], func=AF.Square, accum_out=stat[:, c0:c0 + 1])),
                     reads=[("xs", sl)], writes=[("xn", sl), ("ssA", tt)])
                S.op("act", (lambda e, c0=c0: e.activation(out=stat[:, c0 + 1:c0 + 2], in_=stat[:, c0:c0 + 1], func=AF.Sqrt, bias=1e-6, scale=1.0 / D)),
                     reads=[("ssA", tt)], writes=[("sqA", tt)])
                S.op("dve", (lambda e, c0=c0: e.reciprocal(out=stat[:, c0 + 2:c0 + 3], in_=stat[:, c0 + 1:c0 + 2])),
                     reads=[("sqA", tt)], writes=[("rsA", tt)])
                S.op("act", (lambda e, sl=sl, c0=c0: e.activation(out=xn[sl], in_=xs[sl], func=AF.Copy, scale=stat[:, c0 + 2:c0 + 3])),
                     reads=[("xs", sl), ("rsA", tt)], writes=[("xn", sl)])
                for g in range(4):
                    pbf = pb[g][:, :].bitcast(BF16).rearrange("p (a b) -> p a b", b=128)

                    def trf(e, sl=sl, g=g, pbf=pbf):
                        ins = None
                        for i in range(8):
                            c = g * 8 + i
                            ins = e.transpose(out=pbf[:, i, :], in_=xn[sl][:, c * 128:(c + 1) * 128], identity=identb[:])
                        return ins
                    S.op("pe", trf, reads=[("xn", sl), "identb"], writes=[("ps", g)])
                    S.op("dve", (lambda e, g=g, tt=tt, pbf=pbf: e.tensor_tensor(
                        out=uT[:, g * 8:(g + 1) * 8, tt * 128:(tt + 1) * 128], in0=pbf[:, 0:8, :],
                        in1=normwT[:, g * 8:(g + 1) * 8].unsqueeze(2).to_broadcast([128, 8, 128]), op=ALU.mult)),
                        reads=[("ps", g), "normwT"], writes=[("uT", tt)])
            S.barrier()

        def hgrn_pair(hp, own):
            a1 = Arena(big, A1_LO, A1_HI)
            a2 = Arena(wk2, 0, WK2N)
            kte_tok = a1.alloc([128, 8, 256], BF16)
            v_tok = a1.alloc([128, 8, 256], BF16)
            enRT = a1.alloc([128, 2, 1024], BF16)
            kteT = a1.alloc([128, 2, 1024], BF16)
            qeT = a1.alloc([128, 2, 1024], BF16)
            sgT = a1.alloc([128, 2, 1024], BF16)
            Dbf = a1.alloc([128, 2, 16, 128], BF16)
            sg = [a2.alloc([128, 256], F32) for _ in range(2)]
            t1 = a2.alloc([128, 256], F32)
            lf = [a2.alloc([128, 256], F32) for _ in range(2)]
            ktok = [a2.alloc([128, 256], F32) for _ in range(2)]
            eR = a2.alloc([128, 256], F32)
            omlp = a2.alloc([128, 256], F32)
            scTm = [a2.alloc([128, 4, 128], BF16) for _ in range(2)]
            on_tok = [a2.alloc([128, 4, 128], F32) for _ in range(2)]
            oTh = [a2.alloc([128, 1024], BF16) for _ in range(2)]
            dec = a2.alloc([128, 2, 16], F32)
            hst = a2.alloc([128, 64], F32)
            vm = a2.alloc([128, 8, 2, 256], BF16)
            if hp == 0:
                S.op("pool", lambda e: e.memset(vm, 0.0), writes=["vm0"] + [("vm", tt, j) for tt in range(NT) for j in range(2)])
            lbp = lbB[:, hp * 256:(hp + 1) * 256]
            P = "A%d%d" % (hp, int(own))
            S.op("dve", lambda e: e.tensor_scalar(out=omlp, in0=lbp, scalar1=-1.0, scalar2=1.0, op0=ALU.mult, op1=ALU.add),
                 reads=["lbB"], writes=["omlp"])
            sf = next_tile()

            def rstuff(tt):
                b = tt % 2
                psR, pkR = acc_next()
                S.op("pe", (lambda e, psR=psR, b=b: e.matmul(psR[:, 0:256], lhsT=TI[:, 0:128], rhs=lf[b], start=True, stop=True)),
                     reads=[("lf", b), "TI"], writes=[pkR])
                S.op("act", (lambda e, psR=psR: e.activation(out=eR, in_=psR[:, 0:256], func=AF.Exp)),
                     reads=[pkR], writes=["eR"])
                S.op("dve", (lambda e, tt=tt, b=b: e.tensor_tensor(out=kte_tok[:, tt, :], in0=ktok[b], in1=eR, op=ALU.mult)),
                     reads=[("ktok", b), "eR"], writes=[("kte", tt)])
                psT, pkT = acc_next()
                if own:
                    def rtf(e, psT=psT, b=b):
                        e.matmul(psT[:, 0:130], lhsT=lf[b][:, 0:128], rhs=TI[:, 0:130], start=True, stop=True)
                        return e.matmul(psT[:, 256:386], lhsT=lf[b][:, 128:256], rhs=TI[:, 0:130], start=True, stop=True)
                    S.op("pe", rtf, reads=[("lf", b), "TI"], writes=[pkT])
                    for hh in range(2):
                        S.op("act", (lambda e, psT=psT, hh=hh, tt=tt: e.activation(out=enRT[:, hh, tt * 128:(tt + 1) * 128], in_=psT[:, hh * 256:hh * 256 + 128], func=AF.Exp, scale=-1.0)),
                             reads=[pkT], writes=[("enRT", hh, tt)])
                        S.op("act", (lambda e, psT=psT, hh=hh, tt=tt: e.activation(out=dec[:, hh, tt * 2:tt * 2 + 2], in_=psT[:, hh * 256 + 128:hh * 256 + 130], func=AF.Exp)),
                             reads=[pkT], writes=[("dec", hh, tt)])
                else:
                    def rtf(e, psT=psT, b=b):
                        e.matmul(psT[:, 0:2], lhsT=lf[b][:, 0:128], rhs=TI[:, 128:130], start=True, stop=True)
                        return e.matmul(psT[:, 256:258], lhsT=lf[b][:, 128:256], rhs=TI[:, 128:130], start=True, stop=True)
                    S.op("pe", rtf, reads=[("lf", b), "TI"], writes=[pkT])
                    for hh in range(2):
                        S.op("act", (lambda e, psT=psT, hh=hh, tt=tt: e.activation(out=dec[:, hh, tt * 2:tt * 2 + 2], in_=psT[:, hh * 256:hh * 256 + 2], func=AF.Exp)),
                             reads=[pkT], writes=[("dec", hh, tt)])

            for tt in range(NT):
                ps, pk = acc_next()
                proj_tm(sf, tt, ps, pk)
                b = tt % 2
                S.op("act", (lambda e, ps=ps, b=b: e.activation(out=sg[b], in_=ps[:, 0:256], func=AF.Exp, scale=-1.0)),
                     reads=[pk], writes=[("sg", b)])
                S.op("act", (lambda e, b=b: e.activation(out=sg[b], in_=sg[b], func=AF.Ln, bias=one1[:, 0:1], scale=1.0)),
                     reads=[("sg", b), "eps"], writes=[("sg", b)])
                S.op("act", (lambda e, b=b: e.activation(out=sg[b], in_=sg[b], func=AF.Exp, scale=-1.0)),
                     reads=[("sg", b)], writes=[("sg", b)])
                S.op("dve", (lambda e, b=b: e.tensor_tensor(out=t1, in0=sg[b], in1=omlp, op=ALU.mult)),
                     reads=[("sg", b), "omlp"], writes=["t1"])
                S.op("dve", (lambda e, b=b: e.tensor_tensor(out=sg[b], in0=t1, in1=lbp, op=ALU.add)),
                     reads=["t1", "lbB"], writes=[("sg", b)])
                S.op("dve", (lambda e, b=b: e.tensor_tensor(out=ktok[b], in0=omlp, in1=t1, op=ALU.subtract)),
                     reads=["t1", "omlp"], writes=[("ktok", b)])
                S.op("act", (lambda e, b=b: e.activation(out=lf[b], in_=sg[b], func=AF.Ln)),
                     reads=[("sg", b)], writes=[("lf", b)])
                pump(2 if tt < 4 else 1)
                if tt >= 1:
                    rstuff(tt - 1)
            pump(99)
            rstuff(NT - 1)
            if HST < 1:
                return

            def scan_group(tp):
                for hh in range(2):
                    h = hp * 2 + hh
                    b, bk = sbank_next()
                    pv = pb[b][:, 0:512].rearrange("p (t j v) -> p t j v", t=2, j=2)

                    def csf(e, pv=pv, tp=tp, hh=hh):
                        ins = None
                        for t2 in range(2):
                            tt = tp * 2 + t2
                            ins = e.matmul(pv[:, t2, :, :], lhsT=kte_tok[:, tt, hh * 128:(hh + 1) * 128],
                                           rhs=vm[:, tt, :, hh * 128:(hh + 1) * 128], start=True, stop=True)
                        return ins
                    S.op("pe", csf, reads=[("kte", tp * 2), ("kte", tp * 2 + 1)] + [("vm", tp * 2 + t2, j) for t2 in range(2) for j in range(2)], writes=[bk])
                    for t2 in range(2):
                        for j in range(2):
                            tt = tp * 2 + t2
                            n = tt * 2 + j
                            psc = pv[:, t2, j, :]
                            if own:
                                S.op("dve", (lambda e, h=h, hh=hh, n=n: e.tensor_scalar(out=Dbf[:, hh, n, :], in0=Smid[:, h, :], scalar1=dec[:, hh, n:n + 1], scalar2=None, op0=ALU.mult)),
                                     reads=[("Smid", h), ("dec", hh, tt)], writes=[("Dbf", hh, n)])
                            S.op("dve", (lambda e, h=h, hh=hh, n=n, psc=psc: e.scalar_tensor_tensor(out=Smid[:, h, :], in0=Smid[:, h, :], scalar=dec[:, hh, n:n + 1], in1=psc, op0=ALU.mult, op1=ALU.add)),
                                 reads=[("Smid", h), "Smid", ("dec", hh, tt), bk], writes=[("Smid", h)])

            def kte_transposes():
                if not own:
                    return
                for hh in range(2):
                    b, bk = sbank_next()
                    pvb = pb[b][:, :].bitcast(BF16).rearrange("p (a c) -> p a c", c=128)

                    def ktf(e, pvb=pvb, hh=hh):
                        ins = None
                        for tt in range(NT):
                            ins = e.transpose(out=pvb[:, tt, :], in_=kte_tok[:, tt, hh * 128:(hh + 1) * 128], identity=identb[:])
                        return ins
                    S.op("pe", ktf, reads=[("kte", tt) for tt in range(NT)] + ["identb"], writes=[bk])
                    S.op("act", (lambda e, pvb=pvb, hh=hh: e.activation(out=kteT[:, hh, :].rearrange("p (a c) -> p a c", c=128), in_=pvb[:, 0:8, :], func=AF.Copy)),
                         reads=[bk], writes=[("kteT", hh)])
            si = next_tile()
            for tt in range(NT):
                ps, pk = acc_next()
                proj_tm(si, tt, ps, pk)
                S.op("act", (lambda e, ps=ps, tt=tt: e.activation(out=v_tok[:, tt, :], in_=ps[:, 0:256], func=AF.Copy)),
                     reads=[pk], writes=[("vtok", tt)])
                for j in range(2):
                    S.op("act", (lambda e, ps=ps, tt=tt, j=j: e.activation(out=vm[j * 64:(j + 1) * 64, tt, j, :], in_=ps[j * 64:(j + 1) * 64, 0:256], func=AF.Copy)),
                         reads=[pk, "vm0"], writes=[("vm", tt, j)])
                if tt == 1:
                    kte_transposes()
                if tt >= 2 and tt % 2 == 0:
                    scan_group(tt // 2 - 1)
            if not own:
                scan_group(3)
                return
            if HST < 4:
                return
            sq = next_tile()
            qg = 0
            for hh in range(2):
                for half in range(2):
                    ps, pk = acc_next()
                    proj_fm(sq, hh * 128, half, ps, pk)
                    S.op("dve", (lambda e, ps=ps, hh=hh, half=half: e.tensor_tensor(out=qeT[:, hh, half * 512:(half + 1) * 512], in0=ps[:, 0:512], in1=enRT[:, hh, half * 512:(half + 1) * 512], op=ALU.mult)),
                         reads=[pk] + [("enRT", hh, t_) for t_ in range(half * 4, half * 4 + 4)], writes=[("qeT", hh, half)])
                    if qg == 0:
                        scan_group(3)
                    qg += 1
            sgw = next_tile()
            for hh in range(2):
                for half in range(2):
                    ps, pk = acc_next()
                    proj_fm(sgw, hh * 128, half, ps, pk)
                    S.op("act", (lambda e, ps=ps, hh=hh, half=half: e.activation(out=sgT[:, hh, half * 512:(half + 1) * 512], in_=ps[:, 0:512], func=AF.Silu)),
                         reads=[pk], writes=[("sgT", hh, half)])
            gi = 0
            links = []
            for hh in range(2):
                h = hp * 2 + hh
                ob = hh % 2
                for half in range(2):
                    g = gi % 2
                    gi += 1
                    bs = 6 + g
                    bo = 4 + g
                    tts = list(range(half * 4, half * 4 + 4))
                    pS = pb[bs][:, :].rearrange("p (a c) -> p a c", c=128)
                    pO = pb[bo][:, :].rearrange("p (a c) -> p a c", c=128)

                    def link1(pS=pS, hh=hh, tts=tts, g=g, bs=bs, half=half):
                        def scf(e):
                            ins = None
                            for i, tt in enumerate(tts):
                                ins = e.matmul(pS[:, i, :], lhsT=kteT[:, hh, tt * 128:(tt + 1) * 128], rhs=qeT[:, hh, tt * 128:(tt + 1) * 128], start=True, stop=True)
                            return ins
                        S.op("pe", scf, reads=[("kteT", hh), ("qeT", hh, half)], writes=[("ps", bs)])
                        S.op("dve", (lambda e: e.tensor_tensor(out=scTm[g], in0=pS[:, 0:4, :], in1=maskA[:].unsqueeze(1).to_broadcast([128, 4, 128]), op=ALU.mult)),
                             reads=[("ps", bs), "maskA"], writes=[("scTm", g)])

                    def link2(pO=pO, hh=hh, tts=tts, g=g, bo=bo, half=half):
                        def omm(e):
                            ins = None
                            for i, tt in enumerate(tts):
                                e.matmul(pO[:, i, :], lhsT=scTm[g][:, i, :], rhs=v_tok[:, tt, hh * 128:(hh + 1) * 128], start=True, stop=False)
                                e.matmul(pO[0:64, i, :], lhsT=qeT[:, hh, tt * 128:tt * 128 + 64], rhs=Dbf[:, hh, 2 * tt, :], start=False, stop=True)
                                ins = e.matmul(pO[64:128, i, :], lhsT=qeT[:, hh, tt * 128 + 64:tt * 128 + 128], rhs=Dbf[:, hh, 2 * tt + 1, :], start=False, stop=True)
                            return ins
                        S.op("pe", omm, reads=[("scTm", g), ("qeT", hh, half)] + [("vtok", tt) for tt in tts] + [("Dbf", hh, n) for n in range(half * 8, half * 8 + 8)], writes=[("ps", bo)])
                        c0 = g * 16
                        for i in range(4):
                            S.op("act", (lambda e, i=i: e.activation(out=on_tok[g][:, i, :], in_=pO[:, i, :], func=AF.Square, accum_out=hst[:, c0 + i:c0 + i + 1])),
                                 reads=[("ps", bo)], writes=[("on", g), ("hs0", g, i)])
                        S.op("act", (lambda e: e.activation(out=hst[:, c0 + 4:c0 + 8], in_=hst[:, c0:c0 + 4], func=AF.Ln, bias=eps6[:, 0:1], scale=1.0 / 128)),
                             reads=[("hs0", g, i) for i in range(4)] + ["eps"], writes=[("hs1", g)])
                        S.op("act", (lambda e: e.activation(out=hst[:, c0 + 8:c0 + 12], in_=hst[:, c0 + 4:c0 + 8], func=AF.Exp, scale=-0.5)),
                             reads=[("hs1", g)], writes=[("hs2", g)])
                        for i in range(4):
                            S.op("dve", (lambda e, i=i: e.scalar_tensor_tensor(out=on_tok[g][:, i, :], in0=pO[:, i, :], scalar=hst[:, c0 + 8 + i:c0 + 9 + i], in1=hnwB[:], op0=ALU.mult, op1=ALU.mult)),
                                 reads=[("ps", bo), ("hs2", g), ("on", g), "hnwB"], writes=[("on", g)])

                    def link3(pS=pS, hh=hh, g=g, bs=bs, half=half, ob=ob, h=h):
                        def otf(e):
                            ins = None
                            for i in range(4):
                                ins = e.transpose(out=pS[:, i, :], in_=on_tok[g][:, i, :], identity=identf[:])
                            return ins
                        S.op("pe", otf, reads=[("on", g), "identf"], writes=[("ps", bs)])
                        S.op("dve", (lambda e: e.tensor_tensor(out=oTh[ob][:, half * 512:(half + 1) * 512], in0=pS[:, 0:4, :].rearrange("p a c -> p (a c)"), in1=sgT[:, hh, half * 512:(half + 1) * 512], op=ALU.mult)),
                             reads=[("ps", bs), ("sgT", hh, half)], writes=[("oTh", ob)])
                        if half == 1:
                            S.op("sp", (lambda e: e.dma_start(out=oT_scr[h], in_=oTh[ob])), reads=[("oTh", ob)], writes=[("oTs", h)], dma="d_oTh%d" % ob)
                    links.append((link1, link2, link3))
            order = [(0, 0), (0, 1), (1, 0), (1, 1), (0, 2), (2, 0), (1, 2), (2, 1), (0, 3), (3, 0), (1, 3), (2, 2), (3, 1), (3, 2)]
            L = links
            seq = [L[0][0], L[0][1], L[1][0], L[1][1], L[0][2], L[2][0], L[2][1], L[1][2], L[3][0], L[3][1], L[2][2], L[3][2]]
            deferred.extend(seq)


        def attn_prev(h):
            a1 = Arena(big, A1_LO, A1_HI)
            kTp = a1.alloc([128, 2, 1024], BF16)
            Vp = a1.alloc([128, 8, 256], BF16)
            skw = next_tile()
            for m in range(2):
                for half in range(2):
                    ps, pk = acc_next()
                    proj_fm(skw, m * 128, half, ps, pk)
                    S.op("act", (lambda e, ps=ps, m=m, half=half: e.activation(out=kTp[:, m, half * 512:(half + 1) * 512], in_=ps[:, 0:512], func=AF.Copy)),
                         reads=[pk], writes=[("kTp", m, half)])
            S.op("sp", (lambda e: e.dma_start(out=kprev[h], in_=kTp.rearrange("p a b -> p (a b)"))),
                 reads=[("kTp", m, half) for m in range(2) for half in range(2)], writes=[("kprev", h)], dma="d_kTp")
            svw = next_tile()
            for tt in range(NT):
                ps, pk = acc_next()
                proj_tm(svw, tt, ps, pk)
                S.op("dve", (lambda e, ps=ps, tt=tt: e.tensor_copy(out=Vp[:, tt, :], in_=ps[:, 0:256])),
                     reads=[pk], writes=[("Vp", tt)])
            S.op("sp", (lambda e: e.dma_start(out=vprev[h], in_=Vp.rearrange("p a b -> p (a b)"))),
                 reads=[("Vp", tt) for tt in range(NT)], writes=[("vprev", h)], dma="d_Vp")

        def attn_own(h):
            a1 = Arena(big, A1_LO, A1_HI)
            a2 = Arena(wk2, 0, WK2N)
            kT = a1.alloc([128, 2, 2048], BF16)
            V = a1.alloc([128, 16, 264], BF16)
            qT = a1.alloc([128, 2, 1024], BF16)
            sgT = a1.alloc([128, 2, 1024], BF16)
            PT = [a1.alloc([128, 128], BF16) for _ in range(16)]
            ob_ = [a2.alloc([128, 256], F32) for _ in range(2)]
            tb_ = [a2.alloc([128, 256], F32) for _ in range(2)]
            sqj = a2.alloc([128, 256], BF16)
            oTh = [a2.alloc([128, 1024], BF16) for _ in range(2)]
            bst = a2.alloc([128, 64], F32)
            S.op("pool", lambda e: e.memset(V[:, :, 256:257], 1.0), writes=["Vones"])
            S.op("sp", (lambda e: e.dma_start(out=kT[:, :, 0:1024], in_=kprev[h].rearrange("p (a b) -> p a b", a=2))),
                 reads=[("kprev", h)], writes=["kTprev"], dma="d_kTl")
            S.op("sp", (lambda e: e.dma_start(out=V[:, 0:8, 0:256], in_=vprev[h].rearrange("p (a b) -> p a b", a=8))),
                 reads=[("vprev", h)], writes=["Vprev"], dma="d_Vl")
            skw = next_tile()
            for m in range(2):
                for half in range(2):
                    ps, pk = acc_next()
                    proj_fm(skw, m * 128, half, ps, pk)
                    S.op("act", (lambda e, ps=ps, m=m, half=half: e.activation(out=kT[:, m, 1024 + half * 512:1024 + (half + 1) * 512], in_=ps[:, 0:512], func=AF.Copy)),
                         reads=[pk], writes=[("kT", m, half)])
                    pump(1)
            svw = next_tile()
            for tt in range(NT):
                ps, pk = acc_next()
                proj_tm(svw, tt, ps, pk)
                S.op("dve", (lambda e, ps=ps, tt=tt: e.tensor_copy(out=V[:, 8 + tt, 0:256], in_=ps[:, 0:256])),
                     reads=[pk], writes=[("V", tt)])
            sqw = next_tile()
            for m in range(2):
                for half in range(2):
                    ps, pk = acc_next()
                    proj_fm(sqw, m * 128, half, ps, pk)
                    S.op("act", (lambda e, ps=ps, m=m, half=half: e.activation(out=qT[:, m, half * 512:(half + 1) * 512], in_=ps[:, 0:512], func=AF.Copy)),
                         reads=[pk], writes=[("qT", m, half)])
            sgw = next_tile()
            for j in range(2):
                for half in range(2):
                    ps, pk = acc_next()
                    proj_fm(sgw, j * 128, half, ps, pk)
                    S.op("act", (lambda e, ps=ps, j=j, half=half: e.activation(out=sgT[:, j, half * 512:(half + 1) * 512], in_=ps[:, 0:512], func=AF.Silu)),
                         reads=[pk], writes=[("sgT", j, half)])
            ptc = [0]
            for qb in range(NT):
                qhalf = qb // 4
                nkb = 9 + qb

                def kkeys(m, kbs):
                    ks = set()
                    for kb in kbs:
                        ks.add("kTprev" if kb < 8 else ("kT", m, (kb - 8) // 4))
                    return list(ks)

                def vkeys(kbs):
                    ks = {"Vones"}
                    for kb in kbs:
                        ks.add("Vprev" if kb < 8 else ("V", kb - 8))
                    return list(ks)

                groups = []
                for g0 in range(0, nkb, 4):
                    for m in range(2):
                        groups.append((m, list(range(g0, min(g0 + 4, nkb)))))

                oa = (4, 5) if qb % 2 == 0 else (2, 3)

                def issue_s(m, kbs, qb=qb, nkb=nkb):
                    b = (6, 7, 0, 1)[cnt["s4"] % 4]
                    cnt["s4"] += 1
                    bk = ("ps", b)
                    pS = pb[b][:, :].rearrange("p (a c) -> p a c", c=128)

                    def fn(e, pS=pS, m=m, kbs=kbs):
                        ins = None
                        for i, kb in enumerate(kbs):
                            diag = (kb == nkb - 1)
                            ins = e.matmul(pS[:, i, :], lhsT=kT[:, m, kb * 128:(kb + 1) * 128], rhs=qT[:, m, qb * 128:(qb + 1) * 128], start=True, stop=not diag)
                            if diag:
                                ins = e.matmul(pS[:, i, :], lhsT=identb[:], rhs=cmask[:], start=False, stop=True)
                        return ins
                    S.op("pe", fn, reads=kkeys(m, kbs) + [("qT", m, qhalf), "identb", "cmask"], writes=[bk])
                    slots = []
                    for i, kb in enumerate(kbs):
                        pi = ptc[0] % 16
                        ptc[0] += 1
                        S.op("act", (lambda e, pS=pS, i=i, pi=pi, kb=kb: e.activation(out=PT[pi], in_=pS[:, i, :], func=AF.Exp, bias=alibi[:, h, qb, kb:kb + 1], scale=QSCALE)),
                             reads=[bk, "alibi"], writes=[("PT", pi)])
                        slots.append(pi)
                    return slots

                def issue_av(m, kbs, slots, nkb=nkb, oa=oa):
                    po = pb[oa[m]]

                    def fn(e, po=po, kbs=kbs, slots=slots):
                        ins = None
                        for kb, pi in zip(kbs, slots):
                            ins = e.matmul(po[:, 0:257], lhsT=PT[pi], rhs=V[:, kb, 0:257], start=(kb == 0), stop=(kb == nkb - 1))
                        return ins
                    S.op("pe", fn, reads=[("PT", pi) for pi in slots] + vkeys(kbs), writes=[("ps", oa[m])])

                pend = []
                for gi_, (m, kbs) in enumerate(groups):
                    slots = issue_s(m, kbs)
                    pend.append((m, kbs, slots))
                    if gi_ == 1:
                        pump(1)
                    if len(pend) > 2:
                        issue_av(*pend.pop(0))
                while pend:
                    issue_av(*pend.pop(0))

                ib = qb % 2
                o0, o1 = oa
                S.op("dve", (lambda e, o0=o0: e.reciprocal(out=bst[:, 0:1], in_=pb[o0][:, 256:257])), reads=[("ps", o0)], writes=["r0"])
                S.op("dve", (lambda e, o1=o1: e.reciprocal(out=bst[:, 1:2], in_=pb[o1][:, 256:257])), reads=[("ps", o1)], writes=["r1"])
                S.op("dve", (lambda e: e.tensor_tensor(out=bst[:, 2:3], in0=bst[:, 1:2], in1=neglam, op=ALU.mult)), reads=["r1", "neglam"], writes=["r1l"])
                S.op("dve", (lambda e, ib=ib, o1=o1: e.tensor_scalar(out=tb_[ib], in0=pb[o1][:, 0:256], scalar1=bst[:, 2:3], scalar2=None, op0=ALU.mult)),
                     reads=[("ps", o1), "r1l"], writes=[("tb", ib)])
                S.op("dve", (lambda e, ib=ib, o0=o0: e.scalar_tensor_tensor(out=ob_[ib], in0=pb[o0][:, 0:256], scalar=bst[:, 0:1], in1=tb_[ib], op0=ALU.mult, op1=ALU.add)),
                     reads=[("ps", o0), "r0", ("tb", ib)], writes=[("ob", ib)])
                c0 = 8 + ib * 4
                S.op("act", (lambda e, ib=ib, c0=c0: e.activation(out=sqj, in_=ob_[ib], func=AF.Square, accum_out=bst[:, c0:c0 + 1])),
                     reads=[("ob", ib)], writes=["sqj", ("bs0", ib)])
                S.op("act", (lambda e, c0=c0: e.activation(out=bst[:, c0 + 1:c0 + 2], in_=bst[:, c0:c0 + 1], func=AF.Ln, bias=eps5[:, 0:1], scale=1.0 / 256)),
                     reads=[("bs0", ib), "eps"], writes=[("bs1", ib)])
                S.op("act", (lambda e, c0=c0: e.activation(out=bst[:, c0 + 2:c0 + 3], in_=bst[:, c0 + 1:c0 + 2], func=AF.Exp, scale=-0.5)),
                     reads=[("bs1", ib)], writes=[("bs2", ib)])
                S.op("dve", (lambda e, ib=ib, c0=c0: e.scalar_tensor_tensor(out=ob_[ib], in0=ob_[ib], scalar=bst[:, c0 + 2:c0 + 3], in1=sublnB[:], op0=ALU.mult, op1=ALU.mult)),
                     reads=[("ob", ib), ("bs2", ib), "sublnB"], writes=[("ob", ib)])

                def fin_pe(ib=ib, qb=qb, qhalf=qhalf):
                    bT = (6, 7, 0, 1)[cnt["s4"] % 4]
                    cnt["s4"] += 1
                    bTk = ("ps", bT)

                    def btf(e):
                        ins = None
                        for j in range(2):
                            ins = e.transpose(out=pb[bT][:, j * 128:(j + 1) * 128], in_=ob_[ib][:, j * 128:(j + 1) * 128], identity=identf[:])
                        return ins
                    S.op("pe", btf, reads=[("ob", ib), "identf"], writes=[bTk])
                    for j in range(2):
                        S.op("dve", (lambda e, j=j: e.tensor_tensor(out=oTh[j][:, qb * 128:(qb + 1) * 128], in0=pb[bT][:, j * 128:(j + 1) * 128], in1=sgT[:, j, qb * 128:(qb + 1) * 128], op=ALU.mult)),
                             reads=[bTk, ("sgT", j, qhalf)], writes=[("oThB", j)])
                    if qb == NT - 1:
                        for j in range(2):
                            S.op("sp", (lambda e, j=j: e.dma_start(out=oT_scr[16 + 2 * h + j], in_=oTh[j])), reads=[("oThB", j)], writes=[("oTs", 16 + 2 * h + j)], dma="d_oThB%d" % j)
                deferred.append(fin_pe)

        def gates():
            a2 = Arena(wk2, 0, WK2N)
            sigt = [a2.alloc([128, 1024], BF16) for _ in range(4)]
            gc = 0
            for gt in range(NGT):
                sw = next_tile()
                for j in range(2):
                    b = gc % 4
                    gc += 1
                    for half in range(2):
                        ps, pk = acc_next()
                        proj_fm(sw, j * 128, half, ps, pk)
                        S.op("act", (lambda e, ps=ps, b=b, half=half: e.activation(out=sigt[b][:, half * 512:(half + 1) * 512], in_=ps[:, 0:512], func=AF.Sigmoid)),
                             reads=[pk], writes=[("sigt", b, half)])
                    S.op("act", (lambda e, b=b, gt=gt, j=j: e.dma_start(out=sig_scr[gt * 2 + j], in_=sigt[b])),
                         reads=[("sigt", b, 0), ("sigt", b, 1)], writes=[("sigs", gt * 2 + j)], dma="d_sig%d" % b)

        def phase_C():
            oT = uT
            for q4 in range(4):
                S.op("sp", (lambda e, q4=q4: e.dma_start(out=oT[:, q4 * 8:(q4 + 1) * 8, :], in_=oT_scr[q4 * 8:(q4 + 1) * 8].rearrange("c p t -> p c t"))),
                     reads=[("oTs", c) for c in range(q4 * 8, q4 * 8 + 8)], writes=[("oT", q4)], dma="d_oTl%d" % q4)
            a2 = Arena(wk2, 0, WK2N)
            sA = [a2.alloc([128, 1024], BF16) for _ in range(4)]
            sB = [a2.alloc([128, 1024], BF16) for _ in range(4)]

            def sig_load(ft):
                for j in range(2):
                    fc = ft * 2 + j
                    b = fc % 4
                    S.op("act", (lambda e, b=b, fc=fc: e.dma_start(out=sA[b], in_=sig_scr[fc])), reads=[("sigs", fc)], writes=[("sA", b)], dma="d_sA%d" % b)
                    S.op("act", (lambda e, b=b, fc=fc: e.dma_start(out=sB[b], in_=sig_scr[32 + fc])), reads=[("sigs", 32 + fc)], writes=[("sB", b)], dma="d_sB%d" % b)
            sig_load(0)
            tA = [a2.alloc([128, 512], F32) for _ in range(2)]
            tB = [a2.alloc([128, 512], F32) for _ in range(2)]
            yst = [a2.alloc([128, 1024], BF16) for _ in range(2)]
            fcn = 0
            for ft in range(16):
                sw = next_tile()
                if ft + 1 < 16:
                    sig_load(ft + 1)
                for j in range(2):
                    fc = ft * 2 + j
                    b = fc % 4
                    yb = fc % 2
                    for half in range(2):
                        psA, pkA = acc_next()
                        proj_fm(sw, j * 128, half, psA, pkA, src=oT, srckeys=[("oT", 0), ("oT", 1)], k0=0, nk=16, ksrc0=0)
                        psB, pkB = acc_next()
                        proj_fm(sw, j * 128, half, psB, pkB, src=oT, srckeys=[("oT", 2), ("oT", 3)], k0=16, nk=16, ksrc0=16)
                        tb = half
                        S.op("dve", (lambda e, psA=psA, b=b, half=half, tb=tb: e.tensor_tensor(out=tA[tb], in0=psA[:, 0:512], in1=sA[b][:, half * 512:(half + 1) * 512], op=ALU.mult)),
                             reads=[pkA, ("sA", b)], writes=[("tA", tb)])
                        S.op("dve", (lambda e, psB=psB, b=b, half=half, tb=tb: e.tensor_tensor(out=tB[tb], in0=psB[:, 0:512], in1=sB[b][:, half * 512:(half + 1) * 512], op=ALU.mult)),
                             reads=[pkB, ("sB", b)], writes=[("tB", tb)])
                        S.op("dve", (lambda e, yb=yb, half=half, tb=tb: e.tensor_tensor(out=yst[yb][:, half * 512:(half + 1) * 512], in0=tA[tb], in1=tB[tb], op=ALU.add)),
                             reads=[("tA", tb), ("tB", tb)], writes=[("yst", yb, half)])
                    S.op("act", (lambda e, yb=yb, fc=fc: e.dma_start(out=yT_scr[fc], in_=yst[yb])),
                         reads=[("yst", yb, 0), ("yst", yb, 1)], writes=[("yTs", fc)], dma="d_yst%d" % yb)
            S.barrier()
            yT = uT
            for q4 in range(4):
                S.op("sp", (lambda e, q4=q4: e.dma_start(out=yT[:, q4 * 8:(q4 + 1) * 8, :], in_=yT_scr[q4 * 8:(q4 + 1) * 8].rearrange("c p t -> p c t"))),
                     reads=[("yTs", c) for c in range(q4 * 8, q4 * 8 + 8)], writes=[("yT", q4)], dma="d_yTl%d" % q4)
            a2 = Arena(wk2, 0, WK2N)
            hsb = [a2.alloc([128, 256], F32) for _ in range(4)]
            xr = [a2.alloc([128, 256], F32) for _ in range(4)]
            sqj = a2.alloc([128, 256], BF16)
            ssq = a2.alloc([128, 8, 16], F32)
            fst = a2.alloc([128, 32], F32)
            rc = 0
            for cb in range(16):
                sw = next_tile()
                for tt in range(NT):
                    r = rc % 4
                    rc += 1
                    S.op("act", (lambda e, r=r, tt=tt, cb=cb: e.dma_start(out=xr[r], in_=x_own[tt * 128:(tt + 1) * 128, cb * 256:(cb + 1) * 256])),
                         writes=[("xr", r)], dma="d_xr%d" % r)
                    ps, pk = acc_next()
                    proj_tm(sw, tt, ps, pk, src=yT, srckeys=[("yT", q4) for q4 in range(4)])
                    S.op("dve", (lambda e, ps=ps, r=r: e.tensor_tensor(out=hsb[r], in0=ps[:, 0:256], in1=xr[r], op=ALU.add)),
                         reads=[pk, ("xr", r)], writes=[("hsb", r)])
                    S.op("act", (lambda e, r=r, tt=tt, cb=cb: e.activation(out=sqj, in_=hsb[r], func=AF.Square, accum_out=ssq[:, tt, cb:cb + 1])),
                         reads=[("hsb", r)], writes=["sqjC", ("ssq", tt, cb)])
                    S.op("act", (lambda e, r=r, tt=tt, cb=cb: e.dma_start(out=out[tt * 128:(tt + 1) * 128, cb * 256:(cb + 1) * 256], in_=hsb[r])),
                         reads=[("hsb", r)], writes=[("outh", tt, cb)], dma="d_hs%d" % r)
            S.barrier()
            arF = Arena(big, 0, R2)
            fwB = arF.alloc([128, D], F32)
            hrow = [arF.alloc([128, D], F32) for _ in range(2)]
            S.op("sp", (lambda e: e.dma_start(out=fwB, in_=fwB_d[:, :])), writes=["fwB"], dma="d_fwB")
            for tt in range(NT):
                b = tt % 2
                S.op("sp", (lambda e, b=b, tt=tt: e.dma_start(out=hrow[b], in_=out[tt * 128:(tt + 1) * 128, :])),
                     reads=[("outh", tt, cb) for cb in range(16)], writes=[("hrow", b)], dma="d_hrl%d" % b)
                S.op("dve", (lambda e, tt=tt: e.reduce_sum(out=fst[:, tt * 3:tt * 3 + 1], in_=ssq[:, tt, :], axis=AX.X)),
                     reads=[("ssq", tt, cb) for cb in range(16)], writes=[("f0", tt)])
                S.op("act", (lambda e, tt=tt: e.activation(out=fst[:, tt * 3 + 1:tt * 3 + 2], in_=fst[:, tt * 3:tt * 3 + 1], func=AF.Sqrt, bias=1e-6, scale=1.0 / D)),
                     reads=[("f0", tt)], writes=[("f1", tt)])
                S.op("dve", (lambda e, tt=tt: e.reciprocal(out=fst[:, tt * 3 + 2:tt * 3 + 3], in_=fst[:, tt * 3 + 1:tt * 3 + 2])),
                     reads=[("f1", tt)], writes=[("f2", tt)])
                S.op("act", (lambda e, b=b, tt=tt: e.activation(out=hrow[b], in_=hrow[b], func=AF.Copy, scale=fst[:, tt * 3 + 2:tt * 3 + 3])),
                     reads=[("hrow", b), ("f2", tt)], writes=[("hrowB", b)])
                S.op("dve", (lambda e, b=b, tt=tt: e.tensor_tensor(out=hrow[b], in0=hrow[b], in1=fwB, op=ALU.mult)),
                     reads=[("hrowB", b), "fwB"], writes=[("hrowA", b)])
                S.op("sp", (lambda e, b=b, tt=tt: e.dma_start(out=out[tt * 128:(tt + 1) * 128, :], in_=hrow[b])),
                     reads=[("hrowA", b), ("hrowB", b)], writes=[("outf", tt), ("hrow", b)], dma="d_hrs%d" % b)
            S.op("sp", None, reads=[("outf", tt) for tt in range(NT)])

        for hp in range(NPAIR):
            plan.append(win_tile(C_AF + hp * 256))
            plan.append(win_tile(C_AI + hp * 256))
        for h in range(NBH):
            plan.append(win_tile(C_BK + h * 256))
            plan.append(win_tile(C_BV + h * 256))
        for hp in range(NPAIR):
            for c in (C_AF, C_AI, C_AQ, C_AG):
                plan.append(win_tile(c + hp * 256))
        for h in range(NBH):
            for c in (C_BK, C_BV, C_BQ, C_BG):
                plan.append(win_tile(c + h * 256))
        wide0[0] = len(plan)
        for gt in range(NGT):
            plan.append(win_tile(C_GA + gt * 256))
        if DO_C:
            for ft in range(16):
                plan.append([w_a[0:1024, ft * 256:(ft + 1) * 256], w_a[1024:2048, ft * 256:(ft + 1) * 256],
                             w_b[0:1024, ft * 256:(ft + 1) * 256], w_b[1024:2048, ft * 256:(ft + 1) * 256]])
            for cb in range(16):
                plan.append([w_out[j * 1024:(j + 1) * 1024, cb * 256:(cb + 1) * 256] for j in range(4)])

        phase_A(x_prev, "p")
        for hp in range(NPAIR):
            hgrn_pair(hp, own=False)
        S.barrier()
        for h in range(NBH):
            attn_prev(h)
        S.barrier()
        phase_A(x_own, "o")
        for hp in range(NPAIR):
            hgrn_pair(hp, own=True)
        pump(999)
        S.barrier()
        for h in range(NBH):
            attn_own(h)
        pump(999)
        S.barrier()
        gates()
        S.barrier()
        if DO_C:
            phase_C()
        else:
            S.op("sp", None, reads=[("oTs", c) for c in range(32)] + [("sigs", c) for c in range(64)])
        assert HST < 99 or cur[0] == len(plan), (cur[0], len(plan))
        S.emit(nc, st)
    return nc


def host_consts(half):
    p = np.arange(128)
    identf = np.eye(128, dtype=np.float32)
    TI = np.zeros((128, 130), np.float32)
    s = p[:, None]
    t = p[None, :]
    TI[:, :128] = ((s // 64 == t // 64) & (s > t)).astype(np.float32)
    TI[:, 128] = (p < 64)
    TI[:, 129] = (p >= 64)
    maskA = ((s // 64 == t // 64) & (s <= t)).astype(np.float32)
    cmask = np.where(s <= t, 0.0, -30000.0).astype(np.float32)
    slopes = np.exp2(-(np.arange(8, dtype=np.float64) + 1.0))
    al = np.zeros((128, 8, 8, 16), np.float32)
    for qb in range(8):
        gq = half * 8 + qb
        for kb in range(16):
            gk = kb if half == 1 else kb - 8
            if gk < 0:
                al[:, :, qb, kb] = -30000.0
            else:
                kpos = gk * 128 + p
                cref = gq * 128 + 64
                al[:, :, qb, kb] = (slopes[None, :] * (kpos[:, None] - cref)).astype(np.float32)
    return dict(identf=identf, TI=TI, maskA=maskA, cmask=cmask, alibi=al.reshape(128, 1024))


def make_in_maps(x, norm_w, w_in, lower_bound_table, hgrn_norm_w, lambda_q1, lambda_k1,
                 lambda_q2, lambda_k2, subln_w, w_branch_a, w_branch_b, w_out, final_w, cores=range(NCORES)):
    f = lambda a: np.ascontiguousarray(np.asarray(a, dtype=np.float32))
    x = f(x)
    shared = dict(
        w_in=f(w_in[0]), w_a=f(w_branch_a[0]), w_b=f(w_branch_b[0]), w_out=f(w_out[0]),
        normwT=f(np.asarray(norm_w[0]).reshape(32, 128).T),
        fwB=f(np.broadcast_to(np.asarray(final_w)[None, :], (128, D))),
        lbt=f(np.broadcast_to(np.asarray(lower_bound_table).reshape(1, 4096), (128, 4096))),
        hnwB=f(np.broadcast_to(np.asarray(hgrn_norm_w[0])[None, :], (128, 128))),
        sublnB=f(np.broadcast_to(np.asarray(subln_w[0])[None, :], (128, 256))),
        lamv=f(np.broadcast_to(np.concatenate([np.asarray(lambda_q1[0]), np.asarray(lambda_k1[0]),
                                               np.asarray(lambda_q2[0]), np.asarray(lambda_k2[0])])[None, :], (128, 512))),
    )
    hc = [host_consts(0), host_consts(1)]
    zeros = np.zeros((T, D), np.float32)
    maps = []
    for c in cores:
        b, half = c // 2, c % 2
        m = dict(shared)
        m.update(hc[half])
        m["x_own"] = f(x[b, half * T:(half + 1) * T])
        m["x_prev"] = f(x[b, 0:T]) if half == 1 else zeros
        maps.append(m)
    return maps


_NC_CACHE = {}


def kernel(**inputs):
    if "nc" not in _NC_CACHE:
        _NC_CACHE["nc"] = build_program()
    nc = _NC_CACHE["nc"]
    maps = make_in_maps(**inputs)
    res = run_bass_kernel_spmd(nc, maps, core_ids=list(range(NCORES)))
    outp = np.zeros((4, 2048, D), np.float32)
    for c in range(NCORES):
        b, half = c // 2, c % 2
        outp[b, half * T:(half + 1) * T] = np.asarray(res.results[c]["out"], dtype=np.float32)
    return outp
```

```python
import numpy as np
from contextlib import ExitStack
import concourse.bass as bass
import concourse.mybir as mybir
from concourse.bass_utils import run_bass_kernel_spmd

F32 = mybir.dt.float32
BF16 = mybir.dt.bfloat16
AF = mybir.ActivationFunctionType
ALU = mybir.AluOpType
AX = mybir.AxisListType

NCORES = 8
T = 1024
NT = 8
D = 4096
KC = 32
NIN = 24576
QSCALE = 128 ** -0.5
WK2N = 14336
C_AQ, C_AF, C_AI, C_AG, C_BQ, C_BK, C_BV, C_BG, C_GA, C_GB = 0, 2048, 4096, 6144, 8192, 10240, 12288, 14336, 16384, 20480


class Sched:
    ENG = ("pe", "act", "dve", "pool", "sp")

    def __init__(self):
        self.ops = []
        self.last_writer = {}
        self.readers = {}
        self.dom_pos = {}
        self.pending = {e: set() for e in self.ENG}
        self.last_in_dom = {}

    def op(self, eng, fn, reads=(), writes=(), dma=None, drain=False):
        oid = len(self.ops)
        raw = set()
        oth = set()
        force = set()
        if drain and eng in self.last_in_dom:
            force.add(self.last_in_dom[eng])
        for k in reads:
            w = self.last_writer.get(k)
            if w is not None:
                raw.add(w)
        for k in writes:
            w = self.last_writer.get(k)
            if w is not None:
                oth.add(w)
            for r in self.readers.get(k, ()):
                oth.add(r)
        raw |= self.pending[eng]
        self.pending[eng] = set()
        for k in reads:
            self.readers.setdefault(k, []).append(oid)
        for k in writes:
            self.last_writer[k] = oid
            self.readers[k] = []
        dom = dma if dma else eng
        pos = self.dom_pos.get(dom, 0)
        self.dom_pos[dom] = pos + 1
        self.ops.append(dict(id=oid, eng=eng, fn=fn, dom=dom, pos=pos, raw=raw, oth=oth,
                             signal=False, waits=[], isdma=dma is not None, force=force))
        self.last_in_dom[dom] = oid
        return oid

    def barrier(self):
        s = set(self.last_in_dom.values())
        for e in self.ENG:
            self.pending[e] |= s

    def finalize(self):
        waited = {}
        by_dom = {}
        for o in self.ops:
            by_dom.setdefault(o["dom"], []).append(o)
        for o in self.ops:
            need = {}
            F = o["eng"]
            for d in o["raw"] | o["oth"] | o["force"]:
                dd = self.ops[d]
                E = dd["dom"]
                if (not dd["isdma"]) and E == F and d not in o["force"]:
                    if F in ("pe", "sp"):
                        continue
                if waited.get((F, E), -1) >= dd["pos"]:
                    continue
                need[E] = max(need.get(E, -1), dd["pos"])
            for E, p in need.items():
                by_dom[E][p]["signal"] = True
                waited[(F, E)] = p
                o["waits"].append((E, p))
        for dom, lst in by_dom.items():
            c = 0
            for o in lst:
                if o["signal"]:
                    c += 1
                    o["val"] = c * (16 if o["isdma"] else 1)
        for o in self.ops:
            o["waitvals"] = [(E, by_dom[E][p]["val"]) for (E, p) in o["waits"]]
        self.domains = list(by_dom.keys())

    def emit(self, nc, stack):
        self.finalize()
        sems = {}
        for dom in self.domains:
            sems[dom] = stack.enter_context(nc.semaphore("s_" + dom))
        block = stack.enter_context(nc.Block())
        streams = {e: [o for o in self.ops if o["eng"] == e] for e in self.ENG}

        def run(eh, lst):
            for o in lst:
                for (E, v) in o["waitvals"]:
                    eh.wait_ge(sems[E], v)
                ins = o["fn"](eh) if o["fn"] is not None else None
                if o["signal"]:
                    assert ins is not None
                    ins.then_inc(sems[o["dom"]], 16 if o["isdma"] else 1)

        @block.tensor
        def _(e):
            run(e, streams["pe"])

        @block.scalar
        def _(e):
            run(e, streams["act"])

        @block.vector
        def _(e):
            run(e, streams["dve"])

        @block.gpsimd
        def _(e):
            run(e, streams["pool"])

        @block.sync
        def _(e):
            run(e, streams["sp"])


class Arena:
    def __init__(self, flat, lo, hi):
        self.flat, self.lo, self.hi, self.cur = flat, lo, hi, lo

    def alloc(self, shape, dt):
        esz = 4 if dt == F32 else 2
        n = 1
        for s in shape[1:]:
            n *= s
        nel = n * esz // 2
        nel = (nel + 31) // 32 * 32
        assert self.cur + nel <= self.hi, ("arena overflow", shape, self.cur, self.hi)
        v = self.flat[:, self.cur:self.cur + n * esz // 2]
        self.cur += nel
        if dt == F32:
            v = v.bitcast(F32)
        if len(shape) == 3:
            v = v.rearrange("p (a b) -> p a b", b=shape[2])
        elif len(shape) == 4:
            v = v.rearrange("p (a b c) -> p a b c", b=shape[2], c=shape[3])
        return v


def build_program(debug=False, cfg=None):
    cfg = cfg or {}
    NPAIR = cfg.get("npair", 8)
    NBH = cfg.get("nbh", 8)
    NGT = cfg.get("ngt", 32)
    DO_C = cfg.get("do_c", True)
    HST = cfg.get("hstage", 99)
    nc = bass.Bass("TRN2", target_bir_lowering=False)

    def din(name, shape, dt=F32):
        return nc.dram_tensor(name, list(shape), dt, kind="ExternalInput").ap()

    x_own = din("x_own", [T, D])
    x_prev = din("x_prev", [T, D])
    w_in = din("w_in", [D, NIN])
    w_a = din("w_a", [2048, D])
    w_b = din("w_b", [2048, D])
    w_out = din("w_out", [D, D])
    normwT_d = din("normwT", [128, 32])
    fwB_d = din("fwB", [128, D])
    lbt_d = din("lbt", [128, 4096])
    hnwB_d = din("hnwB", [128, 128])
    sublnB_d = din("sublnB", [128, 256])
    lamv_d = din("lamv", [128, 512])
    identf_d = din("identf", [128, 128])
    TI_d = din("TI", [128, 130])
    maskA_d = din("maskA", [128, 128])
    cmask_d = din("cmask", [128, 128])
    alibi_d = din("alibi", [128, 1024])
    out = nc.dram_tensor("out", [T, D], F32, kind="ExternalOutput").ap()
    sk = "ExternalOutput" if debug else "Internal"
    kprev = nc.dram_tensor("kprev_scr", [8, 128, 2048], BF16, kind="Internal").ap()
    vprev = nc.dram_tensor("vprev_scr", [8, 128, 2048], BF16, kind="Internal").ap()
    oT_scr = nc.dram_tensor("oT_scr", [32, 128, 1024], BF16, kind=sk).ap()
    sig_scr = nc.dram_tensor("sig_scr", [64, 128, 1024], BF16, kind=sk).ap()
    yT_scr = nc.dram_tensor("yT_scr", [32, 128, 1024], BF16, kind=sk).ap()

    S = Sched()
    st = ExitStack()
    with st:
        def sb(name, shape, dt):
            return st.enter_context(nc.sbuf_tensor(name, shape, dt))

        big = sb("big", [128, 65536], BF16)
        wk2 = sb("wk2", [128, WK2N], BF16)
        stg = [sb("stg%d" % i, [128, 8, 256], F32) for i in range(3)]
        identf = sb("identf_s", [128, 128], F32)
        identb = sb("identb_s", [128, 128], BF16)
        TI = sb("TI_s", [128, 130], F32)
        maskA = sb("maskA_s", [128, 128], F32)
        cmask = sb("cmask_s", [128, 128], BF16)
        alibi = sb("alibi_s", [128, 8, 8, 16], F32)
        normwT = sb("normwT_s", [128, 32], F32)
        hnwB = sb("hnwB_s", [128, 128], F32)
        sublnB = sb("sublnB_s", [128, 256], F32)
        lbB = sb("lbB_s", [128, 2048], F32)
        Smid = sb("Smid_s", [128, 16, 128], F32)
        stat = sb("stat_s", [128, 512], F32)
        lam = stat[:, 0:1]
        neglam = stat[:, 1:2]
        eps6 = stat[:, 8:9]
        eps5 = stat[:, 9:10]
        one1 = stat[:, 10:11]
        pb = [st.enter_context(nc.psum_tensor("pb%d" % i, [128, 512], F32)) for i in range(8)]

        R2 = 32768
        uT = big[:, 0:R2].rearrange("p (k t) -> p k t", t=1024)
        W = [big[:, R2 + s * 8192: R2 + (s + 1) * 8192].rearrange("p (k c) -> p k c", c=256) for s in range(4)]
        wide0 = [10 ** 9]
        A1_LO, A1_HI = R2 + 16384, 65536

        cnt = {"acc": 0, "small": 0, "stg": 0, "w": 0, "s4": 0, "since": 0}

        late_casts = []

        def flush_casts(upto=None):
            keep = []
            while late_casts:
                t, fn = late_casts.pop(0)
                if upto is None or t <= upto:
                    fn()
                else:
                    keep.append((t, fn))
            late_casts.extend(keep)

        def acc_next():
            cnt["since"] += 1
            if cnt["since"] >= 3:
                flush_casts()
            i = cnt["acc"] % 4
            cnt["acc"] += 1
            return pb[i], ("ps", i)

        def sbank_next():
            b = 6 + cnt["small"] % 2
            cnt["small"] += 1
            return b, ("ps", b)

        plan = []
        cur = [0]
        deferred = []

        def pump(n=1):
            while n > 0 and deferred:
                deferred.pop(0)()
                n -= 1

        def slot_of(i):
            return i % 2 if i < wide0[0] else (i - wide0[0]) % 4

        loaded = [0]

        def issue_load(i):
            s = slot_of(i)
            for j, seg in enumerate(plan[i]):
                g = cnt["stg"] % 3
                cnt["stg"] += 1
                S.op("sp", (lambda e, g=g, seg=seg: e.dma_start(out=stg[g][:], in_=seg.rearrange("(k p) c -> p k c", p=128))),
                     writes=[("stg", g)], dma="d_stg%d" % g)
                late = True
                ceng = ("pool", "act", "act", "dve")[j] if i >= wide0[0] else ("act", "act", "act", "dve")[j]

                def cast(ceng=ceng, g=g, s=s, j=j):
                    if ceng == "act":
                        S.op("act", (lambda e: e.activation(out=W[s][:, j * 8:(j + 1) * 8, :], in_=stg[g][:], func=AF.Copy)),
                             reads=[("stg", g)], writes=[("w", s, j)])
                    else:
                        S.op(ceng, (lambda e: e.tensor_copy(out=W[s][:, j * 8:(j + 1) * 8, :], in_=stg[g][:])),
                             reads=[("stg", g)], writes=[("w", s, j)])
                if late and j != 0:
                    late_casts.append((i, cast))
                    cnt["since"] = 0
                else:
                    cast()

        def next_tile(check=None):
            i = cur[0]
            cur[0] += 1
            if check is not None:
                assert plan[i][0] is check[0], "tile plan mismatch at %d" % i
            flush_casts()
            la = 1 if i < wide0[0] else 3
            while loaded[0] <= min(i + la, len(plan) - 1):
                flush_casts()
                issue_load(loaded[0])
                loaded[0] += 1
            flush_casts(upto=i)
            return slot_of(i)

        def win_tile(col0):
            return [w_in[j * 1024:(j + 1) * 1024, col0:col0 + 256] for j in range(4)]

        def wkeys(s):
            return [("w", s, j) for j in range(4)]

        def proj_fm(s, c0, half, ps, pkey, src=uT, srckeys=None, k0=0, nk=32, ksrc0=0):
            def fn(e):
                ins = None
                for k in range(nk):
                    ins = e.matmul(ps[:, 0:512], lhsT=W[s][:, k0 + k, c0:c0 + 128],
                                   rhs=src[:, ksrc0 + k, half * 512:(half + 1) * 512],
                                   start=(k == 0), stop=(k == nk - 1))
                return ins
            rk = srckeys if srckeys is not None else [("uT", tt) for tt in range(half * 4, half * 4 + 4)]
            S.op("pe", fn, reads=wkeys(s) + rk, writes=[pkey])

        def proj_tm(s, tt, ps, pkey, src=uT, srckeys=None):
            def fn(e):
                ins = None
                for k in range(KC):
                    ins = e.matmul(ps[:, 0:256], lhsT=src[:, k, tt * 128:(tt + 1) * 128], rhs=W[s][:, k, :],
                                   start=(k == 0), stop=(k == KC - 1))
                return ins
            rk = srckeys if srckeys is not None else [("uT", tt)]
            S.op("pe", fn, reads=wkeys(s) + rk, writes=[pkey])

        def cload(dst, src, key):
            S.op("sp", (lambda e: e.dma_start(out=dst, in_=src)), writes=[key], dma="d_c_" + key)

        cload(identf[:], identf_d[:, :], "identf")
        cload(TI[:], TI_d[:, :], "TI")
        cload(maskA[:], maskA_d[:, :], "maskA")
        cload(alibi[:], alibi_d.rearrange("p (a b c) -> p a b c", a=8, b=8), "alibi")
        cload(normwT[:], normwT_d[:, :], "normwT")
        cload(hnwB[:], hnwB_d[:, :], "hnwB")
        cload(sublnB[:], sublnB_d[:, :], "sublnB0")
        ar0 = Arena(big, A1_LO, A1_HI)
        lbt = ar0.alloc([128, 2, 2048], F32)
        lamv = ar0.alloc([128, 4, 128], F32)
        cm32 = ar0.alloc([128, 128], F32)
        lamp = ar0.alloc([128, 2, 128], F32)
        cload(lbt, lbt_d.rearrange("p (a b) -> p a b", a=2), "lbt")
        cload(lamv, lamv_d.rearrange("p (a b) -> p a b", a=4), "lamv")
        cload(cm32, cmask_d[:, :], "cm32")
        S.op("pool", lambda e: e.tensor_copy(out=identb[:], in_=identf[:]), reads=["identf"], writes=["identb"])
        S.op("pool", lambda e: e.tensor_copy(out=cmask[:], in_=cm32), reads=["cm32"], writes=["cmask"])
        S.op("pool", lambda e: e.memset(Smid[:], 0.0), writes=["Smid"])
        S.op("pool", lambda e: e.memset(eps6, 1e-6), writes=["eps6"])
        S.op("pool", lambda e: e.memset(one1, 1.0), writes=["one1"])
        S.op("pool", lambda e: e.memset(eps5, 1e-5), writes=["eps"])
        S.op("dve", lambda e: e.tensor_tensor(out=lbt[:, 0, :], in0=lbt[:, 0, :], in1=lbt[:, 1, :], op=ALU.subtract), reads=["lbt"], writes=["lbd"])
        S.op("act", lambda e: e.activation(out=lbB[:], in_=lbt[:, 0, :], func=AF.Sigmoid), reads=["lbd"], writes=["lbB"])
        S.op("dve", lambda e: e.tensor_tensor(out=lamp[:, 0, :], in0=lamv[:, 0, :], in1=lamv[:, 1, :], op=ALU.mult), reads=["lamv"], writes=["lamp0"])
        S.op("dve", lambda e: e.tensor_tensor(out=lamp[:, 1, :], in0=lamv[:, 2, :], in1=lamv[:, 3, :], op=ALU.mult), reads=["lamv"], writes=["lamp1"])
        S.op("dve", lambda e: e.reduce_sum(out=stat[:, 2:3], in_=lamp[:, 0, :], axis=AX.X), reads=["lamp0"], writes=["ls0"])
        S.op("dve", lambda e: e.reduce_sum(out=stat[:, 3:4], in_=lamp[:, 1, :], axis=AX.X), reads=["lamp1"], writes=["ls1"])
        S.op("act", lambda e: e.activation(out=stat[:, 4:6], in_=stat[:, 2:4], func=AF.Exp), reads=["ls0", "ls1"], writes=["le"])
        S.op("dve", lambda e: e.scalar_tensor_tensor(out=lam, in0=stat[:, 4:5], scalar=0.2, in1=stat[:, 5:6], op0=ALU.add, op1=ALU.subtract), reads=["le"], writes=["lam"])
        S.op("dve", lambda e: e.tensor_scalar(out=neglam, in0=lam, scalar1=-1.0, scalar2=None, op0=ALU.mult), reads=["lam"], writes=["neglam"])
        S.op("dve", lambda e: e.tensor_scalar(out=sublnB[:], in0=sublnB[:], scalar1=0.8, scalar2=None, op0=ALU.mult), reads=["sublnB0"], writes=["sublnB"])
        S.barrier()

        def phase_A(xsrc, tag):
            arA = Arena(big, A1_LO, A1_HI)
            arA2 = Arena(wk2, 0, WK2N)
            xs = [arA.alloc([128, D], F32) for _ in range(2)]
            xn = [arA2.alloc([128, D], BF16) for _ in range(2)]
            for tt in range(NT):
                sl = tt % 2
                S.op("sp", (lambda e, sl=sl, tt=tt: e.dma_start(out=xs[sl], in_=xsrc[tt * 128:(tt + 1) * 128, :])),
                     writes=[("xs", sl)], dma="d_xs%d" % sl)
                c0 = 16 + tt * 3
                S.op("act", (lambda e, sl=sl, c0=c0: e.activation(out=xn[sl], in_=xs[sl], func=AF.Square, accum_out=stat[:, c0:c0 + 1])),
                     reads=[("xs", sl)], writes=[("xn", sl), ("ssA", tt)])
                S.op("act", (lambda e, c0=c0: e.activation(out=stat[:, c0 + 1:c0 + 2], in_=stat[:, c0:c0 + 1], func=AF.Sqrt, bias=1e-6, scale=1.0 / D)),
                     reads=[("ssA", tt)], writes=[("sqA", tt)])
                S.op("dve", (lambda e, c0=c0: e.reciprocal(out=stat[:, c0 + 2:c0 + 3], in_=stat[:, c0 + 1:c0 + 2])),
                     reads=[("sqA", tt)], writes=[("rsA", tt)])
                S.op("act", (lambda e, sl=sl, c0=c0: e.activation(out=xn[sl], in_=xs[sl], func=AF.Copy, scale=stat[:, c0 + 2:c0 + 3])),
                     reads=[("xs", sl), ("rsA", tt)], writes=[("xn", sl)])
                for g in range(4):
                    pbf = pb[g][:, :].bitcast(BF16).rearrange("p (a b) -> p a b", b=128)

                    def trf(e, sl=sl, g=g, pbf=pbf):
                        ins = None
                        for i in range(8):
                            c = g * 8 + i
                            ins = e.transpose(out=pbf[:, i, :], in_=xn[sl][:, c * 128:(c + 1) * 128], identity=identb[:])
                        return ins
                    S.op("pe", trf, reads=[("xn", sl), "identb"], writes=[("ps", g)])
                    S.op("dve", (lambda e, g=g, tt=tt, pbf=pbf: e.tensor_tensor(
                        out=uT[:, g * 8:(g + 1) * 8, tt * 128:(tt + 1) * 128], in0=pbf[:, 0:8, :],
                        in1=normwT[:, g * 8:(g + 1) * 8].unsqueeze(2).to_broadcast([128, 8, 128]), op=ALU.mult)),
                        reads=[("ps", g), "normwT"], writes=[("uT", tt)])
            S.barrier()

        def hgrn_pair(hp, own):
            a1 = Arena(big, A1_LO, A1_HI)
            a2 = Arena(wk2, 0, WK2N)
            kte_tok = a1.alloc([128, 8, 256], BF16)
            v_tok = a1.alloc([128, 8, 256], BF16)
            enRT = a1.alloc([128, 2, 1024], BF16)
            kteT = a1.alloc([128, 2, 1024], BF16)
            qeT = a1.alloc([128, 2, 1024], BF16)
            sgT = a1.alloc([128, 2, 1024], BF16)
            Dbf = a1.alloc([128, 2, 16, 128], BF16)
            sg = [a2.alloc([128, 256], F32) for _ in range(2)]
            t1 = a2.alloc([128, 256], F32)
            lf = [a2.alloc([128, 256], F32) for _ in range(2)]
            ktok = [a2.alloc([128, 256], F32) for _ in range(2)]
            eR = a2.alloc([128, 256], F32)
            omlp = a2.alloc([128, 256], F32)
            scTm = [a2.alloc([128, 4, 128], BF16) for _ in range(2)]
            on_tok = [a2.alloc([128, 4, 128], F32) for _ in range(2)]
            oTh = [a2.alloc([128, 1024], BF16) for _ in range(2)]
            dec = a2.alloc([128, 2, 16], F32)
            hst = a2.alloc([128, 64], F32)
            vm = a2.alloc([128, 8, 2, 256], BF16)
            if hp == 0:
                S.op("pool", lambda e: e.memset(vm, 0.0), writes=["vm0"] + [("vm", tt, j) for tt in range(NT) for j in range(2)])
            lbp = lbB[:, hp * 256:(hp + 1) * 256]
            P = "A%d%d" % (hp, int(own))
            S.op("dve", lambda e: e.tensor_scalar(out=omlp, in0=lbp, scalar1=-1.0, scalar2=1.0, op0=ALU.mult, op1=ALU.add),
                 reads=["lbB"], writes=["omlp"])
            sf = next_tile()

            def rstuff(tt):
                b = tt % 2
                psR, pkR = acc_next()
                S.op("pe", (lambda e, psR=psR, b=b: e.matmul(psR[:, 0:256], lhsT=TI[:, 0:128], rhs=lf[b], start=True, stop=True)),
                     reads=[("lf", b), "TI"], writes=[pkR])
                S.op("act", (lambda e, psR=psR: e.activation(out=eR, in_=psR[:, 0:256], func=AF.Exp)),
                     reads=[pkR], writes=["eR"])
                S.op("dve", (lambda e, tt=tt, b=b: e.tensor_tensor(out=kte_tok[:, tt, :], in0=ktok[b], in1=eR, op=ALU.mult)),
                     reads=[("ktok", b), "eR"], writes=[("kte", tt)])
                psT, pkT = acc_next()
                if own:
                    def rtf(e, psT=psT, b=b):
                        e.matmul(psT[:, 0:130], lhsT=lf[b][:, 0:128], rhs=TI[:, 0:130], start=True, stop=True)
                        return e.matmul(psT[:, 256:386], lhsT=lf[b][:, 128:256], rhs=TI[:, 0:130], start=True, stop=True)
                    S.op("pe", rtf, reads=[("lf", b), "TI"], writes=[pkT])
                    for hh in range(2):
                        S.op("act", (lambda e, psT=psT, hh=hh, tt=tt: e.activation(out=enRT[:, hh, tt * 128:(tt + 1) * 128], in_=psT[:, hh * 256:hh * 256 + 128], func=AF.Exp, scale=-1.0)),
                             reads=[pkT], writes=[("enRT", hh, tt)])
                        S.op("act", (lambda e, psT=psT, hh=hh, tt=tt: e.activation(out=dec[:, hh, tt * 2:tt * 2 + 2], in_=psT[:, hh * 256 + 128:hh * 256 + 130], func=AF.Exp)),
                             reads=[pkT], writes=[("dec", hh, tt)])
                else:
                    def rtf(e, psT=psT, b=b):
                        e.matmul(psT[:, 0:2], lhsT=lf[b][:, 0:128], rhs=TI[:, 128:130], start=True, stop=True)
                        return e.matmul(psT[:, 256:258], lhsT=lf[b][:, 128:256], rhs=TI[:, 128:130], start=True, stop=True)
                    S.op("pe", rtf, reads=[("lf", b), "TI"], writes=[pkT])
                    for hh in range(2):
                        S.op("act", (lambda e, psT=psT, hh=hh, tt=tt: e.activation(out=dec[:, hh, tt * 2:tt * 2 + 2], in_=psT[:, hh * 256:hh * 256 + 2], func=AF.Exp)),
                             reads=[pkT], writes=[("dec", hh, tt)])

            for tt in range(NT):
                ps, pk = acc_next()
                proj_tm(sf, tt, ps, pk)
                b = tt % 2
                S.op("act", (lambda e, ps=ps, b=b: e.activation(out=sg[b], in_=ps[:, 0:256], func=AF.Exp, scale=-1.0)),
                     reads=[pk], writes=[("sg", b)])
                S.op("act", (lambda e, b=b: e.activation(out=sg[b], in_=sg[b], func=AF.Ln, bias=one1[:, 0:1], scale=1.0)),
                     reads=[("sg", b), "eps"], writes=[("sg", b)])
                S.op("act", (lambda e, b=b: e.activation(out=sg[b], in_=sg[b], func=AF.Exp, scale=-1.0)),
                     reads=[("sg", b)], writes=[("sg", b)])
                S.op("dve", (lambda e, b=b: e.tensor_tensor(out=t1, in0=sg[b], in1=omlp, op=ALU.mult)),
                     reads=[("sg", b), "omlp"], writes=["t1"])
                S.op("dve", (lambda e, b=b: e.tensor_tensor(out=sg[b], in0=t1, in1=lbp, op=ALU.add)),
                     reads=["t1", "lbB"], writes=[("sg", b)])
                S.op("dve", (lambda e, b=b: e.tensor_tensor(out=ktok[b], in0=omlp, in1=t1, op=ALU.subtract)),
                     reads=["t1", "omlp"], writes=[("ktok", b)])
                S.op("act", (lambda e, b=b: e.activation(out=lf[b], in_=sg[b], func=AF.Ln)),
                     reads=[("sg", b)], writes=[("lf", b)])
                pump(2 if tt < 4 else 1)
                if tt >= 1:
                    rstuff(tt - 1)
            pump(99)
            rstuff(NT - 1)
            if HST < 1:
                return

            def scan_group(tp):
                for hh in range(2):
                    h = hp * 2 + hh
                    b, bk = sbank_next()
                    pv = pb[b][:, 0:512].rearrange("p (t j v) -> p t j v", t=2, j=2)

                    def csf(e, pv=pv, tp=tp, hh=hh):
                        ins = None
                        for t2 in range(2):
                            tt = tp * 2 + t2
                            ins = e.matmul(pv[:, t2, :, :], lhsT=kte_tok[:, tt, hh * 128:(hh + 1) * 128],
                                           rhs=vm[:, tt, :, hh * 128:(hh + 1) * 128], start=True, stop=True)
                        return ins
                    S.op("pe", csf, reads=[("kte", tp * 2), ("kte", tp * 2 + 1)] + [("vm", tp * 2 + t2, j) for t2 in range(2) for j in range(2)], writes=[bk])
                    for t2 in range(2):
                        for j in range(2):
                            tt = tp * 2 + t2
                            n = tt * 2 + j
                            psc = pv[:, t2, j, :]
                            if own:
                                S.op("dve", (lambda e, h=h, hh=hh, n=n: e.tensor_scalar(out=Dbf[:, hh, n, :], in0=Smid[:, h, :], scalar1=dec[:, hh, n:n + 1], scalar2=None, op0=ALU.mult)),
                                     reads=[("Smid", h), ("dec", hh, tt)], writes=[("Dbf", hh, n)])
                            S.op("dve", (lambda e, h=h, hh=hh, n=n, psc=psc: e.scalar_tensor_tensor(out=Smid[:, h, :], in0=Smid[:, h, :], scalar=dec[:, hh, n:n + 1], in1=psc, op0=ALU.mult, op1=ALU.add)),
                                 reads=[("Smid", h), "Smid", ("dec", hh, tt), bk], writes=[("Smid", h)])

            def kte_transposes():
                if not own:
                    return
                for hh in range(2):
                    b, bk = sbank_next()
                    pvb = pb[b][:, :].bitcast(BF16).rearrange("p (a c) -> p a c", c=128)

                    def ktf(e, pvb=pvb, hh=hh):
                        ins = None
                        for tt in range(NT):
                            ins = e.transpose(out=pvb[:, tt, :], in_=kte_tok[:, tt, hh * 128:(hh + 1) * 128], identity=identb[:])
                        return ins
                    S.op("pe", ktf, reads=[("kte", tt) for tt in range(NT)] + ["identb"], writes=[bk])
                    S.op("act", (lambda e, pvb=pvb, hh=hh: e.activation(out=kteT[:, hh, :].rearrange("p (a c) -> p a c", c=128), in_=pvb[:, 0:8, :], func=AF.Copy)),
                         reads=[bk], writes=[("kteT", hh)])
            si = next_tile()
            for tt in range(NT):
                ps, pk = acc_next()
                proj_tm(si, tt, ps, pk)
                S.op("act", (lambda e, ps=ps, tt=tt: e.activation(out=v_tok[:, tt, :], in_=ps[:, 0:256], func=AF.Copy)),
                     reads=[pk], writes=[("vtok", tt)])
                for j in range(2):
                    S.op("act", (lambda e, ps=ps, tt=tt, j=j: e.activation(out=vm[j * 64:(j + 1) * 64, tt, j, :], in_=ps[j * 64:(j + 1) * 64, 0:256], func=AF.Copy)),
                         reads=[pk, "vm0"], writes=[("vm", tt, j)])
                if tt == 1:
                    kte_transposes()
                if tt >= 2 and tt % 2 == 0:
                    scan_group(tt // 2 - 1)
            if not own:
                scan_group(3)
                return
            if HST < 4:
                return
            sq = next_tile()
            qg = 0
            for hh in range(2):
                for half in range(2):
                    ps, pk = acc_next()
                    proj_fm(sq, hh * 128, half, ps, pk)
                    S.op("dve", (lambda e, ps=ps, hh=hh, half=half: e.tensor_tensor(out=qeT[:, hh, half * 512:(half + 1) * 512], in0=ps[:, 0:512], in1=enRT[:, hh, half * 512:(half + 1) * 512], op=ALU.mult)),
                         reads=[pk] + [("enRT", hh, t_) for t_ in range(half * 4, half * 4 + 4)], writes=[("qeT", hh, half)])
                    if qg == 0:
                        scan_group(3)
                    qg += 1
            sgw = next_tile()
            for hh in range(2):
                for half in range(2):
                    ps, pk = acc_next()
                    proj_fm(sgw, hh * 128, half, ps, pk)
                    S.op("act", (lambda e, ps=ps, hh=hh, half=half: e.activation(out=sgT[:, hh, half * 512:(half + 1) * 512], in_=ps[:, 0:512], func=AF.Silu)),
                         reads=[pk], writes=[("sgT", hh, half)])
            gi = 0
            links = []
            for hh in range(2):
                h = hp * 2 + hh
                ob = hh % 2
                for half in range(2):
                    g = gi % 2
                    gi += 1
                    bs = 6 + g
                    bo = 4 + g
                    tts = list(range(half * 4, half * 4 + 4))
                    pS = pb[bs][:, :].rearrange("p (a c) -> p a c", c=128)
                    pO = pb[bo][:, :].rearrange("p (a c) -> p a c", c=128)

                    def link1(pS=pS, hh=hh, tts=tts, g=g, bs=bs, half=half):
                        def scf(e):
                            ins = None
                            for i, tt in enumerate(tts):
                                ins = e.matmul(pS[:, i, :], lhsT=kteT[:, hh, tt * 128:(tt + 1) * 128], rhs=qeT[:, hh, tt * 128:(tt + 1) * 128], start=True, stop=True)
                            return ins
                        S.op("pe", scf, reads=[("kteT", hh), ("qeT", hh, half)], writes=[("ps", bs)])
                        S.op("dve", (lambda e: e.tensor_tensor(out=scTm[g], in0=pS[:, 0:4, :], in1=maskA[:].unsqueeze(1).to_broadcast([128, 4, 128]), op=ALU.mult)),
                             reads=[("ps", bs), "maskA"], writes=[("scTm", g)])

                    def link2(pO=pO, hh=hh, tts=tts, g=g, bo=bo, half=half):
                        def omm(e):
                            ins = None
                            for i, tt in enumerate(tts):
                                e.matmul(pO[:, i, :], lhsT=scTm[g][:, i, :], rhs=v_tok[:, tt, hh * 128:(hh + 1) * 128], start=True, stop=False)
                                e.matmul(pO[0:64, i, :], lhsT=qeT[:, hh, tt * 128:tt * 128 + 64], rhs=Dbf[:, hh, 2 * tt, :], start=False, stop=True)
                                ins = e.matmul(pO[64:128, i, :], lhsT=qeT[:, hh, tt * 128 + 64:tt * 128 + 128], rhs=Dbf[:, hh, 2 * tt + 1, :], start=False, stop=True)
                            return ins
                        S.op("pe", omm, reads=[("scTm", g), ("qeT", hh, half)] + [("vtok", tt) for tt in tts] + [("Dbf", hh, n) for n in range(half * 8, half * 8 + 8)], writes=[("ps", bo)])
                        c0 = g * 16
                        for i in range(4):
                            S.op("act", (lambda e, i=i: e.activation(out=on_tok[g][:, i, :], in_=pO[:, i, :], func=AF.Square, accum_out=hst[:, c0 + i:c0 + i + 1])),
                                 reads=[("ps", bo)], writes=[("on", g), ("hs0", g, i)])
                        S.op("act", (lambda e: e.activation(out=hst[:, c0 + 4:c0 + 8], in_=hst[:, c0:c0 + 4], func=AF.Ln, bias=eps6[:, 0:1], scale=1.0 / 128)),
                             reads=[("hs0", g, i) for i in range(4)] + ["eps"], writes=[("hs1", g)])
                        S.op("act", (lambda e: e.activation(out=hst[:, c0 + 8:c0 + 12], in_=hst[:, c0 + 4:c0 + 8], func=AF.Exp, scale=-0.5)),
                             reads=[("hs1", g)], writes=[("hs2", g)])
                        for i in range(4):
                            S.op("dve", (lambda e, i=i: e.scalar_tensor_tensor(out=on_tok[g][:, i, :], in0=pO[:, i, :], scalar=hst[:, c0 + 8 + i:c0 + 9 + i], in1=hnwB[:], op0=ALU.mult, op1=ALU.mult)),
                                 reads=[("ps", bo), ("hs2", g), ("on", g), "hnwB"], writes=[("on", g)])

                    def link3(pS=pS, hh=hh, g=g, bs=bs, half=half, ob=ob, h=h):
                        def otf(e):
                            ins = None
                            for i in range(4):
                                ins = e.transpose(out=pS[:, i, :], in_=on_tok[g][:, i, :], identity=identf[:])
                            return ins
                        S.op("pe", otf, reads=[("on", g), "identf"], writes=[("ps", bs)])
                        S.op("dve", (lambda e: e.tensor_tensor(out=oTh[ob][:, half * 512:(half + 1) * 512], in0=pS[:, 0:4, :].rearrange("p a c -> p (a c)"), in1=sgT[:, hh, half * 512:(half + 1) * 512], op=ALU.mult)),
                             reads=[("ps", bs), ("sgT", hh, half)], writes=[("oTh", ob)])
                        if half == 1:
                            S.op("sp", (lambda e: e.dma_start(out=oT_scr[h], in_=oTh[ob])), reads=[("oTh", ob)], writes=[("oTs", h)], dma="d_oTh%d" % ob)
                    links.append((link1, link2, link3))
            order = [(0, 0), (0, 1), (1, 0), (1, 1), (0, 2), (2, 0), (1, 2), (2, 1), (0, 3), (3, 0), (1, 3), (2, 2), (3, 1), (3, 2)]
            L = links
            seq = [L[0][0], L[0][1], L[1][0], L[1][1], L[0][2], L[2][0], L[2][1], L[1][2], L[3][0], L[3][1], L[2][2], L[3][2]]
            deferred.extend(seq)


        def attn_prev(h):
            a1 = Arena(big, A1_LO, A1_HI)
            kTp = a1.alloc([128, 2, 1024], BF16)
            Vp = a1.alloc([128, 8, 256], BF16)
            skw = next_tile()
            for m in range(2):
                for half in range(2):
                    ps, pk = acc_next()
                    proj_fm(skw, m * 128, half, ps, pk)
                    S.op("act", (lambda e, ps=ps, m=m, half=half: e.activation(out=kTp[:, m, half * 512:(half + 1) * 512], in_=ps[:, 0:512], func=AF.Copy)),
                         reads=[pk], writes=[("kTp", m, half)])
            S.op("sp", (lambda e: e.dma_start(out=kprev[h], in_=kTp.rearrange("p a b -> p (a b)"))),
                 reads=[("kTp", m, half) for m in range(2) for half in range(2)], writes=[("kprev", h)], dma="d_kTp")
            svw = next_tile()
            for tt in range(NT):
                ps, pk = acc_next()
                proj_tm(svw, tt, ps, pk)
                S.op("dve", (lambda e, ps=ps, tt=tt: e.tensor_copy(out=Vp[:, tt, :], in_=ps[:, 0:256])),
                     reads=[pk], writes=[("Vp", tt)])
            S.op("sp", (lambda e: e.dma_start(out=vprev[h], in_=Vp.rearrange("p a b -> p (a b)"))),
                 reads=[("Vp", tt) for tt in range(NT)], writes=[("vprev", h)], dma="d_Vp")

        def attn_own(h):
            a1 = Arena(big, A1_LO, A1_HI)
            a2 = Arena(wk2, 0, WK2N)
            kT = a1.alloc([128, 2, 2048], BF16)
            V = a1.alloc([128, 16, 264], BF16)
            qT = a1.alloc([128, 2, 1024], BF16)
            sgT = a1.alloc([128, 2, 1024], BF16)
            PT = [a1.alloc([128, 128], BF16) for _ in range(16)]
            ob_ = [a2.alloc([128, 256], F32) for _ in range(2)]
            tb_ = [a2.alloc([128, 256], F32) for _ in range(2)]
            sqj = a2.alloc([128, 256], BF16)
            oTh = [a2.alloc([128, 1024], BF16) for _ in range(2)]
            bst = a2.alloc([128, 64], F32)
            S.op("pool", lambda e: e.memset(V[:, :, 256:257], 1.0), writes=["Vones"])
            S.op("sp", (lambda e: e.dma_start(out=kT[:, :, 0:1024], in_=kprev[h].rearrange("p (a b) -> p a b", a=2))),
                 reads=[("kprev", h)], writes=["kTprev"], dma="d_kTl")
            S.op("sp", (lambda e: e.dma_start(out=V[:, 0:8, 0:256], in_=vprev[h].rearrange("p (a b) -> p a b", a=8))),
                 reads=[("vprev", h)], writes=["Vprev"], dma="d_Vl")
            skw = next_tile()
            for m in range(2):
                for half in range(2):
                    ps, pk = acc_next()
                    proj_fm(skw, m * 128, half, ps, pk)
                    S.op("act", (lambda e, ps=ps, m=m, half=half: e.activation(out=kT[:, m, 1024 + half * 512:1024 + (half + 1) * 512], in_=ps[:, 0:512], func=AF.Copy)),
                         reads=[pk], writes=[("kT", m, half)])
                    pump(1)
            svw = next_tile()
            for tt in range(NT):
                ps, pk = acc_next()
                proj_tm(svw, tt, ps, pk)
                S.op("dve", (lambda e, ps=ps, tt=tt: e.tensor_copy(out=V[:, 8 + tt, 0:256], in_=ps[:, 0:256])),
                     reads=[pk], writes=[("V", tt)])
            sqw = next_tile()
            for m in range(2):
                for half in range(2):
                    ps, pk = acc_next()
                    proj_fm(sqw, m * 128, half, ps, pk)
                    S.op("act", (lambda e, ps=ps, m=m, half=half: e.activation(out=qT[:, m, half * 512:(half + 1) * 512], in_=ps[:, 0:512], func=AF.Copy)),
                         reads=[pk], writes=[("qT", m, half)])
            sgw = next_tile()
            for j in range(2):
                for half in range(2):
                    ps, pk = acc_next()
                    proj_fm(sgw, j * 128, half, ps, pk)
                    S.op("act", (lambda e, ps=ps, j=j, half=half: e.activation(out=sgT[:, j, half * 512:(half + 1) * 512], in_=ps[:, 0:512], func=AF.Silu)),
                         reads=[pk], writes=[("sgT", j, half)])
            ptc = [0]
            for qb in range(NT):
                qhalf = qb // 4
                nkb = 9 + qb

                def kkeys(m, kbs):
                    ks = set()
                    for kb in kbs:
                        ks.add("kTprev" if kb < 8 else ("kT", m, (kb - 8) // 4))
                    return list(ks)

                def vkeys(kbs):
                    ks = {"Vones"}
                    for kb in kbs:
                        ks.add("Vprev" if kb < 8 else ("V", kb - 8))
                    return list(ks)

                groups = []
                for g0 in range(0, nkb, 4):
                    for m in range(2):
                        groups.append((m, list(range(g0, min(g0 + 4, nkb)))))

                oa = (4, 5) if qb % 2 == 0 else (2, 3)

                def issue_s(m, kbs, qb=qb, nkb=nkb):
                    b = (6, 7, 0, 1)[cnt["s4"] % 4]
                    cnt["s4"] += 1
                    bk = ("ps", b)
                    pS = pb[b][:, :].rearrange("p (a c) -> p a c", c=128)

                    def fn(e, pS=pS, m=m, kbs=kbs):
                        ins = None
                        for i, kb in enumerate(kbs):
                            diag = (kb == nkb - 1)
                            ins = e.matmul(pS[:, i, :], lhsT=kT[:, m, kb * 128:(kb + 1) * 128], rhs=qT[:, m, qb * 128:(qb + 1) * 128], start=True, stop=not diag)
                            if diag:
                                ins = e.matmul(pS[:, i, :], lhsT=identb[:], rhs=cmask[:], start=False, stop=True)
                        return ins
                    S.op("pe", fn, reads=kkeys(m, kbs) + [("qT", m, qhalf), "identb", "cmask"], writes=[bk])
                    slots = []
                    for i, kb in enumerate(kbs):
                        pi = ptc[0] % 16
                        ptc[0] += 1
                        S.op("act", (lambda e, pS=pS, i=i, pi=pi, kb=kb: e.activation(out=PT[pi], in_=pS[:, i, :], func=AF.Exp, bias=alibi[:, h, qb, kb:kb + 1], scale=QSCALE)),
                             reads=[bk, "alibi"], writes=[("PT", pi)])
                        slots.append(pi)
                    return slots

                def issue_av(m, kbs, slots, nkb=nkb, oa=oa):
                    po = pb[oa[m]]

                    def fn(e, po=po, kbs=kbs, slots=slots):
                        ins = None
                        for kb, pi in zip(kbs, slots):
                            ins = e.matmul(po[:, 0:257], lhsT=PT[pi], rhs=V[:, kb, 0:257], start=(kb == 0), stop=(kb == nkb - 1))
                        return ins
                    S.op("pe", fn, reads=[("PT", pi) for pi in slots] + vkeys(kbs), writes=[("ps", oa[m])])

                pend = []
                for gi_, (m, kbs) in enumerate(groups):
                    slots = issue_s(m, kbs)
                    pend.append((m, kbs, slots))
                    if gi_ == 1:
                        pump(1)
                    if len(pend) > 2:
                        issue_av(*pend.pop(0))
                while pend:
                    issue_av(*pend.pop(0))

                ib = qb % 2
                o0, o1 = oa
                S.op("dve", (lambda e, o0=o0: e.reciprocal(out=bst[:, 0:1], in_=pb[o0][:, 256:257])), reads=[("ps", o0)], writes=["r0"])
                S.op("dve", (lambda e, o1=o1: e.reciprocal(out=bst[:, 1:2], in_=pb[o1][:, 256:257])), reads=[("ps", o1)], writes=["r1"])
                S.op("dve", (lambda e: e.tensor_tensor(out=bst[:, 2:3], in0=bst[:, 1:2], in1=neglam, op=ALU.mult)), reads=["r1", "neglam"], writes=["r1l"])
                S.op("dve", (lambda e, ib=ib, o1=o1: e.tensor_scalar(out=tb_[ib], in0=pb[o1][:, 0:256], scalar1=bst[:, 2:3], scalar2=None, op0=ALU.mult)),
                     reads=[("ps", o1), "r1l"], writes=[("tb", ib)])
                S.op("dve", (lambda e, ib=ib, o0=o0: e.scalar_tensor_tensor(out=ob_[ib], in0=pb[o0][:, 0:256], scalar=bst[:, 0:1], in1=tb_[ib], op0=ALU.mult, op1=ALU.add)),
                     reads=[("ps", o0), "r0", ("tb", ib)], writes=[("ob", ib)])
                c0 = 8 + ib * 4
                S.op("act", (lambda e, ib=ib, c0=c0: e.activation(out=sqj, in_=ob_[ib], func=AF.Square, accum_out=bst[:, c0:c0 + 1])),
                     reads=[("ob", ib)], writes=["sqj", ("bs0", ib)])
                S.op("act", (lambda e, c0=c0: e.activation(out=bst[:, c0 + 1:c0 + 2], in_=bst[:, c0:c0 + 1], func=AF.Ln, bias=eps5[:, 0:1], scale=1.0 / 256)),
                     reads=[("bs0", ib), "eps"], writes=[("bs1", ib)])
                S.op("act", (lambda e, c0=c0: e.activation(out=bst[:, c0 + 2:c0 + 3], in_=bst[:, c0 + 1:c0 + 2], func=AF.Exp, scale=-0.5)),
                     reads=[("bs1", ib)], writes=[("bs2", ib)])
                S.op("dve", (lambda e, ib=ib, c0=c0: e.scalar_tensor_tensor(out=ob_[ib], in0=ob_[ib], scalar=bst[:, c0 + 2:c0 + 3], in1=sublnB[:], op0=ALU.mult, op1=ALU.mult)),
                     reads=[("ob", ib), ("bs2", ib), "sublnB"], writes=[("ob", ib)])

                def fin_pe(ib=ib, qb=qb, qhalf=qhalf):
                    bT = (6, 7, 0, 1)[cnt["s4"] % 4]
                    cnt["s4"] += 1
                    bTk = ("ps", bT)

                    def btf(e):
                        ins = None
                        for j in range(2):
                            ins = e.transpose(out=pb[bT][:, j * 128:(j + 1) * 128], in_=ob_[ib][:, j * 128:(j + 1) * 128], identity=identf[:])
                        return ins
                    S.op("pe", btf, reads=[("ob", ib), "identf"], writes=[bTk])
                    for j in range(2):
                        S.op("dve", (lambda e, j=j: e.tensor_tensor(out=oTh[j][:, qb * 128:(qb + 1) * 128], in0=pb[bT][:, j * 128:(j + 1) * 128], in1=sgT[:, j, qb * 128:(qb + 1) * 128], op=ALU.mult)),
                             reads=[bTk, ("sgT", j, qhalf)], writes=[("oThB", j)])
                    if qb == NT - 1:
                        for j in range(2):
                            S.op("sp", (lambda e, j=j: e.dma_start(out=oT_scr[16 + 2 * h + j], in_=oTh[j])), reads=[("oThB", j)], writes=[("oTs", 16 + 2 * h + j)], dma="d_oThB%d" % j)
                deferred.append(fin_pe)

        def gates():
            a2 = Arena(wk2, 0, WK2N)
            sigt = [a2.alloc([128, 1024], BF16) for _ in range(4)]
            gc = 0
            for gt in range(NGT):
                sw = next_tile()
                for j in range(2):
                    b = gc % 4
                    gc += 1
                    for half in range(2):
                        ps, pk = acc_next()
                        proj_fm(sw, j * 128, half, ps, pk)
                        S.op("act", (lambda e, ps=ps, b=b, half=half: e.activation(out=sigt[b][:, half * 512:(half + 1) * 512], in_=ps[:, 0:512], func=AF.Sigmoid)),
                             reads=[pk], writes=[("sigt", b, half)])
                    S.op("act", (lambda e, b=b, gt=gt, j=j: e.dma_start(out=sig_scr[gt * 2 + j], in_=sigt[b])),
                         reads=[("sigt", b, 0), ("sigt", b, 1)], writes=[("sigs", gt * 2 + j)], dma="d_sig%d" % b)

        def phase_C():
            oT = uT
            for q4 in range(4):
                S.op("sp", (lambda e, q4=q4: e.dma_start(out=oT[:, q4 * 8:(q4 + 1) * 8, :], in_=oT_scr[q4 * 8:(q4 + 1) * 8].rearrange("c p t -> p c t"))),
                     reads=[("oTs", c) for c in range(q4 * 8, q4 * 8 + 8)], writes=[("oT", q4)], dma="d_oTl%d" % q4)
            a2 = Arena(wk2, 0, WK2N)
            sA = [a2.alloc([128, 1024], BF16) for _ in range(4)]
            sB = [a2.alloc([128, 1024], BF16) for _ in range(4)]

            def sig_load(ft):
                for j in range(2):
                    fc = ft * 2 + j
                    b = fc % 4
                    S.op("act", (lambda e, b=b, fc=fc: e.dma_start(out=sA[b], in_=sig_scr[fc])), reads=[("sigs", fc)], writes=[("sA", b)], dma="d_sA%d" % b)
                    S.op("act", (lambda e, b=b, fc=fc: e.dma_start(out=sB[b], in_=sig_scr[32 + fc])), reads=[("sigs", 32 + fc)], writes=[("sB", b)], dma="d_sB%d" % b)
            sig_load(0)
            tA = [a2.alloc([128, 512], F32) for _ in range(2)]
            tB = [a2.alloc([128, 512], F32) for _ in range(2)]
            yst = [a2.alloc([128, 1024], BF16) for _ in range(2)]
            fcn = 0
            for ft in range(16):
                sw = next_tile()
                if ft + 1 < 16:
                    sig_load(ft + 1)
                for j in range(2):
                    fc = ft * 2 + j
                    b = fc % 4
                    yb = fc % 2
                    for half in range(2):
                        psA, pkA = acc_next()
                        proj_fm(sw, j * 128, half, psA, pkA, src=oT, srckeys=[("oT", 0), ("oT", 1)], k0=0, nk=16, ksrc0=0)
                        psB, pkB = acc_next()
                        proj_fm(sw, j * 128, half, psB, pkB, src=oT, srckeys=[("oT", 2), ("oT", 3)], k0=16, nk=16, ksrc0=16)
                        tb = half
                        S.op("dve", (lambda e, psA=psA, b=b, half=half, tb=tb: e.tensor_tensor(out=tA[tb], in0=psA[:, 0:512], in1=sA[b][:, half * 512:(half + 1) * 512], op=ALU.mult)),
                             reads=[pkA, ("sA", b)], writes=[("tA", tb)])
                        S.op("dve", (lambda e, psB=psB, b=b, half=half, tb=tb: e.tensor_tensor(out=tB[tb], in0=psB[:, 0:512], in1=sB[b][:, half * 512:(half + 1) * 512], op=ALU.mult)),
                             reads=[pkB, ("sB", b)], writes=[("tB", tb)])
                        S.op("dve", (lambda e, yb=yb, half=half, tb=tb: e.tensor_tensor(out=yst[yb][:, half * 512:(half + 1) * 512], in0=tA[tb], in1=tB[tb], op=ALU.add)),
                             reads=[("tA", tb), ("tB", tb)], writes=[("yst", yb, half)])
                    S.op("act", (lambda e, yb=yb, fc=fc: e.dma_start(out=yT_scr[fc], in_=yst[yb])),
                         reads=[("yst", yb, 0), ("yst", yb, 1)], writes=[("yTs", fc)], dma="d_yst%d" % yb)
            S.barrier()
            yT = uT
            for q4 in range(4):
                S.op("sp", (lambda e, q4=q4: e.dma_start(out=yT[:, q4 * 8:(q4 + 1) * 8, :], in_=yT_scr[q4 * 8:(q4 + 1) * 8].rearrange("c p t -> p c t"))),
                     reads=[("yTs", c) for c in range(q4 * 8, q4 * 8 + 8)], writes=[("yT", q4)], dma="d_yTl%d" % q4)
            a2 = Arena(wk2, 0, WK2N)
            hsb = [a2.alloc([128, 256], F32) for _ in range(4)]
            xr = [a2.alloc([128, 256], F32) for _ in range(4)]
            sqj = a2.alloc([128, 256], BF16)
            ssq = a2.alloc([128, 8, 16], F32)
            fst = a2.alloc([128, 32], F32)
            rc = 0
            for cb in range(16):
                sw = next_tile()
                for tt in range(NT):
                    r = rc % 4
                    rc += 1
                    S.op("act", (lambda e, r=r, tt=tt, cb=cb: e.dma_start(out=xr[r], in_=x_own[tt * 128:(tt + 1) * 128, cb * 256:(cb + 1) * 256])),
                         writes=[("xr", r)], dma="d_xr%d" % r)
                    ps, pk = acc_next()
                    proj_tm(sw, tt, ps, pk, src=yT, srckeys=[("yT", q4) for q4 in range(4)])
                    S.op("dve", (lambda e, ps=ps, r=r: e.tensor_tensor(out=hsb[r], in0=ps[:, 0:256], in1=xr[r], op=ALU.add)),
                         reads=[pk, ("xr", r)], writes=[("hsb", r)])
                    S.op("act", (lambda e, r=r, tt=tt, cb=cb: e.activation(out=sqj, in_=hsb[r], func=AF.Square, accum_out=ssq[:, tt, cb:cb + 1])),
                         reads=[("hsb", r)], writes=["sqjC", ("ssq", tt, cb)])
                    S.op("act", (lambda e, r=r, tt=tt, cb=cb: e.dma_start(out=out[tt * 128:(tt + 1) * 128, cb * 256:(cb + 1) * 256], in_=hsb[r])),
                         reads=[("hsb", r)], writes=[("outh", tt, cb)], dma="d_hs%d" % r)
            S.barrier()
            arF = Arena(big, 0, R2)
            fwB = arF.alloc([128, D], F32)
            hrow = [arF.alloc([128, D], F32) for _ in range(2)]
            S.op("sp", (lambda e: e.dma_start(out=fwB, in_=fwB_d[:, :])), writes=["fwB"], dma="d_fwB")
            for tt in range(NT):
                b = tt % 2
                S.op("sp", (lambda e, b=b, tt=tt: e.dma_start(out=hrow[b], in_=out[tt * 128:(tt + 1) * 128, :])),
                     reads=[("outh", tt, cb) for cb in range(16)], writes=[("hrow", b)], dma="d_hrl%d" % b)
                S.op("dve", (lambda e, tt=tt: e.reduce_sum(out=fst[:, tt * 3:tt * 3 + 1], in_=ssq[:, tt, :], axis=AX.X)),
                     reads=[("ssq", tt, cb) for cb in range(16)], writes=[("f0", tt)])
                S.op("act", (lambda e, tt=tt: e.activation(out=fst[:, tt * 3 + 1:tt * 3 + 2], in_=fst[:, tt * 3:tt * 3 + 1], func=AF.Sqrt, bias=1e-6, scale=1.0 / D)),
                     reads=[("f0", tt)], writes=[("f1", tt)])
                S.op("dve", (lambda e, tt=tt: e.reciprocal(out=fst[:, tt * 3 + 2:tt * 3 + 3], in_=fst[:, tt * 3 + 1:tt * 3 + 2])),
                     reads=[("f1", tt)], writes=[("f2", tt)])
                S.op("act", (lambda e, b=b, tt=tt: e.activation(out=hrow[b], in_=hrow[b], func=AF.Copy, scale=fst[:, tt * 3 + 2:tt * 3 + 3])),
                     reads=[("hrow", b), ("f2", tt)], writes=[("hrowB", b)])
                S.op("dve", (lambda e, b=b, tt=tt: e.tensor_tensor(out=hrow[b], in0=hrow[b], in1=fwB, op=ALU.mult)),
                     reads=[("hrowB", b), "fwB"], writes=[("hrowA", b)])
                S.op("sp", (lambda e, b=b, tt=tt: e.dma_start(out=out[tt * 128:(tt + 1) * 128, :], in_=hrow[b])),
                     reads=[("hrowA", b), ("hrowB", b)], writes=[("outf", tt), ("hrow", b)], dma="d_hrs%d" % b)
            S.op("sp", None, reads=[("outf", tt) for tt in range(NT)])

        for hp in range(NPAIR):
            plan.append(win_tile(C_AF + hp * 256))
            plan.append(win_tile(C_AI + hp * 256))
        for h in range(NBH):
            plan.append(win_tile(C_BK + h * 256))
            plan.append(win_tile(C_BV + h * 256))
        for hp in range(NPAIR):
            for c in (C_AF, C_AI, C_AQ, C_AG):
                plan.append(win_tile(c + hp * 256))
        for h in range(NBH):
            for c in (C_BK, C_BV, C_BQ, C_BG):
                plan.append(win_tile(c + h * 256))
        wide0[0] = len(plan)
        for gt in range(NGT):
            plan.append(win_tile(C_GA + gt * 256))
        if DO_C:
            for ft in range(16):
                plan.append([w_a[0:1024, ft * 256:(ft + 1) * 256], w_a[1024:2048, ft * 256:(ft + 1) * 256],
                             w_b[0:1024, ft * 256:(ft + 1) * 256], w_b[1024:2048, ft * 256:(ft + 1) * 256]])
            for cb in range(16):
                plan.append([w_out[j * 1024:(j + 1) * 1024, cb * 256:(cb + 1) * 256] for j in range(4)])

        phase_A(x_prev, "p")
        for hp in range(NPAIR):
            hgrn_pair(hp, own=False)
        S.barrier()
        for h in range(NBH):
            attn_prev(h)
        S.barrier()
        phase_A(x_own, "o")
        for hp in range(NPAIR):
            hgrn_pair(hp, own=True)
        pump(999)
        S.barrier()
        for h in range(NBH):
            attn_own(h)
        pump(999)
        S.barrier()
        gates()
        S.barrier()
        if DO_C:
            phase_C()
        else:
            S.op("sp", None, reads=[("oTs", c) for c in range(32)] + [("sigs", c) for c in range(64)])
        assert HST < 99 or cur[0] == len(plan), (cur[0], len(plan))
        S.emit(nc, st)
    return nc


def host_consts(half):
    p = np.arange(128)
    identf = np.eye(128, dtype=np.float32)
    TI = np.zeros((128, 130), np.float32)
    s = p[:, None]
    t = p[None, :]
    TI[:, :128] = ((s // 64 == t // 64) & (s > t)).astype(np.float32)
    TI[:, 128] = (p < 64)
    TI[:, 129] = (p >= 64)
    maskA = ((s // 64 == t // 64) & (s <= t)).astype(np.float32)
    cmask = np.where(s <= t, 0.0, -30000.0).astype(np.float32)
    slopes = np.exp2(-(np.arange(8, dtype=np.float64) + 1.0))
    al = np.zeros((128, 8, 8, 16), np.float32)
    for qb in range(8):
        gq = half * 8 + qb
        for kb in range(16):
            gk = kb if half == 1 else kb - 8
            if gk < 0:
                al[:, :, qb, kb] = -30000.0
            else:
                kpos = gk * 128 + p
                cref = gq * 128 + 64
                al[:, :, qb, kb] = (slopes[None, :] * (kpos[:, None] - cref)).astype(np.float32)
    return dict(identf=identf, TI=TI, maskA=maskA, cmask=cmask, alibi=al.reshape(128, 1024))


def make_in_maps(x, norm_w, w_in, lower_bound_table, hgrn_norm_w, lambda_q1, lambda_k1,
                 lambda_q2, lambda_k2, subln_w, w_branch_a, w_branch_b, w_out, final_w, cores=range(NCORES)):
    f = lambda a: np.ascontiguousarray(np.asarray(a, dtype=np.float32))
    x = f(x)
    shared = dict(
        w_in=f(w_in[0]), w_a=f(w_branch_a[0]), w_b=f(w_branch_b[0]), w_out=f(w_out[0]),
        normwT=f(np.asarray(norm_w[0]).reshape(32, 128).T),
        fwB=f(np.broadcast_to(np.asarray(final_w)[None, :], (128, D))),
        lbt=f(np.broadcast_to(np.asarray(lower_bound_table).reshape(1, 4096), (128, 4096))),
        hnwB=f(np.broadcast_to(np.asarray(hgrn_norm_w[0])[None, :], (128, 128))),
        sublnB=f(np.broadcast_to(np.asarray(subln_w[0])[None, :], (128, 256))),
        lamv=f(np.broadcast_to(np.concatenate([np.asarray(lambda_q1[0]), np.asarray(lambda_k1[0]),
                                               np.asarray(lambda_q2[0]), np.asarray(lambda_k2[0])])[None, :], (128, 512))),
    )
    hc = [host_consts(0), host_consts(1)]
    zeros = np.zeros((T, D), np.float32)
    maps = []
    for c in cores:
        b, half = c // 2, c % 2
        m = dict(shared)
        m.update(hc[half])
        m["x_own"] = f(x[b, half * T:(half + 1) * T])
        m["x_prev"] = f(x[b, 0:T]) if half == 1 else zeros
        maps.append(m)
    return maps


_NC_CACHE = {}


def kernel(**inputs):
    if "nc" not in _NC_CACHE:
        _NC_CACHE["nc"] = build_program()
    nc = _NC_CACHE["nc"]
    maps = make_in_maps(**inputs)
    res = run_bass_kernel_spmd(nc, maps, core_ids=list(range(NCORES)))
    outp = np.zeros((4, 2048, D), np.float32)
    for c in range(NCORES):
        b, half = c // 2, c % 2
        outp[b, half * T:(half + 1) * T] = np.asarray(res.results[c]["out"], dtype=np.float32)
    return outp
```

```python
import numpy as np
from contextlib import ExitStack
import concourse.bass as bass
import concourse.mybir as mybir
from concourse.bass_utils import run_bass_kernel_spmd

F32 = mybir.dt.float32
BF16 = mybir.dt.bfloat16
AF = mybir.ActivationFunctionType
ALU = mybir.AluOpType
AX = mybir.AxisListType

NCORES = 8
T = 1024
NT = 8
D = 4096
KC = 32
NIN = 24576
QSCALE = 128 ** -0.5
WK2N = 14336
C_AQ, C_AF, C_AI, C_AG, C_BQ, C_BK, C_BV, C_BG, C_GA, C_GB = 0, 2048, 4096, 6144, 8192, 10240, 12288, 14336, 16384, 20480


class Sched:
    ENG = ("pe", "act", "dve", "pool", "sp")

    def __init__(self):
        self.ops = []
        self.last_writer = {}
        self.readers = {}
        self.dom_pos = {}
        self.pending = {e: set() for e in self.ENG}
        self.last_in_dom = {}

    def op(self, eng, fn, reads=(), writes=(), dma=None, drain=False):
        oid = len(self.ops)
        raw = set()
        oth = set()
        force = set()
        if drain and eng in self.last_in_dom:
            force.add(self.last_in_dom[eng])
        for k in reads:
            w = self.last_writer.get(k)
            if w is not None:
                raw.add(w)
        for k in writes:
            w = self.last_writer.get(k)
            if w is not None:
                oth.add(w)
            for r in self.readers.get(k, ()):
                oth.add(r)
        raw |= self.pending[eng]
        self.pending[eng] = set()
        for k in reads:
            self.readers.setdefault(k, []).append(oid)
        for k in writes:
            self.last_writer[k] = oid
            self.readers[k] = []
        dom = dma if dma else eng
        pos = self.dom_pos.get(dom, 0)
        self.dom_pos[dom] = pos + 1
        self.ops.append(dict(id=oid, eng=eng, fn=fn, dom=dom, pos=pos, raw=raw, oth=oth,
                             signal=False, waits=[], isdma=dma is not None, force=force))
        self.last_in_dom[dom] = oid
        return oid

    def barrier(self):
        s = set(self.last_in_dom.values())
        for e in self.ENG:
            self.pending[e] |= s

    def finalize(self):
        waited = {}
        by_dom = {}
        for o in self.ops:
            by_dom.setdefault(o["dom"], []).append(o)
        for o in self.ops:
            need = {}
            F = o["eng"]
            for d in o["raw"] | o["oth"] | o["force"]:
                dd = self.ops[d]
                E = dd["dom"]
                if (not dd["isdma"]) and E == F and d not in o["force"]:
                    if F in ("pe", "sp"):
                        continue
                if waited.get((F, E), -1) >= dd["pos"]:
                    continue
                need[E] = max(need.get(E, -1), dd["pos"])
            for E, p in need.items():
                by_dom[E][p]["signal"] = True
                waited[(F, E)] = p
                o["waits"].append((E, p))
        for dom, lst in by_dom.items():
            c = 0
            for o in lst:
                if o["signal"]:
                    c += 1
                    o["val"] = c * (16 if o["isdma"] else 1)
        for o in self.ops:
            o["waitvals"] = [(E, by_dom[E][p]["val"]) for (E, p) in o["waits"]]
        self.domains = list(by_dom.keys())

    def emit(self, nc, stack):
        self.finalize()
        sems = {}
        for dom in self.domains:
            sems[dom] = stack.enter_context(nc.semaphore("s_" + dom))
        block = stack.enter_context(nc.Block())
        streams = {e: [o for o in self.ops if o["eng"] == e] for e in self.ENG}

        def run(eh, lst):
            for o in lst:
                for (E, v) in o["waitvals"]:
                    eh.wait_ge(sems[E], v)
                ins = o["fn"](eh) if o["fn"] is not None else None
                if o["signal"]:
                    assert ins is not None
                    ins.then_inc(sems[o["dom"]], 16 if o["isdma"] else 1)

        @block.tensor
        def _(e):
            run(e, streams["pe"])

        @block.scalar
        def _(e):
            run(e, streams["act"])

        @block.vector
        def _(e):
            run(e, streams["dve"])

        @block.gpsimd
        def _(e):
            run(e, streams["pool"])

        @block.sync
        def _(e):
            run(e, streams["sp"])


class Arena:
    def __init__(self, flat, lo, hi):
        self.flat, self.lo, self.hi, self.cur = flat, lo, hi, lo

    def alloc(self, shape, dt):
        esz = 4 if dt == F32 else 2
        n = 1
        for s in shape[1:]:
            n *= s
        nel = n * esz // 2
        nel = (nel + 31) // 32 * 32
        assert self.cur + nel <= self.hi, ("arena overflow", shape, self.cur, self.hi)
        v = self.flat[:, self.cur:self.cur + n * esz // 2]
        self.cur += nel
        if dt == F32:
            v = v.bitcast(F32)
        if len(shape) == 3:
            v = v.rearrange("p (a b) -> p a b", b=shape[2])
        elif len(shape) == 4:
            v = v.rearrange("p (a b c) -> p a b c", b=shape[2], c=shape[3])
        return v


def build_program(debug=False, cfg=None):
    cfg = cfg or {}
    NPAIR = cfg.get("npair", 8)
    NBH = cfg.get("nbh", 8)
    NGT = cfg.get("ngt", 32)
    DO_C = cfg.get("do_c", True)
    HST = cfg.get("hstage", 99)
    nc = bass.Bass("TRN2", target_bir_lowering=False)

    def din(name, shape, dt=F32):
        return nc.dram_tensor(name, list(shape), dt, kind="ExternalInput").ap()

    x_own = din("x_own", [T, D])
    x_prev = din("x_prev", [T, D])
    w_in = din("w_in", [D, NIN])
    w_a = din("w_a", [2048, D])
    w_b = din("w_b", [2048, D])
    w_out = din("w_out", [D, D])
    normwT_d = din("normwT", [128, 32])
    fwB_d = din("fwB", [128, D])
    lbt_d = din("lbt", [128, 4096])
    hnwB_d = din("hnwB", [128, 128])
    sublnB_d = din("sublnB", [128, 256])
    lamv_d = din("lamv", [128, 512])
    identf_d = din("identf", [128, 128])
    TI_d = din("TI", [128, 130])
    maskA_d = din("maskA", [128, 128])
    cmask_d = din("cmask", [128, 128])
    alibi_d = din("alibi", [128, 1024])
    out = nc.dram_tensor("out", [T, D], F32, kind="ExternalOutput").ap()
    sk = "ExternalOutput" if debug else "Internal"
    kprev = nc.dram_tensor("kprev_scr", [8, 128, 2048], BF16, kind="Internal").ap()
    vprev = nc.dram_tensor("vprev_scr", [8, 128, 2048], BF16, kind="Internal").ap()
    oT_scr = nc.dram_tensor("oT_scr", [32, 128, 1024], BF16, kind=sk).ap()
    sig_scr = nc.dram_tensor("sig_scr", [64, 128, 1024], BF16, kind=sk).ap()
    yT_scr = nc.dram_tensor("yT_scr", [32, 128, 1024], BF16, kind=sk).ap()

    S = Sched()
    st = ExitStack()
    with st:
        def sb(name, shape, dt):
            return st.enter_context(nc.sbuf_tensor(name, shape, dt))

        big = sb("big", [128, 65536], BF16)
        wk2 = sb("wk2", [128, WK2N], BF16)
        stg = [sb("stg%d" % i, [128, 8, 256], F32) for i in range(3)]
        identf = sb("identf_s", [128, 128], F32)
        identb = sb("identb_s", [128, 128], BF16)
        TI = sb("TI_s", [128, 130], F32)
        maskA = sb("maskA_s", [128, 128], F32)
        cmask = sb("cmask_s", [128, 128], BF16)
        alibi = sb("alibi_s", [128, 8, 8, 16], F32)
        normwT = sb("normwT_s", [128, 32], F32)
        hnwB = sb("hnwB_s", [128, 128], F32)
        sublnB = sb("sublnB_s", [128, 256], F32)
        lbB = sb("lbB_s", [128, 2048], F32)
        Smid = sb("Smid_s", [128, 16, 128], F32)
        stat = sb("stat_s", [128, 512], F32)
        lam = stat[:, 0:1]
        neglam = stat[:, 1:2]
        eps6 = stat[:, 8:9]
        eps5 = stat[:, 9:10]
        one1 = stat[:, 10:11]
        pb = [st.enter_context(nc.psum_tensor("pb%d" % i, [128, 512], F32)) for i in range(8)]

        R2 = 32768
        uT = big[:, 0:R2].rearrange("p (k t) -> p k t", t=1024)
        W = [big[:, R2 + s * 8192: R2 + (s + 1) * 8192].rearrange("p (k c) -> p k c", c=256) for s in range(4)]
        wide0 = [10 ** 9]
        A1_LO, A1_HI = R2 + 16384, 65536

        cnt = {"acc": 0, "small": 0, "stg": 0, "w": 0, "s4": 0, "since": 0}

        late_casts = []

        def flush_casts(upto=None):
            keep = []
            while late_casts:
                t, fn = late_casts.pop(0)
                if upto is None or t <= upto:
                    fn()
                else:
                    keep.append((t, fn))
            late_casts.extend(keep)

        def proj_tick():
            cnt["since"] += 1
            if cnt["since"] >= 4:
                flush_casts()

        def acc_next():
            i = cnt["acc"] % 4
            cnt["acc"] += 1
            return pb[i], ("ps", i)

        def sbank_next():
            b = 6 + cnt["small"] % 2
            cnt["small"] += 1
            return b, ("ps", b)

        plan = []
        cur = [0]
        deferred = []

        def pump(n=1):
            while n > 0 and deferred:
                deferred.pop(0)()
                n -= 1

        def slot_of(i):
            return i % 2 if i < wide0[0] else (i - wide0[0]) % 4

        loaded = [0]

        def issue_load(i):
            s = slot_of(i)
            for j, seg in enumerate(plan[i]):
                g = cnt["stg"] % 3
                cnt["stg"] += 1
                S.op("sp", (lambda e, g=g, seg=seg: e.dma_start(out=stg[g][:], in_=seg.rearrange("(k p) c -> p k c", p=128))),
                     writes=[("stg", g)], dma="d_stg%d" % g)
                late = True
                ceng = ("pool", "act", "act", "dve")[j] if i >= wide0[0] else ("act", "act", "act", "dve")[j]

                def cast(ceng=ceng, g=g, s=s, j=j):
                    if ceng == "act":
                        S.op("act", (lambda e: e.activation(out=W[s][:, j * 8:(j + 1) * 8, :], in_=stg[g][:], func=AF.Copy)),
                             reads=[("stg", g)], writes=[("w", s, j)])
                    else:
                        S.op(ceng, (lambda e: e.tensor_copy(out=W[s][:, j * 8:(j + 1) * 8, :], in_=stg[g][:])),
                             reads=[("stg", g)], writes=[("w", s, j)])
                if late and j != 0:
                    late_casts.append((i, cast))
                    cnt["since"] = 0
                else:
                    cast()

        def next_tile(check=None):
            i = cur[0]
            cur[0] += 1
            if check is not None:
                assert plan[i][0] is check[0], "tile plan mismatch at %d" % i
            flush_casts()
            la = 1 if i < wide0[0] else 3
            while loaded[0] <= min(i + la, len(plan) - 1):
                flush_casts()
                issue_load(loaded[0])
                loaded[0] += 1
            flush_casts(upto=i)
            return slot_of(i)

        def win_tile(col0):
            return [w_in[j * 1024:(j + 1) * 1024, col0:col0 + 256] for j in range(4)]

        def wkeys(s):
            return [("w", s, j) for j in range(4)]

        def proj_fm(s, c0, half, ps, pkey, src=uT, srckeys=None, k0=0, nk=32, ksrc0=0):
            def fn(e):
                ins = None
                for k in range(nk):
                    ins = e.matmul(ps[:, 0:512], lhsT=W[s][:, k0 + k, c0:c0 + 128],
                                   rhs=src[:, ksrc0 + k, half * 512:(half + 1) * 512],
                                   start=(k == 0), stop=(k == nk - 1))
                return ins
            rk = srckeys if srckeys is not None else [("uT", tt) for tt in range(half * 4, half * 4 + 4)]
            proj_tick()
            S.op("pe", fn, reads=wkeys(s) + rk, writes=[pkey])

        def proj_tm(s, tt, ps, pkey, src=uT, srckeys=None):
            def fn(e):
                ins = None
                for k in range(KC):
                    ins = e.matmul(ps[:, 0:256], lhsT=src[:, k, tt * 128:(tt + 1) * 128], rhs=W[s][:, k, :],
                                   start=(k == 0), stop=(k == KC - 1))
                return ins
            rk = srckeys if srckeys is not None else [("uT", tt)]
            proj_tick()
            S.op("pe", fn, reads=wkeys(s) + rk, writes=[pkey])

        def cload(dst, src, key):
            S.op("sp", (lambda e: e.dma_start(out=dst, in_=src)), writes=[key], dma="d_c_" + key)

        cload(identf[:], identf_d[:, :], "identf")
        cload(TI[:], TI_d[:, :], "TI")
        cload(maskA[:], maskA_d[:, :], "maskA")
        cload(alibi[:], alibi_d.rearrange("p (a b c) -> p a b c", a=8, b=8), "alibi")
        cload(normwT[:], normwT_d[:, :], "normwT")
        cload(hnwB[:], hnwB_d[:, :], "hnwB")
        cload(sublnB[:], sublnB_d[:, :], "sublnB0")
        ar0 = Arena(big, A1_LO, A1_HI)
        lbt = ar0.alloc([128, 2, 2048], F32)
        lamv = ar0.alloc([128, 4, 128], F32)
        cm32 = ar0.alloc([128, 128], F32)
        lamp = ar0.alloc([128, 2, 128], F32)
        cload(lbt, lbt_d.rearrange("p (a b) -> p a b", a=2), "lbt")
        cload(lamv, lamv_d.rearrange("p (a b) -> p a b", a=4), "lamv")
        cload(cm32, cmask_d[:, :], "cm32")
        S.op("pool", lambda e: e.tensor_copy(out=identb[:], in_=identf[:]), reads=["identf"], writes=["identb"])
        S.op("pool", lambda e: e.tensor_copy(out=cmask[:], in_=cm32), reads=["cm32"], writes=["cmask"])
        S.op("pool", lambda e: e.memset(Smid[:], 0.0), writes=["Smid"])
        S.op("pool", lambda e: e.memset(eps6, 1e-6), writes=["eps6"])
        S.op("pool", lambda e: e.memset(one1, 1.0), writes=["one1"])
        S.op("pool", lambda e: e.memset(eps5, 1e-5), writes=["eps"])
        S.op("dve", lambda e: e.tensor_tensor(out=lbt[:, 0, :], in0=lbt[:, 0, :], in1=lbt[:, 1, :], op=ALU.subtract), reads=["lbt"], writes=["lbd"])
        S.op("act", lambda e: e.activation(out=lbB[:], in_=lbt[:, 0, :], func=AF.Sigmoid), reads=["lbd"], writes=["lbB"])
        S.op("dve", lambda e: e.tensor_tensor(out=lamp[:, 0, :], in0=lamv[:, 0, :], in1=lamv[:, 1, :], op=ALU.mult), reads=["lamv"], writes=["lamp0"])
        S.op("dve", lambda e: e.tensor_tensor(out=lamp[:, 1, :], in0=lamv[:, 2, :], in1=lamv[:, 3, :], op=ALU.mult), reads=["lamv"], writes=["lamp1"])
        S.op("dve", lambda e: e.reduce_sum(out=stat[:, 2:3], in_=lamp[:, 0, :], axis=AX.X), reads=["lamp0"], writes=["ls0"])
        S.op("dve", lambda e: e.reduce_sum(out=stat[:, 3:4], in_=lamp[:, 1, :], axis=AX.X), reads=["lamp1"], writes=["ls1"])
        S.op("act", lambda e: e.activation(out=stat[:, 4:6], in_=stat[:, 2:4], func=AF.Exp), reads=["ls0", "ls1"], writes=["le"])
        S.op("dve", lambda e: e.scalar_tensor_tensor(out=lam, in0=stat[:, 4:5], scalar=0.2, in1=stat[:, 5:6], op0=ALU.add, op1=ALU.subtract), reads=["le"], writes=["lam"])
        S.op("dve", lambda e: e.tensor_scalar(out=neglam, in0=lam, scalar1=-1.0, scalar2=None, op0=ALU.mult), reads=["lam"], writes=["neglam"])
        S.op("dve", lambda e: e.tensor_scalar(out=sublnB[:], in0=sublnB[:], scalar1=0.8, scalar2=None, op0=ALU.mult), reads=["sublnB0"], writes=["sublnB"])
        S.barrier()

        def phase_A(xsrc, tag):
            arA = Arena(big, A1_LO, A1_HI)
            arA2 = Arena(wk2, 0, WK2N)
            xs = [arA.alloc([128, D], F32) for _ in range(2)]
            xn = [arA2.alloc([128, D], BF16) for _ in range(2)]
            for tt in range(NT):
                sl = tt % 2
                S.op("sp", (lambda e, sl=sl, tt=tt: e.dma_start(out=xs[sl], in_=xsrc[tt * 128:(tt + 1) * 128, :])),
                     writes=[("xs", sl)], dma="d_xs%d" % sl)
                c0 = 16 + tt * 3
                S.op("act", (lambda e, sl=sl, c0=c0: e.activation(out=xn[sl], in_=xs[sl], func=AF.Square, accum_out=stat[:, c0:c0 + 1])),
                     reads=[("xs", sl)], writes=[("xn", sl), ("ssA", tt)])
                S.op("act", (lambda e, c0=c0: e.activation(out=stat[:, c0 + 1:c0 + 2], in_=stat[:, c0:c0 + 1], func=AF.Sqrt, bias=1e-6, scale=1.0 / D)),
                     reads=[("ssA", tt)], writes=[("sqA", tt)])
                S.op("dve", (lambda e, c0=c0: e.reciprocal(out=stat[:, c0 + 2:c0 + 3], in_=stat[:, c0 + 1:c0 + 2])),
                     reads=[("sqA", tt)], writes=[("rsA", tt)])
                S.op("act", (lambda e, sl=sl, c0=c0: e.activation(out=xn[sl], in_=xs[sl], func=AF.Copy, scale=stat[:, c0 + 2:c0 + 3])),
                     reads=[("xs", sl), ("rsA", tt)], writes=[("xn", sl)])
                for g in range(4):
                    pbf = pb[g][:, :].bitcast(BF16).rearrange("p (a b) -> p a b", b=128)

                    def trf(e, sl=sl, g=g, pbf=pbf):
                        ins = None
                        for i in range(8):
                            c = g * 8 + i
                            ins = e.transpose(out=pbf[:, i, :], in_=xn[sl][:, c * 128:(c + 1) * 128], identity=identb[:])
                        return ins
                    S.op("pe", trf, reads=[("xn", sl), "identb"], writes=[("ps", g)])
                    S.op("dve", (lambda e, g=g, tt=tt, pbf=pbf: e.tensor_tensor(
                        out=uT[:, g * 8:(g + 1) * 8, tt * 128:(tt + 1) * 128], in0=pbf[:, 0:8, :],
                        in1=normwT[:, g * 8:(g + 1) * 8].unsqueeze(2).to_broadcast([128, 8, 128]), op=ALU.mult)),
                        reads=[("ps", g), "normwT"], writes=[("uT", tt)])
            S.barrier()

        def hgrn_pair(hp, own):
            a1 = Arena(big, A1_LO, A1_HI)
            a2 = Arena(wk2, 0, WK2N)
            kte_tok = a1.alloc([128, 8, 256], BF16)
            v_tok = a1.alloc([128, 8, 256], BF16)
            enRT = a1.alloc([128, 2, 1024], BF16)
            kteT = a1.alloc([128, 2, 1024], BF16)
            qeT = a1.alloc([128, 2, 1024], BF16)
            sgT = a1.alloc([128, 2, 1024], BF16)
            Dbf = a1.alloc([128, 2, 16, 128], BF16)
            sg = [a2.alloc([128, 256], F32) for _ in range(2)]
            t1 = a2.alloc([128, 256], F32)
            lf = [a2.alloc([128, 256], F32) for _ in range(2)]
            ktok = [a2.alloc([128, 256], F32) for _ in range(2)]
            eR = a2.alloc([128, 256], F32)
            omlp = a2.alloc([128, 256], F32)
            scTm = [a2.alloc([128, 4, 128], BF16) for _ in range(2)]
            on_tok = [a2.alloc([128, 4, 128], F32) for _ in range(2)]
            oTh = [a2.alloc([128, 1024], BF16) for _ in range(2)]
            dec = a2.alloc([128, 2, 16], F32)
            hst = a2.alloc([128, 64], F32)
            vm = a2.alloc([128, 8, 2, 256], BF16)
            if hp == 0:
                S.op("pool", lambda e: e.memset(vm, 0.0), writes=["vm0"] + [("vm", tt, j) for tt in range(NT) for j in range(2)])
            lbp = lbB[:, hp * 256:(hp + 1) * 256]
            P = "A%d%d" % (hp, int(own))
            S.op("dve", lambda e: e.tensor_scalar(out=omlp, in0=lbp, scalar1=-1.0, scalar2=1.0, op0=ALU.mult, op1=ALU.add),
                 reads=["lbB"], writes=["omlp"])
            sf = next_tile()

            def rstuff(tt):
                b = tt % 2
                psR, pkR = acc_next()
                S.op("pe", (lambda e, psR=psR, b=b: e.matmul(psR[:, 0:256], lhsT=TI[:, 0:128], rhs=lf[b], start=True, stop=True)),
                     reads=[("lf", b), "TI"], writes=[pkR])
                S.op("act", (lambda e, psR=psR: e.activation(out=eR, in_=psR[:, 0:256], func=AF.Exp)),
                     reads=[pkR], writes=["eR"])
                S.op("dve", (lambda e, tt=tt, b=b: e.tensor_tensor(out=kte_tok[:, tt, :], in0=ktok[b], in1=eR, op=ALU.mult)),
                     reads=[("ktok", b), "eR"], writes=[("kte", tt)])
                psT, pkT = acc_next()
                if own:
                    def rtf(e, psT=psT, b=b):
                        e.matmul(psT[:, 0:130], lhsT=lf[b][:, 0:128], rhs=TI[:, 0:130], start=True, stop=True)
                        return e.matmul(psT[:, 256:386], lhsT=lf[b][:, 128:256], rhs=TI[:, 0:130], start=True, stop=True)
                    S.op("pe", rtf, reads=[("lf", b), "TI"], writes=[pkT])
                    for hh in range(2):
                        S.op("act", (lambda e, psT=psT, hh=hh, tt=tt: e.activation(out=enRT[:, hh, tt * 128:(tt + 1) * 128], in_=psT[:, hh * 256:hh * 256 + 128], func=AF.Exp, scale=-1.0)),
                             reads=[pkT], writes=[("enRT", hh, tt)])
                        S.op("act", (lambda e, psT=psT, hh=hh, tt=tt: e.activation(out=dec[:, hh, tt * 2:tt * 2 + 2], in_=psT[:, hh * 256 + 128:hh * 256 + 130], func=AF.Exp)),
                             reads=[pkT], writes=[("dec", hh, tt)])
                else:
                    def rtf(e, psT=psT, b=b):
                        e.matmul(psT[:, 0:2], lhsT=lf[b][:, 0:128], rhs=TI[:, 128:130], start=True, stop=True)
                        return e.matmul(psT[:, 256:258], lhsT=lf[b][:, 128:256], rhs=TI[:, 128:130], start=True, stop=True)
                    S.op("pe", rtf, reads=[("lf", b), "TI"], writes=[pkT])
                    for hh in range(2):
                        S.op("act", (lambda e, psT=psT, hh=hh, tt=tt: e.activation(out=dec[:, hh, tt * 2:tt * 2 + 2], in_=psT[:, hh * 256:hh * 256 + 2], func=AF.Exp)),
                             reads=[pkT], writes=[("dec", hh, tt)])

            for tt in range(NT):
                ps, pk = acc_next()
                proj_tm(sf, tt, ps, pk)
                b = tt % 2
                S.op("act", (lambda e, ps=ps, b=b: e.activation(out=sg[b], in_=ps[:, 0:256], func=AF.Exp, scale=-1.0)),
                     reads=[pk], writes=[("sg", b)])
                S.op("act", (lambda e, b=b: e.activation(out=sg[b], in_=sg[b], func=AF.Ln, bias=one1[:, 0:1], scale=1.0)),
                     reads=[("sg", b), "eps"], writes=[("sg", b)])
                S.op("act", (lambda e, b=b: e.activation(out=sg[b], in_=sg[b], func=AF.Exp, scale=-1.0)),
                     reads=[("sg", b)], writes=[("sg", b)])
                S.op("dve", (lambda e, b=b: e.tensor_tensor(out=t1, in0=sg[b], in1=omlp, op=ALU.mult)),
                     reads=[("sg", b), "omlp"], writes=["t1"])
                S.op("dve", (lambda e, b=b: e.tensor_tensor(out=sg[b], in0=t1, in1=lbp, op=ALU.add)),
                     reads=["t1", "lbB"], writes=[("sg", b)])
                S.op("dve", (lambda e, b=b: e.tensor_tensor(out=ktok[b], in0=omlp, in1=t1, op=ALU.subtract)),
                     reads=["t1", "omlp"], writes=[("ktok", b)])
                S.op("act", (lambda e, b=b: e.activation(out=lf[b], in_=sg[b], func=AF.Ln)),
                     reads=[("sg", b)], writes=[("lf", b)])
                pump(2 if tt < 4 else 1)
                if tt >= 1:
                    rstuff(tt - 1)
            pump(99)
            rstuff(NT - 1)
            if HST < 1:
                return

            def scan_group(tp):
                for hh in range(2):
                    h = hp * 2 + hh
                    b, bk = sbank_next()
                    pv = pb[b][:, 0:512].rearrange("p (t j v) -> p t j v", t=2, j=2)

                    def csf(e, pv=pv, tp=tp, hh=hh):
                        ins = None
                        for t2 in range(2):
                            tt = tp * 2 + t2
                            ins = e.matmul(pv[:, t2, :, :], lhsT=kte_tok[:, tt, hh * 128:(hh + 1) * 128],
                                           rhs=vm[:, tt, :, hh * 128:(hh + 1) * 128], start=True, stop=True)
                        return ins
                    S.op("pe", csf, reads=[("kte", tp * 2), ("kte", tp * 2 + 1)] + [("vm", tp * 2 + t2, j) for t2 in range(2) for j in range(2)], writes=[bk])
                    for t2 in range(2):
                        for j in range(2):
                            tt = tp * 2 + t2
                            n = tt * 2 + j
                            psc = pv[:, t2, j, :]
                            if own:
                                S.op("dve", (lambda e, h=h, hh=hh, n=n: e.tensor_scalar(out=Dbf[:, hh, n, :], in0=Smid[:, h, :], scalar1=dec[:, hh, n:n + 1], scalar2=None, op0=ALU.mult)),
                                     reads=[("Smid", h), ("dec", hh, tt)], writes=[("Dbf", hh, n)])
                            S.op("dve", (lambda e, h=h, hh=hh, n=n, psc=psc: e.scalar_tensor_tensor(out=Smid[:, h, :], in0=Smid[:, h, :], scalar=dec[:, hh, n:n + 1], in1=psc, op0=ALU.mult, op1=ALU.add)),
                                 reads=[("Smid", h), "Smid", ("dec", hh, tt), bk], writes=[("Smid", h)])

            def kte_transposes():
                if not own:
                    return
                for hh in range(2):
                    b, bk = sbank_next()
                    pvb = pb[b][:, :].bitcast(BF16).rearrange("p (a c) -> p a c", c=128)

                    def ktf(e, pvb=pvb, hh=hh):
                        ins = None
                        for tt in range(NT):
                            ins = e.transpose(out=pvb[:, tt, :], in_=kte_tok[:, tt, hh * 128:(hh + 1) * 128], identity=identb[:])
                        return ins
                    S.op("pe", ktf, reads=[("kte", tt) for tt in range(NT)] + ["identb"], writes=[bk])
                    S.op("act", (lambda e, pvb=pvb, hh=hh: e.activation(out=kteT[:, hh, :].rearrange("p (a c) -> p a c", c=128), in_=pvb[:, 0:8, :], func=AF.Copy)),
                         reads=[bk], writes=[("kteT", hh)])
            si = next_tile()
            for tt in range(NT):
                ps, pk = acc_next()
                proj_tm(si, tt, ps, pk)
                S.op("act", (lambda e, ps=ps, tt=tt: e.activation(out=v_tok[:, tt, :], in_=ps[:, 0:256], func=AF.Copy)),
                     reads=[pk], writes=[("vtok", tt)])
                for j in range(2):
                    S.op("act", (lambda e, ps=ps, tt=tt, j=j: e.activation(out=vm[j * 64:(j + 1) * 64, tt, j, :], in_=ps[j * 64:(j + 1) * 64, 0:256], func=AF.Copy)),
                         reads=[pk, "vm0"], writes=[("vm", tt, j)])
                if tt == 1:
                    kte_transposes()
                if tt >= 2 and tt % 2 == 0:
                    scan_group(tt // 2 - 1)
            if not own:
                scan_group(3)
                return
            if HST < 4:
                return
            sq = next_tile()
            qg = 0
            for hh in range(2):
                for half in range(2):
                    ps, pk = acc_next()
                    proj_fm(sq, hh * 128, half, ps, pk)
                    S.op("dve", (lambda e, ps=ps, hh=hh, half=half: e.tensor_tensor(out=qeT[:, hh, half * 512:(half + 1) * 512], in0=ps[:, 0:512], in1=enRT[:, hh, half * 512:(half + 1) * 512], op=ALU.mult)),
                         reads=[pk] + [("enRT", hh, t_) for t_ in range(half * 4, half * 4 + 4)], writes=[("qeT", hh, half)])
                    if qg == 0:
                        scan_group(3)
                    qg += 1
            sgw = next_tile()
            for hh in range(2):
                for half in range(2):
                    ps, pk = acc_next()
                    proj_fm(sgw, hh * 128, half, ps, pk)
                    S.op("act", (lambda e, ps=ps, hh=hh, half=half: e.activation(out=sgT[:, hh, half * 512:(half + 1) * 512], in_=ps[:, 0:512], func=AF.Silu)),
                         reads=[pk], writes=[("sgT", hh, half)])
            gi = 0
            links = []
            for hh in range(2):
                h = hp * 2 + hh
                ob = hh % 2
                for half in range(2):
                    g = gi % 2
                    gi += 1
                    bs = 6 + g
                    bo = 4 + g
                    tts = list(range(half * 4, half * 4 + 4))
                    pS = pb[bs][:, :].rearrange("p (a c) -> p a c", c=128)
                    pO = pb[bo][:, :].rearrange("p (a c) -> p a c", c=128)

                    def link1(pS=pS, hh=hh, tts=tts, g=g, bs=bs, half=half):
                        def scf(e):
                            ins = None
                            for i, tt in enumerate(tts):
                                ins = e.matmul(pS[:, i, :], lhsT=kteT[:, hh, tt * 128:(tt + 1) * 128], rhs=qeT[:, hh, tt * 128:(tt + 1) * 128], start=True, stop=True)
                            return ins
                        S.op("pe", scf, reads=[("kteT", hh), ("qeT", hh, half)], writes=[("ps", bs)])
                        S.op("dve", (lambda e: e.tensor_tensor(out=scTm[g], in0=pS[:, 0:4, :], in1=maskA[:].unsqueeze(1).to_broadcast([128, 4, 128]), op=ALU.mult)),
                             reads=[("ps", bs), "maskA"], writes=[("scTm", g)])

                    def link2(pO=pO, hh=hh, tts=tts, g=g, bo=bo, half=half):
                        def omm(e):
                            ins = None
                            for i, tt in enumerate(tts):
                                e.matmul(pO[:, i, :], lhsT=scTm[g][:, i, :], rhs=v_tok[:, tt, hh * 128:(hh + 1) * 128], start=True, stop=False)
                                e.matmul(pO[0:64, i, :], lhsT=qeT[:, hh, tt * 128:tt * 128 + 64], rhs=Dbf[:, hh, 2 * tt, :], start=False, stop=True)
                                ins = e.matmul(pO[64:128, i, :], lhsT=qeT[:, hh, tt * 128 + 64:tt * 128 + 128], rhs=Dbf[:, hh, 2 * tt + 1, :], start=False, stop=True)
                            return ins
                        S.op("pe", omm, reads=[("scTm", g), ("qeT", hh, half)] + [("vtok", tt) for tt in tts] + [("Dbf", hh, n) for n in range(half * 8, half * 8 + 8)], writes=[("ps", bo)])
                        c0 = g * 16
                        for i in range(4):
                            S.op("act", (lambda e, i=i: e.activation(out=on_tok[g][:, i, :], in_=pO[:, i, :], func=AF.Square, accum_out=hst[:, c0 + i:c0 + i + 1])),
                                 reads=[("ps", bo)], writes=[("on", g), ("hs0", g, i)])
                        S.op("act", (lambda e: e.activation(out=hst[:, c0 + 4:c0 + 8], in_=hst[:, c0:c0 + 4], func=AF.Ln, bias=eps6[:, 0:1], scale=1.0 / 128)),
                             reads=[("hs0", g, i) for i in range(4)] + ["eps"], writes=[("hs1", g)])
                        S.op("act", (lambda e: e.activation(out=hst[:, c0 + 8:c0 + 12], in_=hst[:, c0 + 4:c0 + 8], func=AF.Exp, scale=-0.5)),
                             reads=[("hs1", g)], writes=[("hs2", g)])
                        for i in range(4):
                            S.op("dve", (lambda e, i=i: e.scalar_tensor_tensor(out=on_tok[g][:, i, :], in0=pO[:, i, :], scalar=hst[:, c0 + 8 + i:c0 + 9 + i], in1=hnwB[:], op0=ALU.mult, op1=ALU.mult)),
                                 reads=[("ps", bo), ("hs2", g), ("on", g), "hnwB"], writes=[("on", g)])

                    def link3(pS=pS, hh=hh, g=g, bs=bs, half=half, ob=ob, h=h):
                        def otf(e):
                            ins = None
                            for i in range(4):
                                ins = e.transpose(out=pS[:, i, :], in_=on_tok[g][:, i, :], identity=identf[:])
                            return ins
                        S.op("pe", otf, reads=[("on", g), "identf"], writes=[("ps", bs)])
                        S.op("dve", (lambda e: e.tensor_tensor(out=oTh[ob][:, half * 512:(half + 1) * 512], in0=pS[:, 0:4, :].rearrange("p a c -> p (a c)"), in1=sgT[:, hh, half * 512:(half + 1) * 512], op=ALU.mult)),
                             reads=[("ps", bs), ("sgT", hh, half)], writes=[("oTh", ob)])
                        if half == 1:
                            S.op("sp", (lambda e: e.dma_start(out=oT_scr[h], in_=oTh[ob])), reads=[("oTh", ob)], writes=[("oTs", h)], dma="d_oTh%d" % ob)
                    links.append((link1, link2, link3))
            order = [(0, 0), (0, 1), (1, 0), (1, 1), (0, 2), (2, 0), (1, 2), (2, 1), (0, 3), (3, 0), (1, 3), (2, 2), (3, 1), (3, 2)]
            L = links
            seq = [L[0][0], L[0][1], L[1][0], L[1][1], L[0][2], L[2][0], L[2][1], L[1][2], L[3][0], L[3][1], L[2][2], L[3][2]]
            deferred.extend(seq)


        def attn_prev(h):
            a1 = Arena(big, A1_LO, A1_HI)
            kTp = a1.alloc([128, 2, 1024], BF16)
            Vp = a1.alloc([128, 8, 256], BF16)
            skw = next_tile()
            for m in range(2):
                for half in range(2):
                    ps, pk = acc_next()
                    proj_fm(skw, m * 128, half, ps, pk)
                    S.op("act", (lambda e, ps=ps, m=m, half=half: e.activation(out=kTp[:, m, half * 512:(half + 1) * 512], in_=ps[:, 0:512], func=AF.Copy)),
                         reads=[pk], writes=[("kTp", m, half)])
            S.op("sp", (lambda e: e.dma_start(out=kprev[h], in_=kTp.rearrange("p a b -> p (a b)"))),
                 reads=[("kTp", m, half) for m in range(2) for half in range(2)], writes=[("kprev", h)], dma="d_kTp")
            svw = next_tile()
            for tt in range(NT):
                ps, pk = acc_next()
                proj_tm(svw, tt, ps, pk)
                S.op("dve", (lambda e, ps=ps, tt=tt: e.tensor_copy(out=Vp[:, tt, :], in_=ps[:, 0:256])),
                     reads=[pk], writes=[("Vp", tt)])
            S.op("sp", (lambda e: e.dma_start(out=vprev[h], in_=Vp.rearrange("p a b -> p (a b)"))),
                 reads=[("Vp", tt) for tt in range(NT)], writes=[("vprev", h)], dma="d_Vp")

        def attn_own(h):
            a1 = Arena(big, A1_LO, A1_HI)
            a2 = Arena(wk2, 0, WK2N)
            kT = a1.alloc([128, 2, 2048], BF16)
            V = a1.alloc([128, 16, 264], BF16)
            qT = a1.alloc([128, 2, 1024], BF16)
            sgT = a1.alloc([128, 2, 1024], BF16)
            PT = [a1.alloc([128, 128], BF16) for _ in range(16)]
            ob_ = [a2.alloc([128, 256], F32) for _ in range(2)]
            tb_ = [a2.alloc([128, 256], F32) for _ in range(2)]
            sqj = a2.alloc([128, 256], BF16)
            oTh = [a2.alloc([128, 1024], BF16) for _ in range(2)]
            bst = a2.alloc([128, 64], F32)
            S.op("pool", lambda e: e.memset(V[:, :, 256:257], 1.0), writes=["Vones"])
            S.op("sp", (lambda e: e.dma_start(out=kT[:, :, 0:1024], in_=kprev[h].rearrange("p (a b) -> p a b", a=2))),
                 reads=[("kprev", h)], writes=["kTprev"], dma="d_kTl")
            S.op("sp", (lambda e: e.dma_start(out=V[:, 0:8, 0:256], in_=vprev[h].rearrange("p (a b) -> p a b", a=8))),
                 reads=[("vprev", h)], writes=["Vprev"], dma="d_Vl")
            skw = next_tile()
            for m in range(2):
                for half in range(2):
                    ps, pk = acc_next()
                    proj_fm(skw, m * 128, half, ps, pk)
                    S.op("act", (lambda e, ps=ps, m=m, half=half: e.activation(out=kT[:, m, 1024 + half * 512:1024 + (half + 1) * 512], in_=ps[:, 0:512], func=AF.Copy)),
                         reads=[pk], writes=[("kT", m, half)])
                    pump(1)
            svw = next_tile()
            for tt in range(NT):
                ps, pk = acc_next()
                proj_tm(svw, tt, ps, pk)
                S.op("dve", (lambda e, ps=ps, tt=tt: e.tensor_copy(out=V[:, 8 + tt, 0:256], in_=ps[:, 0:256])),
                     reads=[pk], writes=[("V", tt)])
            sqw = next_tile()
            for m in range(2):
                for half in range(2):
                    ps, pk = acc_next()
                    proj_fm(sqw, m * 128, half, ps, pk)
                    S.op("act", (lambda e, ps=ps, m=m, half=half: e.activation(out=qT[:, m, half * 512:(half + 1) * 512], in_=ps[:, 0:512], func=AF.Copy)),
                         reads=[pk], writes=[("qT", m, half)])
            sgw = next_tile()
            for j in range(2):
                for half in range(2):
                    ps, pk = acc_next()
                    proj_fm(sgw, j * 128, half, ps, pk)
                    S.op("act", (lambda e, ps=ps, j=j, half=half: e.activation(out=sgT[:, j, half * 512:(half + 1) * 512], in_=ps[:, 0:512], func=AF.Silu)),
                         reads=[pk], writes=[("sgT", j, half)])
            ptc = [0]
            for qb in range(NT):
                qhalf = qb // 4
                nkb = 9 + qb

                def kkeys(m, kbs):
                    ks = set()
                    for kb in kbs:
                        ks.add("kTprev" if kb < 8 else ("kT", m, (kb - 8) // 4))
                    return list(ks)

                def vkeys(kbs):
                    ks = {"Vones"}
                    for kb in kbs:
                        ks.add("Vprev" if kb < 8 else ("V", kb - 8))
                    return list(ks)

                groups = []
                for g0 in range(0, nkb, 4):
                    for m in range(2):
                        groups.append((m, list(range(g0, min(g0 + 4, nkb)))))

                oa = (4, 5) if qb % 2 == 0 else (2, 3)

                def issue_s(m, kbs, qb=qb, nkb=nkb):
                    b = (6, 7, 0, 1)[cnt["s4"] % 4]
                    cnt["s4"] += 1
                    bk = ("ps", b)
                    pS = pb[b][:, :].rearrange("p (a c) -> p a c", c=128)

                    def fn(e, pS=pS, m=m, kbs=kbs):
                        ins = None
                        for i, kb in enumerate(kbs):
                            diag = (kb == nkb - 1)
                            ins = e.matmul(pS[:, i, :], lhsT=kT[:, m, kb * 128:(kb + 1) * 128], rhs=qT[:, m, qb * 128:(qb + 1) * 128], start=True, stop=not diag)
                            if diag:
                                ins = e.matmul(pS[:, i, :], lhsT=identb[:], rhs=cmask[:], start=False, stop=True)
                        return ins
                    S.op("pe", fn, reads=kkeys(m, kbs) + [("qT", m, qhalf), "identb", "cmask"], writes=[bk])
                    slots = []
                    for i, kb in enumerate(kbs):
                        pi = ptc[0] % 16
                        ptc[0] += 1
                        S.op("act", (lambda e, pS=pS, i=i, pi=pi, kb=kb: e.activation(out=PT[pi], in_=pS[:, i, :], func=AF.Exp, bias=alibi[:, h, qb, kb:kb + 1], scale=QSCALE)),
                             reads=[bk, "alibi"], writes=[("PT", pi)])
                        slots.append(pi)
                    return slots

                def issue_av(m, kbs, slots, nkb=nkb, oa=oa):
                    po = pb[oa[m]]

                    def fn(e, po=po, kbs=kbs, slots=slots):
                        ins = None
                        for kb, pi in zip(kbs, slots):
                            ins = e.matmul(po[:, 0:257], lhsT=PT[pi], rhs=V[:, kb, 0:257], start=(kb == 0), stop=(kb == nkb - 1))
                        return ins
                    S.op("pe", fn, reads=[("PT", pi) for pi in slots] + vkeys(kbs), writes=[("ps", oa[m])])

                pend = []
                for gi_, (m, kbs) in enumerate(groups):
                    slots = issue_s(m, kbs)
                    pend.append((m, kbs, slots))
                    if gi_ == 1:
                        pump(1)
                    if len(pend) > 2:
                        issue_av(*pend.pop(0))
                while pend:
                    issue_av(*pend.pop(0))

                ib = qb % 2
                o0, o1 = oa
                S.op("dve", (lambda e, o0=o0: e.reciprocal(out=bst[:, 0:1], in_=pb[o0][:, 256:257])), reads=[("ps", o0)], writes=["r0"])
                S.op("dve", (lambda e, o1=o1: e.reciprocal(out=bst[:, 1:2], in_=pb[o1][:, 256:257])), reads=[("ps", o1)], writes=["r1"])
                S.op("dve", (lambda e: e.tensor_tensor(out=bst[:, 2:3], in0=bst[:, 1:2], in1=neglam, op=ALU.mult)), reads=["r1", "neglam"], writes=["r1l"])
                S.op("dve", (lambda e, ib=ib, o1=o1: e.tensor_scalar(out=tb_[ib], in0=pb[o1][:, 0:256], scalar1=bst[:, 2:3], scalar2=None, op0=ALU.mult)),
                     reads=[("ps", o1), "r1l"], writes=[("tb", ib)])
                S.op("dve", (lambda e, ib=ib, o0=o0: e.scalar_tensor_tensor(out=ob_[ib], in0=pb[o0][:, 0:256], scalar=bst[:, 0:1], in1=tb_[ib], op0=ALU.mult, op1=ALU.add)),
                     reads=[("ps", o0), "r0", ("tb", ib)], writes=[("ob", ib)])
                c0 = 8 + ib * 4
                S.op("act", (lambda e, ib=ib, c0=c0: e.activation(out=sqj, in_=ob_[ib], func=AF.Square, accum_out=bst[:, c0:c0 + 1])),
                     reads=[("ob", ib)], writes=["sqj", ("bs0", ib)])
                S.op("act", (lambda e, c0=c0: e.activation(out=bst[:, c0 + 1:c0 + 2], in_=bst[:, c0:c0 + 1], func=AF.Ln, bias=eps5[:, 0:1], scale=1.0 / 256)),
                     reads=[("bs0", ib), "eps"], writes=[("bs1", ib)])
                S.op("act", (lambda e, c0=c0: e.activation(out=bst[:, c0 + 2:c0 + 3], in_=bst[:, c0 + 1:c0 + 2], func=AF.Exp, scale=-0.5)),
                     reads=[("bs1", ib)], writes=[("bs2", ib)])
                S.op("dve", (lambda e, ib=ib, c0=c0: e.scalar_tensor_tensor(out=ob_[ib], in0=ob_[ib], scalar=bst[:, c0 + 2:c0 + 3], in1=sublnB[:], op0=ALU.mult, op1=ALU.mult)),
                     reads=[("ob", ib), ("bs2", ib), "sublnB"], writes=[("ob", ib)])

                def fin_pe(ib=ib, qb=qb, qhalf=qhalf):
                    bT = (6, 7, 0, 1)[cnt["s4"] % 4]
                    cnt["s4"] += 1
                    bTk = ("ps", bT)

                    def btf(e):
                        ins = None
                        for j in range(2):
                            ins = e.transpose(out=pb[bT][:, j * 128:(j + 1) * 128], in_=ob_[ib][:, j * 128:(j + 1) * 128], identity=identf[:])
                        return ins
                    S.op("pe", btf, reads=[("ob", ib), "identf"], writes=[bTk])
                    for j in range(2):
                        S.op("dve", (lambda e, j=j: e.tensor_tensor(out=oTh[j][:, qb * 128:(qb + 1) * 128], in0=pb[bT][:, j * 128:(j + 1) * 128], in1=sgT[:, j, qb * 128:(qb + 1) * 128], op=ALU.mult)),
                             reads=[bTk, ("sgT", j, qhalf)], writes=[("oThB", j)])
                    if qb == NT - 1:
                        for j in range(2):
                            S.op("sp", (lambda e, j=j: e.dma_start(out=oT_scr[16 + 2 * h + j], in_=oTh[j])), reads=[("oThB", j)], writes=[("oTs", 16 + 2 * h + j)], dma="d_oThB%d" % j)
                deferred.append(fin_pe)

        def gates():
            a2 = Arena(wk2, 0, WK2N)
            sigt = [a2.alloc([128, 1024], BF16) for _ in range(4)]
            gc = 0
            for gt in range(NGT):
                sw = next_tile()
                for j in range(2):
                    b = gc % 4
                    gc += 1
                    for half in range(2):
                        ps, pk = acc_next()
                        proj_fm(sw, j * 128, half, ps, pk)
                        S.op("act", (lambda e, ps=ps, b=b, half=half: e.activation(out=sigt[b][:, half * 512:(half + 1) * 512], in_=ps[:, 0:512], func=AF.Sigmoid)),
                             reads=[pk], writes=[("sigt", b, half)])
                    S.op("act", (lambda e, b=b, gt=gt, j=j: e.dma_start(out=sig_scr[gt * 2 + j], in_=sigt[b])),
                         reads=[("sigt", b, 0), ("sigt", b, 1)], writes=[("sigs", gt * 2 + j)], dma="d_sig%d" % b)

        def phase_C():
            oT = uT
            for q4 in range(4):
                S.op("sp", (lambda e, q4=q4: e.dma_start(out=oT[:, q4 * 8:(q4 + 1) * 8, :], in_=oT_scr[q4 * 8:(q4 + 1) * 8].rearrange("c p t -> p c t"))),
                     reads=[("oTs", c) for c in range(q4 * 8, q4 * 8 + 8)], writes=[("oT", q4)], dma="d_oTl%d" % q4)
            a2 = Arena(wk2, 0, WK2N)
            sA = [a2.alloc([128, 1024], BF16) for _ in range(4)]
            sB = [a2.alloc([128, 1024], BF16) for _ in range(4)]

            def sig_load(ft):
                for j in range(2):
                    fc = ft * 2 + j
                    b = fc % 4
                    S.op("act", (lambda e, b=b, fc=fc: e.dma_start(out=sA[b], in_=sig_scr[fc])), reads=[("sigs", fc)], writes=[("sA", b)], dma="d_sA%d" % b)
                    S.op("act", (lambda e, b=b, fc=fc: e.dma_start(out=sB[b], in_=sig_scr[32 + fc])), reads=[("sigs", 32 + fc)], writes=[("sB", b)], dma="d_sB%d" % b)
            sig_load(0)
            tA = [a2.alloc([128, 512], F32) for _ in range(2)]
            tB = [a2.alloc([128, 512], F32) for _ in range(2)]
            yst = [a2.alloc([128, 1024], BF16) for _ in range(2)]
            fcn = 0
            for ft in range(16):
                sw = next_tile()
                if ft + 1 < 16:
                    sig_load(ft + 1)
                for j in range(2):
                    fc = ft * 2 + j
                    b = fc % 4
                    yb = fc % 2
                    for half in range(2):
                        psA, pkA = acc_next()
                        proj_fm(sw, j * 128, half, psA, pkA, src=oT, srckeys=[("oT", 0), ("oT", 1)], k0=0, nk=16, ksrc0=0)
                        psB, pkB = acc_next()
                        proj_fm(sw, j * 128, half, psB, pkB, src=oT, srckeys=[("oT", 2), ("oT", 3)], k0=16, nk=16, ksrc0=16)
                        tb = half
                        S.op("dve", (lambda e, psA=psA, b=b, half=half, tb=tb: e.tensor_tensor(out=tA[tb], in0=psA[:, 0:512], in1=sA[b][:, half * 512:(half + 1) * 512], op=ALU.mult)),
                             reads=[pkA, ("sA", b)], writes=[("tA", tb)])
                        S.op("dve", (lambda e, psB=psB, b=b, half=half, tb=tb: e.tensor_tensor(out=tB[tb], in0=psB[:, 0:512], in1=sB[b][:, half * 512:(half + 1) * 512], op=ALU.mult)),
                             reads=[pkB, ("sB", b)], writes=[("tB", tb)])
                        S.op("dve", (lambda e, yb=yb, half=half, tb=tb: e.tensor_tensor(out=yst[yb][:, half * 512:(half + 1) * 512], in0=tA[tb], in1=tB[tb], op=ALU.add)),
                             reads=[("tA", tb), ("tB", tb)], writes=[("yst", yb, half)])
                    S.op("act", (lambda e, yb=yb, fc=fc: e.dma_start(out=yT_scr[fc], in_=yst[yb])),
                         reads=[("yst", yb, 0), ("yst", yb, 1)], writes=[("yTs", fc)], dma="d_yst%d" % yb)
            S.barrier()
            yT = uT
            for q4 in range(4):
                S.op("sp", (lambda e, q4=q4: e.dma_start(out=yT[:, q4 * 8:(q4 + 1) * 8, :], in_=yT_scr[q4 * 8:(q4 + 1) * 8].rearrange("c p t -> p c t"))),
                     reads=[("yTs", c) for c in range(q4 * 8, q4 * 8 + 8)], writes=[("yT", q4)], dma="d_yTl%d" % q4)
            a2 = Arena(wk2, 0, WK2N)
            hsb = [a2.alloc([128, 256], F32) for _ in range(4)]
            xr = [a2.alloc([128, 256], F32) for _ in range(4)]
            sqj = a2.alloc([128, 256], BF16)
            ssq = a2.alloc([128, 8, 16], F32)
            fst = a2.alloc([128, 32], F32)
            rc = 0
            for cb in range(16):
                sw = next_tile()
                for tt in range(NT):
                    r = rc % 4
                    rc += 1
                    S.op("act", (lambda e, r=r, tt=tt, cb=cb: e.dma_start(out=xr[r], in_=x_own[tt * 128:(tt + 1) * 128, cb * 256:(cb + 1) * 256])),
                         writes=[("xr", r)], dma="d_xr%d" % r)
                    ps, pk = acc_next()
                    proj_tm(sw, tt, ps, pk, src=yT, srckeys=[("yT", q4) for q4 in range(4)])
                    S.op("dve", (lambda e, ps=ps, r=r: e.tensor_tensor(out=hsb[r], in0=ps[:, 0:256], in1=xr[r], op=ALU.add)),
                         reads=[pk, ("xr", r)], writes=[("hsb", r)])
                    S.op("act", (lambda e, r=r, tt=tt, cb=cb: e.activation(out=sqj, in_=hsb[r], func=AF.Square, accum_out=ssq[:, tt, cb:cb + 1])),
                         reads=[("hsb", r)], writes=["sqjC", ("ssq", tt, cb)])
                    S.op("act", (lambda e, r=r, tt=tt, cb=cb: e.dma_start(out=out[tt * 128:(tt + 1) * 128, cb * 256:(cb + 1) * 256], in_=hsb[r])),
                         reads=[("hsb", r)], writes=[("outh", tt, cb)], dma="d_hs%d" % r)
            S.barrier()
            arF = Arena(big, 0, R2)
            fwB = arF.alloc([128, D], F32)
            hrow = [arF.alloc([128, D], F32) for _ in range(2)]
            S.op("sp", (lambda e: e.dma_start(out=fwB, in_=fwB_d[:, :])), writes=["fwB"], dma="d_fwB")
            for tt in range(NT):
                b = tt % 2
                S.op("sp", (lambda e, b=b, tt=tt: e.dma_start(out=hrow[b], in_=out[tt * 128:(tt + 1) * 128, :])),
                     reads=[("outh", tt, cb) for cb in range(16)], writes=[("hrow", b)], dma="d_hrl%d" % b)
                S.op("dve", (lambda e, tt=tt: e.reduce_sum(out=fst[:, tt * 3:tt * 3 + 1], in_=ssq[:, tt, :], axis=AX.X)),
                     reads=[("ssq", tt, cb) for cb in range(16)], writes=[("f0", tt)])
                S.op("act", (lambda e, tt=tt: e.activation(out=fst[:, tt * 3 + 1:tt * 3 + 2], in_=fst[:, tt * 3:tt * 3 + 1], func=AF.Sqrt, bias=1e-6, scale=1.0 / D)),
                     reads=[("f0", tt)], writes=[("f1", tt)])
                S.op("dve", (lambda e, tt=tt: e.reciprocal(out=fst[:, tt * 3 + 2:tt * 3 + 3], in_=fst[:, tt * 3 + 1:tt * 3 + 2])),
                     reads=[("f1", tt)], writes=[("f2", tt)])
                S.op("act", (lambda e, b=b, tt=tt: e.activation(out=hrow[b], in_=hrow[b], func=AF.Copy, scale=fst[:, tt * 3 + 2:tt * 3 + 3])),
                     reads=[("hrow", b), ("f2", tt)], writes=[("hrowB", b)])
                S.op("dve", (lambda e, b=b, tt=tt: e.tensor_tensor(out=hrow[b], in0=hrow[b], in1=fwB, op=ALU.mult)),
                     reads=[("hrowB", b), "fwB"], writes=[("hrowA", b)])
                S.op("sp", (lambda e, b=b, tt=tt: e.dma_start(out=out[tt * 128:(tt + 1) * 128, :], in_=hrow[b])),
                     reads=[("hrowA", b), ("hrowB", b)], writes=[("outf", tt), ("hrow", b)], dma="d_hrs%d" % b)
            S.op("sp", None, reads=[("outf", tt) for tt in range(NT)])

        for hp in range(NPAIR):
            plan.append(win_tile(C_AF + hp * 256))
            plan.append(win_tile(C_AI + hp * 256))
        for h in range(NBH):
            plan.append(win_tile(C_BK + h * 256))
            plan.append(win_tile(C_BV + h * 256))
        for hp in range(NPAIR):
            for c in (C_AF, C_AI, C_AQ, C_AG):
                plan.append(win_tile(c + hp * 256))
        for h in range(NBH):
            for c in (C_BK, C_BV, C_BQ, C_BG):
                plan.append(win_tile(c + h * 256))
        wide0[0] = len(plan)
        for gt in range(NGT):
            plan.append(win_tile(C_GA + gt * 256))
        if DO_C:
            for ft in range(16):
                plan.append([w_a[0:1024, ft * 256:(ft + 1) * 256], w_a[1024:2048, ft * 256:(ft + 1) * 256],
                             w_b[0:1024, ft * 256:(ft + 1) * 256], w_b[1024:2048, ft * 256:(ft + 1) * 256]])
            for cb in range(16):
                plan.append([w_out[j * 1024:(j + 1) * 1024, cb * 256:(cb + 1) * 256] for j in range(4)])

        phase_A(x_prev, "p")
        for hp in range(NPAIR):
            hgrn_pair(hp, own=False)
        S.barrier()
        for h in range(NBH):
            attn_prev(h)
        S.barrier()
        phase_A(x_own, "o")
        for hp in range(NPAIR):
            hgrn_pair(hp, own=True)
        pump(999)
        S.barrier()
        for h in range(NBH):
            attn_own(h)
        pump(999)
        S.barrier()
        gates()
        S.barrier()
        if DO_C:
            phase_C()
        else:
            S.op("sp", None, reads=[("oTs", c) for c in range(32)] + [("sigs", c) for c in range(64)])
        assert HST < 99 or cur[0] == len(plan), (cur[0], len(plan))
        S.emit(nc, st)
    return nc


def host_consts(half):
    p = np.arange(128)
    identf = np.eye(128, dtype=np.float32)
    TI = np.zeros((128, 130), np.float32)
    s = p[:, None]
    t = p[None, :]
    TI[:, :128] = ((s // 64 == t // 64) & (s > t)).astype(np.float32)
    TI[:, 128] = (p < 64)
    TI[:, 129] = (p >= 64)
    maskA = ((s // 64 == t // 64) & (s <= t)).astype(np.float32)
    cmask = np.where(s <= t, 0.0, -30000.0).astype(np.float32)
    slopes = np.exp2(-(np.arange(8, dtype=np.float64) + 1.0))
    al = np.zeros((128, 8, 8, 16), np.float32)
    for qb in range(8):
        gq = half * 8 + qb
        for kb in range(16):
            gk = kb if half == 1 else kb - 8
            if gk < 0:
                al[:, :, qb, kb] = -30000.0
            else:
                kpos = gk * 128 + p
                cref = gq * 128 + 64
                al[:, :, qb, kb] = (slopes[None, :] * (kpos[:, None] - cref)).astype(np.float32)
    return dict(identf=identf, TI=TI, maskA=maskA, cmask=cmask, alibi=al.reshape(128, 1024))


def make_in_maps(x, norm_w, w_in, lower_bound_table, hgrn_norm_w, lambda_q1, lambda_k1,
                 lambda_q2, lambda_k2, subln_w, w_branch_a, w_branch_b, w_out, final_w, cores=range(NCORES)):
    f = lambda a: np.ascontiguousarray(np.asarray(a, dtype=np.float32))
    x = f(x)
    shared = dict(
        w_in=f(w_in[0]), w_a=f(w_branch_a[0]), w_b=f(w_branch_b[0]), w_out=f(w_out[0]),
        normwT=f(np.asarray(norm_w[0]).reshape(32, 128).T),
        fwB=f(np.broadcast_to(np.asarray(final_w)[None, :], (128, D))),
        lbt=f(np.broadcast_to(np.asarray(lower_bound_table).reshape(1, 4096), (128, 4096))),
        hnwB=f(np.broadcast_to(np.asarray(hgrn_norm_w[0])[None, :], (128, 128))),
        sublnB=f(np.broadcast_to(np.asarray(subln_w[0])[None, :], (128, 256))),
        lamv=f(np.broadcast_to(np.concatenate([np.asarray(lambda_q1[0]), np.asarray(lambda_k1[0]),
                                               np.asarray(lambda_q2[0]), np.asarray(lambda_k2[0])])[None, :], (128, 512))),
    )
    hc = [host_consts(0), host_consts(1)]
    zeros = np.zeros((T, D), np.float32)
    maps = []
    for c in cores:
        b, half = c // 2, c % 2
        m = dict(shared)
        m.update(hc[half])
        m["x_own"] = f(x[b, half * T:(half + 1) * T])
        m["x_prev"] = f(x[b, 0:T]) if half == 1 else zeros
        maps.append(m)
    return maps


_NC_CACHE = {}


def kernel(**inputs):
    if "nc" not in _NC_CACHE:
        _NC_CACHE["nc"] = build_program()
    nc = _NC_CACHE["nc"]
    maps = make_in_maps(**inputs)
    res = run_bass_kernel_spmd(nc, maps, core_ids=list(range(NCORES)))
    outp = np.zeros((4, 2048, D), np.float32)
    for c in range(NCORES):
        b, half = c // 2, c % 2
        outp[b, half * T:(half + 1) * T] = np.asarray(res.results[c]["out"], dtype=np.float32)
    return outp
```
